# Optimizing a Trainium2 kernel written in Bass

```python
import math
import jax, jax.numpy as jnp
from jax import lax
import numpy as np

D_MODEL = 2048
BATCH = 2
SEQ = 4096
DEPTH = 2
DEC_BATCH = 32
DEC_SEQ = 1
PAST_LEN = 16384
PAGE_SIZE = 128

F32 = jnp.float32
NEG_INF = -1e30
BRANCH_W = D_MODEL // 2
N_BRANCH = 4
CONV_W = 4
BLOCK = 128
GDN_DK = 128
GDN_DV = 128
GDN_HEADS = BRANCH_W // GDN_DV
GDN_QKV = GDN_HEADS * (2 * GDN_DK + GDN_DV)
GDN_CHUNK = 64
DIL_GROUPS = ((128, 1), (512, 4), (2048, 16))
N_DIL = len(DIL_GROUPS)
DIL_HD = 128
DIL_QH = BRANCH_W // DIL_HD
DIL_KVH = 2
DIL_G = DIL_QH // DIL_KVH
LRU_W = BRANCH_W
LRU_BLOCKS = 8
LRU_BS = LRU_W // LRU_BLOCKS
LRU_C = 8.0
SWA_HD = 64
SWA_QH = BRANCH_W // SWA_HD
SWA_KVH = 2
SWA_G = SWA_QH // SWA_KVH
SWA_WINDOW = 128
ROPE_THETA = 150000.0
REL_BUCKETS = 32
REL_MAX_DIST = 2048
LN_EPS = 1e-5
RMS_EPS = 1e-6
DN_ALPHA = (2.0 * DEPTH) ** 0.25
DN_BETA = (8.0 * DEPTH) ** -0.25
IN_SIZES = (GDN_QKV, GDN_HEADS * GDN_DV, GDN_HEADS, GDN_HEADS,
            N_DIL * DIL_QH * DIL_HD, N_DIL * DIL_KVH * DIL_HD, N_DIL * DIL_KVH * DIL_HD, DIL_QH * DIL_HD,
            LRU_W, LRU_W,
            SWA_QH * SWA_HD, SWA_KVH * SWA_HD, SWA_KVH * SWA_HD, SWA_QH * SWA_HD,
            N_BRANCH * D_MODEL)
IN_SPLIT = tuple(int(v) for v in np.cumsum(IN_SIZES)[:-1])
D_IN = int(sum(IN_SIZES))

kernel_name = 'hybrid_gdn_dilated_rglru_swa_decode_step'


def l2norm(x):
    return x * lax.rsqrt(jnp.sum(x * x, axis=-1, keepdims=True) + 1e-6)


def rmsnorm(x, w):
    return x * lax.rsqrt(jnp.mean(x * x, axis=-1, keepdims=True) + RMS_EPS) * w


def layernorm(x, g, b):
    mu = jnp.mean(x, axis=-1, keepdims=True)
    xc = x - mu
    var = jnp.mean(xc * xc, axis=-1, keepdims=True)
    return xc * lax.rsqrt(var + LN_EPS) * g + b


def causal_conv(x, buf, w, b=None):
    T = x.shape[1]
    xe = jnp.concatenate([buf.astype(F32), x.astype(F32)], axis=1)
    w = w.astype(F32)
    y = xe[:, 0:T] * w[0]
    for j in range(1, CONV_W):
        y = y + xe[:, j:j + T] * w[j]
    if b is not None:
        y = y + b.astype(F32)
    return y, xe[:, T:]


def softmax_lse(s, sink):
    m = jnp.max(s, axis=-1)
    if sink is not None:
        m = jnp.maximum(m, sink)
    p = jnp.exp(s - m[..., None])
    den = jnp.sum(p, axis=-1)
    if sink is not None:
        den = den + jnp.exp(sink - m)
    return p / den[..., None], m + jnp.log(den)


def rel_bucket(dist):
    max_exact = REL_BUCKETS // 2
    n = dist.astype(F32)
    large = max_exact + (jnp.log(jnp.maximum(n, 1.0) / max_exact) / math.log(REL_MAX_DIST / max_exact)
                         * (REL_BUCKETS - max_exact)).astype(jnp.int32)
    large = jnp.minimum(large, REL_BUCKETS - 1)
    return jnp.where(dist < max_exact, dist, large)


def dil_offset_bias(rel_bias, gi, win, dil):
    J = win // dil + 1
    b = rel_bias[rel_bucket(dil * jnp.arange(J, dtype=jnp.int32))]
    return b[:, gi * DIL_QH:(gi + 1) * DIL_QH].reshape(J, DIL_KVH, DIL_G).astype(F32)


def to_sub(x, d):
    N, L = x.shape[:2]
    x = jnp.swapaxes(x.reshape((N, L // d, d) + x.shape[2:]), 1, 2)
    return x.reshape((N * d, L // d) + x.shape[3:])


def from_sub(x, n, d):
    Ld = x.shape[1]
    x = jnp.swapaxes(x.reshape((n, d, Ld) + x.shape[2:]), 1, 2)
    return x.reshape((n, Ld * d) + x.shape[3:])


def rope(x, pos):
    half = x.shape[-1] // 2
    inv = ROPE_THETA ** (-jnp.arange(half, dtype=F32) / half)
    ang = pos.astype(F32)[:, None] * inv[None, :]
    ang = ang.reshape((1, ang.shape[0]) + (1,) * (x.ndim - 3) + (half,))
    c, s = jnp.cos(ang), jnp.sin(ang)
    x1, x2 = x[..., :half], x[..., half:]
    return jnp.concatenate([x1 * c - x2 * s, x2 * c + x1 * s], axis=-1)


def banded_attn(q, kv, window, off_bias, sink):
    N, L, Hk, G, hd = q.shape
    nb = -(-L // BLOCK)
    pad = nb * BLOCK - L
    q = jnp.pad(q.astype(F32), ((0, 0), (0, pad), (0, 0), (0, 0), (0, 0)))
    kv = jnp.pad(kv.astype(F32), ((0, 0), (0, pad), (0, 0), (0, 0), (0, 0)))
    qb = q.reshape(N, nb, BLOCK, Hk, G, hd)
    kvb = kv.reshape(N, nb, BLOCK, 2, Hk, hd)
    kvp = jnp.pad(kvb, ((0, 0), (1, 0), (0, 0), (0, 0), (0, 0), (0, 0)))[:, :-1]
    kvc = jnp.concatenate([kvp, kvb], axis=2)
    s = jnp.einsum('nbqhgd,nbkhd->nbhgqk', qb, kvc[:, :, :, 0]) * hd ** -0.5
    kj = jnp.arange(2 * BLOCK)
    off = jnp.arange(BLOCK)[:, None] + BLOCK - kj[None, :]
    first = (jnp.arange(nb)[:, None, None] > 0) | (kj[None, None, :] >= BLOCK)
    valid = (off >= 0) & (off <= window) & first
    if off_bias is not None:
        bias = off_bias[jnp.clip(off, 0, window)]
        s = s + jnp.transpose(bias, (2, 3, 0, 1))
    s = jnp.where(valid[:, None, None], s, NEG_INF)
    p, lse = softmax_lse(s, None if sink is None else sink[:, :, None])
    o = jnp.einsum('nbhgqk,nbkhd->nbqhgd', p, kvc[:, :, :, 1])
    o = o.reshape(N, nb * BLOCK, Hk, G, hd)[:, :L]
    lse = jnp.transpose(lse, (0, 1, 4, 2, 3)).reshape(N, nb * BLOCK, Hk, G)[:, :L]
    return o, lse


def gathered_attn(q, kv_ext, n_buf, window, dil, off_bias, sink):
    N, T, Hk, G, hd = q.shape
    J = window // dil + 1
    idx = n_buf + jnp.arange(T)[:, None] - dil * jnp.arange(J)[None, :]
    valid = idx >= 0
    kvg = kv_ext[:, jnp.maximum(idx, 0)].astype(F32)
    s = jnp.einsum('nthgd,ntjhd->nthgj', q.astype(F32), kvg[:, :, :, 0]) * hd ** -0.5
    if off_bias is not None:
        s = s + jnp.transpose(off_bias, (1, 2, 0))
    s = jnp.where(valid[:, None, None, :], s, NEG_INF)
    p, lse = softmax_lse(s, sink)
    o = jnp.einsum('nthgj,ntjhd->nthgd', p, kvg[:, :, :, 1])
    return o, lse


def gdn_chunked(q, k, v, beta, g, S0):
    N, T, H, DK = q.shape
    DV = v.shape[-1]
    C = GDN_CHUNK
    nc = T // C

    def blk(t):
        return jnp.moveaxis(t.reshape(N, nc, C, H, -1), 3, 1)
    q, k, v = blk(q), blk(k), blk(v)
    beta = blk(beta[..., None])[..., 0]
    g = blk(g[..., None])[..., 0]
    gc = jnp.cumsum(g, axis=-1)
    incl = jnp.tril(jnp.ones((C, C), bool))
    strict = jnp.tril(jnp.ones((C, C), bool), -1)
    decay = jnp.where(incl, jnp.exp(jnp.where(incl, gc[..., :, None] - gc[..., None, :], 0.0)), 0.0)
    kb = k * beta[..., None]
    lmat = jnp.where(strict, jnp.einsum('nhcid,nhcjd->nhcij', kb, k) * decay, 0.0)
    amat = lmat + jnp.eye(C, dtype=F32)
    rhs = jnp.concatenate([v * beta[..., None], kb * jnp.exp(gc)[..., None]], axis=-1)
    sol = lax.linalg.triangular_solve(amat, rhs, left_side=True, lower=True)
    u, w = sol[..., :DV], sol[..., DV:]
    qk = jnp.einsum('nhcid,nhcjd->nhcij', q, k) * decay
    qg = q * jnp.exp(gc)[..., None]
    gl = gc[..., -1]
    kd = k * jnp.exp(gl[..., None] - gc)[..., None]

    def step(S, inp):
        qg_c, qk_c, u_c, w_c, kd_c, gl_c = inp
        v_new = u_c - jnp.einsum('nhck,nhkv->nhcv', w_c, S)
        o = jnp.einsum('nhck,nhkv->nhcv', qg_c, S) + jnp.einsum('nhij,nhjv->nhiv', qk_c, v_new)
        S = S * jnp.exp(gl_c)[..., None, None] + jnp.einsum('nhck,nhcv->nhkv', kd_c, v_new)
        return S, o
    xs = tuple(jnp.moveaxis(t, 2, 0) for t in (qg, qk, u, w, kd, gl))
    S, o = lax.scan(step, S0, xs)
    o = jnp.transpose(o, (1, 0, 3, 2, 4)).reshape(N, T, H, DV)
    return o, S


def gdn_recurrent(q, k, v, beta, g, S0):
    def step(S, inp):
        q_t, k_t, v_t, b_t, g_t = inp
        S = S * jnp.exp(g_t)[..., None, None]
        v_old = jnp.einsum('nhk,nhkv->nhv', k_t, S)
        S = S + jnp.einsum('nhk,nhv->nhkv', k_t * b_t[..., None], v_t - v_old)
        return S, jnp.einsum('nhk,nhkv->nhv', q_t, S)
    xs = tuple(jnp.moveaxis(t, 1, 0) for t in (q, k, v, beta, g))
    S, o = lax.scan(step, S0, xs)
    return jnp.moveaxis(o, 0, 1), S


def rglru(x, h0, wa, ba, wx, bx, lam):
    N, T, W = x.shape
    xb = x.reshape(N, T, LRU_BLOCKS, LRU_BS)
    r = jax.nn.sigmoid(jnp.einsum('ntbi,bij->ntbj', xb, wa).reshape(N, T, W) + ba)
    i = jax.nn.sigmoid(jnp.einsum('ntbi,bij->ntbj', xb, wx).reshape(N, T, W) + bx)
    log_a = -LRU_C * r * jax.nn.softplus(-lam)
    a = jnp.exp(log_a)
    b = jnp.sqrt(-jnp.expm1(2.0 * log_a)) * (i * x)
    b = b.at[:, 0].add(a[:, 0] * h0)

    def comb(e1, e2):
        return e1[0] * e2[0], e2[0] * e1[1] + e2[1]
    _, h = lax.associative_scan(comb, (a, b), axis=1)
    return h, h[:, -1]


def mixer_layer(x, prompt, pos0, gdn_s, gdn_buf, lru_h, lru_buf, dil_bufs, swa_buf, rel_bias,
                w_in, gdn_conv_w, gdn_a_log, gdn_dt_bias, gdn_norm_w, lru_conv_w, lru_conv_b,
                lru_wa, lru_ba, lru_wx, lru_bx, lru_lambda, swa_sink, w_branch, w_out, ln_g, ln_b):
    N, T, _ = x.shape
    dt = x.dtype
    h = jnp.einsum('ntd,de->nte', x, w_in)
    (a_qkv, a_z, a_b, a_a, b_q, b_k, b_v, b_g, c_x, c_g,
     d_q, d_k, d_v, d_g, m_g) = jnp.split(h, IN_SPLIT, axis=-1)

    qkv, new_gdn_buf = causal_conv(a_qkv, gdn_buf, gdn_conv_w)
    qkv = jax.nn.silu(qkv)
    q, k, v = jnp.split(qkv, (GDN_HEADS * GDN_DK, 2 * GDN_HEADS * GDN_DK), axis=-1)
    q = l2norm(q.reshape(N, T, GDN_HEADS, GDN_DK)) * GDN_DK ** -0.5
    k = l2norm(k.reshape(N, T, GDN_HEADS, GDN_DK))
    v = v.reshape(N, T, GDN_HEADS, GDN_DV)
    beta = jax.nn.sigmoid(a_b.astype(F32))
    g = -jnp.exp(gdn_a_log.astype(F32)) * jax.nn.softplus(a_a.astype(F32) + gdn_dt_bias.astype(F32))
    if prompt:
        o_a, new_gdn_s = gdn_chunked(q, k, v, beta, g, gdn_s.astype(F32))
    else:
        o_a, new_gdn_s = gdn_recurrent(q, k, v, beta, g, gdn_s.astype(F32))
    y_a = rmsnorm(o_a, gdn_norm_w.astype(F32)).reshape(N, T, -1) * jax.nn.silu(a_z.astype(F32))

    bq = b_q.astype(F32).reshape(N, T, N_DIL, DIL_KVH, DIL_G, DIL_HD)
    bkv = jnp.stack([b_k.reshape(N, T, N_DIL, DIL_KVH, DIL_HD),
                     b_v.reshape(N, T, N_DIL, DIL_KVH, DIL_HD)], axis=3)
    outs, lses, new_dil = [], [], []
    for gi, (win, dil) in enumerate(DIL_GROUPS):
        ob = dil_offset_bias(rel_bias, gi, win, dil)
        qg, kvg = bq[:, :, gi], bkv[:, :, gi]
        if prompt:
            o_g, l_g = banded_attn(to_sub(qg, dil), to_sub(kvg, dil), win // dil, ob, None)
            o_g, l_g = from_sub(o_g, N, dil), from_sub(l_g, N, dil)
            new_dil.append(kvg[:, T - min(win, T):])
        else:
            buf = dil_bufs[gi]
            n_buf = buf.shape[1]
            ext = jnp.concatenate([buf, kvg.astype(buf.dtype)], axis=1)
            o_g, l_g = gathered_attn(qg, ext, n_buf, win, dil, ob, None)
            new_dil.append(ext[:, ext.shape[1] - n_buf:])
        outs.append(o_g)
        lses.append(l_g)
    wts = jax.nn.softmax(jnp.stack(lses), axis=0)
    o_b = jnp.sum(wts[..., None] * jnp.stack(outs), axis=0)
    y_b = o_b.reshape(N, T, -1) * jax.nn.silu(b_g.astype(F32))

    cx, new_lru_buf = causal_conv(c_x, lru_buf, lru_conv_w, lru_conv_b)
    hs, new_lru_h = rglru(cx, lru_h.astype(F32), lru_wa.astype(F32), lru_ba.astype(F32),
                          lru_wx.astype(F32), lru_bx.astype(F32), lru_lambda.astype(F32))
    y_c = hs * jax.nn.silu(c_g.astype(F32))

    pos = pos0 + jnp.arange(T)
    sq = rope(d_q.astype(F32).reshape(N, T, SWA_KVH, SWA_G, SWA_HD), pos)
    sk = rope(d_k.astype(F32).reshape(N, T, SWA_KVH, SWA_HD), pos).astype(dt)
    skv = jnp.stack([sk, d_v.reshape(N, T, SWA_KVH, SWA_HD)], axis=2)
    sink = swa_sink.astype(F32).reshape(SWA_KVH, SWA_G)
    if prompt:
        o_d, _ = banded_attn(sq, skv, SWA_WINDOW, None, sink)
        new_swa = skv[:, T - min(SWA_WINDOW, T):]
    else:
        n_buf = swa_buf.shape[1]
        ext = jnp.concatenate([swa_buf, skv.astype(swa_buf.dtype)], axis=1)
        o_d, _ = gathered_attn(sq, ext, n_buf, SWA_WINDOW, 1, None, sink)
        new_swa = ext[:, ext.shape[1] - n_buf:]
    y_d = o_d.reshape(N, T, -1) * jax.nn.silu(d_g.astype(F32))

    ys = jnp.stack([y_a, y_b, y_c, y_d], axis=2).astype(dt)
    br = jnp.einsum('ntbw,bwd->ntbd', ys, w_branch)
    gates = jax.nn.sigmoid(m_g.astype(F32).reshape(N, T, N_BRANCH, D_MODEL))
    merged = jnp.sum(gates * br.astype(F32), axis=2).astype(dt)
    f = jnp.einsum('ntd,de->nte', merged, w_out)
    x_new = layernorm(DN_ALPHA * x.astype(F32) + f.astype(F32), ln_g.astype(F32), ln_b.astype(F32)).astype(dt)
    return x_new, (new_gdn_s, new_gdn_buf, new_dil[0], new_dil[1], new_dil[2], new_swa, new_lru_h, new_lru_buf)


def setup_inputs(seed: int = 0) -> dict:
    key = jax.random.key(seed)
    ks = jax.random.split(key, 32)

    def nrm(i, shape, scale):
        return scale * jax.random.normal(ks[i], shape, F32)
    dt0 = jnp.exp(jax.random.uniform(ks[20], (DEPTH, GDN_HEADS), F32, math.log(1e-3), math.log(1e-1)))
    a_lru = jax.random.uniform(ks[21], (DEPTH, LRU_W), F32, 0.9, 0.999)
    s_lru = a_lru ** (1.0 / LRU_C)
    w0, w1, w2 = DIL_GROUPS[0][0], DIL_GROUPS[1][0], DIL_GROUPS[2][0]
    return {
        'x_prompt': nrm(0, (BATCH, SEQ, D_MODEL), 1.0),
        'x_sample': nrm(1, (DEC_BATCH, DEC_SEQ, D_MODEL), 1.0),
        'state_gdn': nrm(2, (DEPTH, DEC_BATCH, GDN_HEADS, GDN_DK, GDN_DV), 0.1),
        'state_gdn_conv': nrm(3, (DEPTH, DEC_BATCH, CONV_W - 1, GDN_QKV), 1.0),
        'cache_dil_w128': nrm(4, (DEPTH, DEC_BATCH, min(w0, PAST_LEN), 2, DIL_KVH, DIL_HD), 1.0),
        'cache_dil_w512': nrm(5, (DEPTH, DEC_BATCH, min(w1, PAST_LEN), 2, DIL_KVH, DIL_HD), 1.0),
        'cache_dil_w2048': nrm(6, (DEPTH, DEC_BATCH, min(w2, PAST_LEN), 2, DIL_KVH, DIL_HD), 1.0),
        'cache_swa': nrm(7, (DEPTH, DEC_BATCH, min(SWA_WINDOW, PAST_LEN), 2, SWA_KVH, SWA_HD), 1.0),
        'state_rglru': nrm(8, (DEPTH, DEC_BATCH, LRU_W), 0.5),
        'state_rglru_conv': nrm(9, (DEPTH, DEC_BATCH, CONV_W - 1, LRU_W), 1.0),
        'w_in': nrm(10, (DEPTH, D_MODEL, D_IN), D_MODEL ** -0.5),
        'gdn_conv_w': nrm(11, (DEPTH, CONV_W, GDN_QKV), 0.5),
        'gdn_a_log': jnp.log(jax.random.uniform(ks[12], (DEPTH, GDN_HEADS), F32, 1.0, 16.0)),
        'gdn_dt_bias': dt0 + jnp.log(-jnp.expm1(-dt0)),
        'gdn_norm_w': 1.0 + nrm(13, (DEPTH, GDN_DV), 0.02),
        'lru_conv_w': nrm(14, (DEPTH, CONV_W, LRU_W), 0.5),
        'lru_conv_b': nrm(15, (DEPTH, LRU_W), 0.02),
        'lru_wa': nrm(16, (DEPTH, LRU_BLOCKS, LRU_BS, LRU_BS), LRU_BS ** -0.5),
        'lru_ba': nrm(17, (DEPTH, LRU_W), 0.02),
        'lru_wx': nrm(18, (DEPTH, LRU_BLOCKS, LRU_BS, LRU_BS), LRU_BS ** -0.5),
        'lru_bx': nrm(19, (DEPTH, LRU_W), 0.02),
        'lru_lambda': jnp.log(s_lru) - jnp.log1p(-s_lru),
        'swa_sink': nrm(22, (DEPTH, SWA_QH), 0.5),
        'rel_bias': nrm(23, (REL_BUCKETS, N_DIL * DIL_QH), 0.5),
        'w_branch': nrm(24, (DEPTH, N_BRANCH, BRANCH_W, D_MODEL), BRANCH_W ** -0.5 * DN_BETA),
        'w_out': nrm(25, (DEPTH, D_MODEL, D_MODEL), D_MODEL ** -0.5 * DN_BETA),
        'ln_g': 1.0 + nrm(26, (DEPTH, D_MODEL), 0.02),
        'ln_b': nrm(27, (DEPTH, D_MODEL), 0.02),
    }


def reference(x_prompt, x_sample, state_gdn, state_gdn_conv, cache_dil_w128, cache_dil_w512,
              cache_dil_w2048, cache_swa, state_rglru, state_rglru_conv, w_in, gdn_conv_w, gdn_a_log,
              gdn_dt_bias, gdn_norm_w, lru_conv_w, lru_conv_b, lru_wa, lru_ba, lru_wx, lru_bx, lru_lambda,
              swa_sink, rel_bias, w_branch, w_out, ln_g, ln_b):
    xp, xs = x_prompt, x_sample
    NP = x_prompt.shape[0]
    new_p = [[] for _ in range(8)]
    new_s = [[] for _ in range(8)]
    for l in range(DEPTH):
        lw = (w_in[l], gdn_conv_w[l], gdn_a_log[l], gdn_dt_bias[l], gdn_norm_w[l], lru_conv_w[l],
              lru_conv_b[l], lru_wa[l], lru_ba[l], lru_wx[l], lru_bx[l], lru_lambda[l], swa_sink[l],
              w_branch[l], w_out[l], ln_g[l], ln_b[l])
        xp, sp = mixer_layer(xp, True, 0,
                             jnp.zeros((NP, GDN_HEADS, GDN_DK, GDN_DV), F32),
                             jnp.zeros((NP, CONV_W - 1, GDN_QKV), F32),
                             jnp.zeros((NP, LRU_W), F32),
                             jnp.zeros((NP, CONV_W - 1, LRU_W), F32),
                             None, None, rel_bias, *lw)
        xs, ss = mixer_layer(xs, False, PAST_LEN, state_gdn[l], state_gdn_conv[l], state_rglru[l],
                             state_rglru_conv[l], (cache_dil_w128[l], cache_dil_w512[l], cache_dil_w2048[l]),
                             cache_swa[l], rel_bias, *lw)
        for i in range(8):
            new_p[i].append(sp[i])
            new_s[i].append(ss[i])
    p = [jnp.stack(v) for v in new_p]
    s = [jnp.stack(v) for v in new_s]
    return (xp, xs, p[0], s[0], p[1], s[1], p[2], s[2], p[3], s[3], p[4], s[4], p[5], s[5], p[6], s[6], p[7], s[7])
```

```python
import math
from contextlib import ExitStack
import numpy as np
import concourse.bass as bass
import concourse.mybir as mybir
from concourse.bass_utils import run_bass_kernel_spmd

F32 = mybir.dt.float32
BF16 = mybir.dt.bfloat16
AF = mybir.ActivationFunctionType
ALU = mybir.AluOpType

D = 2048
T = 4096
ND = 4
TT = T + 128 * ND
NST = TT // 512
NCH = TT // 128
DEPTH = 2
PAST = 16384
ENGS = ("tensor", "vector", "scalar", "gpsimd", "sync")
DEBUG = False
PHASES = ('P', 'inject', 'dense', 'bias', 'lru', 'dil', 'swa', 'gdn')
DILOPT = set()
LAYERS = DEPTH

GROUPS = []
GIDX = {}


def _add(name, c0, n=128):
    GIDX[name] = len(GROUPS)
    GROUPS.append((name, c0, n))


for h in range(8):
    _add(f"aq{h}", 0 + h * 128)
for h in range(8):
    _add(f"ak{h}", 1024 + h * 128)
for h in range(8):
    _add(f"av{h}", 2048 + h * 128)
for h in range(8):
    _add(f"az{h}", 3072 + h * 128)
_add("aba", 4096, 16)
for gi in range(3):
    for qh in range(8):
        _add(f"bq{gi}_{qh}", 4112 + (gi * 8 + qh) * 128)
for gi in range(3):
    for kh in range(2):
        _add(f"bk{gi}_{kh}", 7184 + (gi * 2 + kh) * 128)
for gi in range(3):
    for kh in range(2):
        _add(f"bv{gi}_{kh}", 7952 + (gi * 2 + kh) * 128)
for qh in range(8):
    _add(f"bg{qh}", 8720 + qh * 128)
for b in range(8):
    _add(f"cx{b}", 9744 + b * 128)
for b in range(8):
    _add(f"cg{b}", 10768 + b * 128)
for p in range(8):
    _add(f"dq{p}", 11792 + p * 128)
_add("dk", 12816)
_add("dv", 12944)
for p in range(8):
    _add(f"dg{p}", 13072 + p * 128)
NGM = len(GROUPS)
for b in range(4):
    for j in range(16):
        _add(f"mg{b}_{j}", 14096 + b * 2048 + j * 128)
NG = len(GROUPS)


class Trk:
    __slots__ = ("lw", "rd")
    excl = False

    def __init__(self):
        self.lw = None
        self.rd = []


class PTrk(Trk):
    __slots__ = ()
    excl = True


class Op:
    __slots__ = ("eng", "fn", "deps", "dma", "sig", "sem", "val", "prev")

    def __init__(self, eng, fn, dma):
        self.eng, self.fn, self.dma = eng, fn, dma
        self.deps = []
        self.sig = False
        self.sem = None
        self.val = 0
        self.prev = None


class Sched:
    def __init__(self, nc, gstack, n_dma=40, cap=8000):
        self.nc, self.gstack, self.cap = nc, gstack, cap
        self.esem = {e: [] for e in ENGS}
        self.ecnt = {e: cap for e in ENGS}
        self.dsem = [gstack.enter_context(nc.semaphore(f"sd{i}")) for i in range(n_dma)]
        self.dcnt = [0] * n_dma
        self.dlast = [None] * n_dma
        self.rr = 0
        self.reset()

    def reset(self):
        self.ops = {e: [] for e in ENGS}
        self.all = []

    def op(self, eng, fn, r=(), w=(), dma=False):
        o = Op(eng, fn, dma)
        w = list(w) + [t for t in r if t.excl]
        r = [t for t in r if not t.excl]
        deps = set()
        for t in r:
            if t.lw is not None:
                deps.add(t.lw)
        for t in w:
            if t.lw is not None:
                deps.add(t.lw)
            deps.update(t.rd)
        for t in r:
            t.rd.append(o)
        for t in w:
            t.lw = o
            t.rd = []
        deps.discard(o)
        o.deps = list(deps)
        self.ops[eng].append(o)
        self.all.append(o)
        return o

    def emit(self):
        nc = self.nc
        for o in self.all:
            if o.dma:
                o.sig = True
            for d in o.deps:
                if d.dma or not (d.eng == "tensor" and o.eng == "tensor" and not o.dma):
                    d.sig = True
        for o in self.all:
            if not o.sig:
                continue
            if o.dma:
                i = self.rr % len(self.dsem)
                self.rr += 1
                self.dcnt[i] += 16
                o.sem, o.val, o.prev = self.dsem[i], self.dcnt[i], self.dlast[i]
                self.dlast[i] = o
            else:
                e = o.eng
                if self.ecnt[e] >= self.cap:
                    self.esem[e].append(self.gstack.enter_context(nc.semaphore(f"s{e}{len(self.esem[e])}")))
                    self.ecnt[e] = 0
                self.ecnt[e] += 1
                o.sem, o.val = self.esem[e][-1], self.ecnt[e]
        finals = [x for x in self.dlast if x is not None]
        phase_ops = set(map(id, self.all))

        def make(e):
            def body(h):
                waited = {}
                for o in self.ops[e]:
                    need = {}
                    deps = list(o.deps)
                    if o.dma and o.prev is not None and id(o.prev) in phase_ops:
                        deps.append(o.prev)
                    for d in deps:
                        if (not d.dma) and d.eng == "tensor" and e == "tensor" and not o.dma:
                            continue
                        k = id(d.sem)
                        if waited.get(k, 0) >= d.val:
                            continue
                        if k not in need or need[k][1] < d.val:
                            need[k] = (d.sem, d.val)
                    for k, (s, v) in need.items():
                        h.wait_ge(s, v)
                        waited[k] = v
                    ins = o.fn(h)
                    if o.sig:
                        ins.then_inc(o.sem, 16 if o.dma else 1)
                if e == "sync":
                    for d in finals:
                        if waited.get(id(d.sem), 0) < d.val:
                            h.wait_ge(d.sem, d.val)
            return body

        with nc.Block() as block:
            block.tensor(make("tensor"))
            block.vector(make("vector"))
            block.scalar(make("scalar"))
            block.gpsimd(make("gpsimd"))
            block.sync(make("sync"))
        self.reset()


class Ctx:
    def __init__(self):
        self.nc = bass.Bass("TRN2", target_bir_lowering=False)
        self.g = ExitStack()
        self.S = Sched(self.nc, self.g)
        self.ins = {}
        self.outs = {}
        self.ph = None
        self.n = 0

    def inp(self, name, shape, dt=F32):
        t = self.nc.dram_tensor(name, list(shape), dt, kind="ExternalInput")
        self.ins[name] = t
        return t

    def outp(self, name, shape, dt=F32):
        t = self.nc.dram_tensor(name, list(shape), dt, kind="ExternalOutput")
        self.outs[name] = t
        return t

    def scratch(self, name, shape, dt=F32):
        if DEBUG:
            return self.outp(name, shape, dt)
        return self.nc.dram_tensor(name, list(shape), dt)

    def sb(self, shape, dt=F32):
        self.n += 1
        return self.ph.enter_context(self.nc.sbuf_tensor(f"sb{self.n}", list(shape), dt))

    def ps(self, shape, dt=F32):
        self.n += 1
        return self.ph.enter_context(self.nc.psum_tensor(f"ps{self.n}", list(shape), dt))

    def dma(self, eng, out, in_, r=(), w=()):
        return self.S.op(eng, lambda e: e.dma_start(out=out, in_=in_), r=r, w=w, dma=True)

    def mm(self, out, lhsT, rhs, start=True, stop=True, r=(), w=()):
        return self.S.op("tensor", lambda e: e.matmul(out, lhsT=lhsT, rhs=rhs, start=start, stop=stop), r=r, w=w)

    def tr(self, out, in_, ident, r=(), w=()):
        return self.S.op("tensor", lambda e: e.transpose(out, in_, ident), r=r, w=w)

    def act(self, out, in_, func, r=(), w=(), bias=None, scale=None, eng="scalar"):
        kw = {}
        if bias is not None:
            kw["bias"] = bias
        if scale is not None:
            kw["scale"] = scale
        return self.S.op("scalar", lambda e: e.activation(out=out, in_=in_, func=func, **kw), r=r, w=w)

    def tt(self, out, in0, in1, op, r=(), w=(), eng="vector"):
        return self.S.op(eng, lambda e: e.tensor_tensor(out=out, in0=in0, in1=in1, op=op), r=r, w=w)

    def ts(self, out, in0, s1, op0, s2=None, op1=None, r=(), w=(), eng="vector"):
        if op1 is None:
            return self.S.op(eng, lambda e: e.tensor_scalar(out=out, in0=in0, scalar1=s1, scalar2=None, op0=op0), r=r, w=w)
        return self.S.op(eng, lambda e: e.tensor_scalar(out=out, in0=in0, scalar1=s1, scalar2=s2, op0=op0, op1=op1), r=r, w=w)

    def stt(self, out, in0, scalar, in1, op0, op1, r=(), w=()):
        return self.S.op("vector", lambda e: e.scalar_tensor_tensor(out=out, in0=in0, scalar=scalar, in1=in1, op0=op0, op1=op1), r=r, w=w)

    def cp(self, out, in_, r=(), w=(), eng="vector"):
        if eng == "scalar":
            return self.S.op("scalar", lambda e: e.activation(out=out, in_=in_, func=AF.Copy), r=r, w=w)
        return self.S.op(eng, lambda e: e.tensor_copy(out=out, in_=in_), r=r, w=w)

    def memset(self, ap, val, w=(), eng="vector"):
        return self.S.op(eng, lambda e: e.memset(ap, val), w=w)


class Rot:
    def __init__(self, items):
        self.items = items
        self.i = 0

    def get(self):
        x = self.items[self.i % len(self.items)]
        self.i += 1
        return x


def dec_base(n):
    return T + 128 * n


def build_program():
    C = Ctx()
    nc = C.nc
    small = 'P' not in PHASES
    sm = lambda shp: [1, 1] if small else shp
    xT0 = C.inp("xT0", sm([D, TT]))
    x0 = C.inp("x0", sm([TT, D]))
    wg = [C.inp(f"wg{l}", sm([NG, 128, 2048])) for l in range(DEPTH)]
    wb = [C.inp(f"wb{l}", sm([4, 16, 128, 1024])) for l in range(DEPTH)]
    wo = [C.inp(f"wo{l}", sm([128, 16 * D])) for l in range(DEPTH)]
    lng = C.inp("lng", sm([DEPTH, 128, D]))
    lnb = C.inp("lnb", sm([DEPTH, 128, D]))
    ident_d = C.inp("ident", [128, 128])
    lru_par = C.inp("lru_par", [DEPTH, 128, 8 * 8])
    lru_w = C.inp("lru_w", [DEPTH, 2, 128, 8 * 128])
    st_lru = C.inp("st_lru", [DEPTH, ND, 1024])
    st_lruc = C.inp("st_lruc", [DEPTH, ND, 3, 1024])
    gmask_d = C.inp("gmask", [3, 128, 128])
    tmask_d = C.inp("tmask", [2, 8, TT])
    gcw_d = C.inp("gcw", [DEPTH, 128, 96])
    gal_d = C.inp("gal", [DEPTH, 8, 2])
    gnw_d = C.inp("gnw", [DEPTH, 128, 1])
    st_gdn = C.inp("st_gdn", [DEPTH, ND, 8, 128, 128])
    st_gdnc = C.inp("st_gdnc", [DEPTH, ND, 3, 3072])
    rel_d = C.inp("rel", [32, 24])
    oh_d = C.inp("oh", [32, 3 * 129])
    sel_d = C.inp("sel8", [8, 8 * 128])
    masks_d = C.inp("masks", [2, 128, 128])
    rope_d = C.inp("rope", [2, 64, TT])
    sink_d = C.inp("sink", [DEPTH, 128, 16])
    caches = [C.inp(f"cd{g}", [DEPTH, ND, WINS[g], 2, 2, 128]) for g in range(3)]
    cswa = C.inp("cswa", [DEPTH, ND, 128, 2, 2, 64])
    y_out = C.outp("y", [TT, D])
    o_lru = C.outp("o_lru", [DEPTH, 1 + ND, 1024])
    o_lruc = C.outp("o_lruc", [DEPTH, 1 + ND, 3, 1024])
    o_gdn = C.outp("o_gdn", [DEPTH, 1 + ND, 8, 128, 128])
    o_gdnc = C.outp("o_gdnc", [DEPTH, 1 + ND, 3, 3072])
    o_caches = [C.outp(f"o_cd{g}", [DEPTH, 1 + ND, WINS[g], 2, 2, 128]) for g in range(3)]
    o_cswa = C.outp("o_cswa", [DEPTH, 1 + ND, 128, 2, 2, 64])
    hT = C.scratch("hT", [NGM * 128, TT])
    G = C.scratch("G", [64 * 128, TT], BF16)
    ysT = C.scratch("ysT", [4096, TT], BF16)
    y0 = C.scratch("y0", [TT, D])
    xT1 = C.scratch("xT1", [D, TT], BF16)
    C.hT, C.G, C.ysT = hT, G, ysT
    C.wb16 = C.scratch("wb16", [4, 16, 128, 1024], BF16)
    bvd = C.scratch("bvd", [8, 128, 1530])
    btile = C.scratch("btile", [3, 8, 2, 128, 128])
    if "bias" in PHASES:
        phase_bias_setup(C, rel_d, oh_d, sel_d, bvd, btile)

    for l in range(LAYERS):
        if 'P' in PHASES:
            phase_P(C, l, xT0 if l == 0 else xT1, wg[l], cast=(l == 0))
        if 'inject' in PHASES:
            phase_state_inject(C, l, st_lruc, st_gdnc)
        if "gdn" in PHASES:
            phase_gdn(C, l, ident_d, sel_d, gmask_d, tmask_d, gcw_d, gal_d, gnw_d, st_gdn, o_gdn, o_gdnc)
        if "lru" in PHASES:
            phase_LRU(C, l, lru_par, lru_w, st_lru, o_lru, o_lruc)
        if "dil" in PHASES:
            phase_dil(C, l, btile, ident_d, caches, o_caches)
        if "swa" in PHASES:
            phase_swa(C, l, masks_d, ident_d, rope_d, sink_d, cswa, o_cswa)
        if 'dense' in PHASES:
            phase_mirror_ys(C)
            phase_dense(C, l, wb[l], wo[l], lng, lnb, ident_d, x0 if l == 0 else y0,
                        y0 if l == 0 else y_out, xT1 if l == 0 else None)
    C.g.close()
    return C


def phase_P(C, l, xsrc, wgl, cast):
    C.ph = ExitStack()
    with C.ph:
        xT = C.sb([128, 16, TT], BF16)
        txk = [Trk() for _ in range(16)]
        src = xsrc.ap().rearrange("(kc p) t -> p kc t", p=128)
        for kc in range(16):
            C.dma("gpsimd" if cast else "sync", xT[:, kc, :], src[:, kc, :], w=[txk[kc]])
        wrot = Rot([(C.sb([128, 16, 128], BF16), Trk()) for _ in range(3)])
        srot = Rot([(C.sb([128, 1536], F32), Trk()) for _ in range(2)])
        grot = Rot([(C.sb([128, 1536], BF16), Trk()) for _ in range(2)])
        prot = Rot([(C.ps([128, 512], F32), PTrk()) for _ in range(6)])
        k = 0
        for g in range(NG):
            wt, twt = wrot.get()
            C.dma("gpsimd", wt[:, :, :], wgl[g].rearrange("p (kc c) -> p kc c", c=128), w=[twt])
            gate = g >= NGM
            for t3 in range(3):
                stg, tst = (grot if gate else srot).get()
                for tl in range(3):
                    t = t3 * 3 + tl
                    ps, tps = prot.get()
                    for kc in range(16):
                        C.mm(ps[:, :], wt[:, kc, :], xT[:, kc, t * 512:(t + 1) * 512], start=(kc == 0), stop=(kc == 15),
                             r=[twt, txk[kc]], w=[tps])
                    o = stg[:, tl * 512:(tl + 1) * 512]
                    if gate:
                        C.act(o, ps[:, :], AF.Sigmoid, r=[tps], w=[tst])
                    else:
                        k += 1
                        C.cp(o, ps[:, :], r=[tps], w=[tst], eng=("vector" if k % 2 else "scalar"))
                if gate:
                    dst = C.G[(g - NGM) * 128:(g - NGM + 1) * 128, t3 * 1536:(t3 + 1) * 1536]
                else:
                    dst = C.hT[g * 128:(g + 1) * 128, t3 * 1536:(t3 + 1) * 1536]
                C.dma("sync", dst, stg[:, :], r=[tst])
        C.S.emit()


def phase_state_inject(C, l, st_lruc, st_gdnc):
    C.ph = ExitStack()
    with C.ph:
        chain = [Trk() for _ in range(4)]
        ic = [0]

        def nxt():
            ic[0] += 1
            return [chain[ic[0] % 4]]
        for n in range(ND):
            b0 = dec_base(n)
            for blk in range(8):
                g = GIDX[f"cx{blk}"]
                src = st_lruc[l, n, :, blk * 128:(blk + 1) * 128].rearrange("j p -> p j")
                C.S.op("sync", lambda e, g=g, b0=b0, src=src: e.dma_start(
                    out=C.hT[g * 128:(g + 1) * 128, b0:b0 + 3], in_=src, allow_slow_non_contiguous=True), w=nxt(), dma=True)
            for i3, nm in enumerate(("aq", "ak", "av")):
                for h in range(8):
                    g = GIDX[f"{nm}{h}"]
                    cc0 = i3 * 1024 + h * 128
                    src = st_gdnc[l, n, :, cc0:cc0 + 128].rearrange("j p -> p j")
                    C.S.op("sync", lambda e, g=g, b0=b0, src=src: e.dma_start(
                        out=C.hT[g * 128:(g + 1) * 128, b0:b0 + 3], in_=src, allow_slow_non_contiguous=True), w=nxt(), dma=True)
        C.S.emit()


def phase_LRU(C, l, lru_par, lru_w, st_lru, o_lru, o_lruc):
    PC = 1536
    C.ph = ExitStack()
    with C.ph:
        par = C.sb([128, 64])
        tpar = Trk()
        C.dma("sync", par[:, :], lru_par[l], w=[tpar])
        wax = C.sb([128, 2, 1024])
        twax = Trk()
        C.dma("sync", wax[:, 0, :], lru_w[l, 0], w=[twax])
        C.dma("sync", wax[:, 1, :], lru_w[l, 1], w=[twax])
        h0 = C.sb([128, 8, ND])
        th0 = Trk()
        for blk in range(8):
            C.S.op("sync", lambda e, blk=blk: e.dma_start(out=h0[:, blk, :], in_=st_lru[l][:, blk * 128:(blk + 1) * 128].rearrange("n p -> p n"),
                                                          allow_slow_non_contiguous=True), w=[th0], dma=True)
        ccol = C.sb([128, 8])
        tcc = Trk()
        for blk in range(8):
            C.act(ccol[:, blk:blk + 1], par[:, blk * 8 + 7:blk * 8 + 8], AF.Exp, scale=-1.0, r=[tpar], w=[tcc])
        C.act(ccol[:, :], ccol[:, :], AF.Ln, bias=1.0, r=[tcc], w=[tcc])
        C.ts(ccol[:, :], ccol[:, :], -8.0, ALU.mult, r=[tcc], w=[tcc])
        xprot = Rot([(C.sb([128, 3 + PC]), Trk()) for _ in range(2)])
        cgrot = Rot([(C.sb([128, PC]), Trk()) for _ in range(2)])
        hhrot = Rot([(C.sb([128, PC]), Trk()) for _ in range(2)])
        ysrot = Rot([(C.sb([128, PC], BF16), Trk()) for _ in range(2)])
        cx = C.sb([128, PC]); tcx = Trk()
        rr = C.sb([128, PC]); trr = Trk()
        ii = C.sb([128, PC]); tii = Trk()
        aa = C.sb([128, PC]); taa = Trk()
        bb = C.sb([128, PC]); tbb = Trk()
        prot = Rot([(C.ps([128, 512], F32), PTrk()) for _ in range(4)])
        for blk in range(8):
            g = GIDX[f"cx{blk}"]
            gg = GIDX[f"cg{blk}"]
            pw = lambda j, blk=blk: par[:, blk * 8 + j:blk * 8 + j + 1]
            hprev = None
            for pc in range(3):
                c0 = pc * PC
                xp, txp = xprot.get()
                if pc == 0:
                    C.memset(xp[:, 0:3], 0.0, w=[txp])
                    C.dma("sync", xp[:, 3:3 + PC], C.hT[g * 128:(g + 1) * 128, 0:PC], w=[txp])
                else:
                    C.dma("sync", xp[:, :], C.hT[g * 128:(g + 1) * 128, c0 - 3:c0 + PC], w=[txp])
                cg, tcg = cgrot.get()
                C.dma("sync", cg[:, :], C.hT[gg * 128:(gg + 1) * 128, c0:c0 + PC], w=[tcg])
                C.act(cx[:, :], xp[:, 0:PC], AF.Identity, scale=pw(0), bias=pw(4), r=[txp, tpar], w=[tcx])
                for j in range(1, 4):
                    C.stt(cx[:, :], xp[:, j:j + PC], pw(j), cx[:, :], ALU.mult, ALU.add, r=[txp, tpar, tcx], w=[tcx])
                for t in range(3):
                    for which, dst, tdst, bj in ((0, rr, trr, 5), (1, ii, tii, 6)):
                        ps, tps = prot.get()
                        C.mm(ps[:, :], wax[:, which, blk * 128:(blk + 1) * 128], cx[:, t * 512:(t + 1) * 512], r=[twax, tcx], w=[tps])
                        C.act(dst[:, t * 512:(t + 1) * 512], ps[:, :], AF.Sigmoid, bias=pw(bj), r=[tps, tpar], w=[tdst])
                C.act(aa[:, :], rr[:, :], AF.Exp, scale=ccol[:, blk:blk + 1], r=[trr, tcc], w=[taa])
                C.tt(bb[:, :], aa[:, :], aa[:, :], ALU.mult, r=[taa], w=[tbb])
                C.ts(bb[:, :], bb[:, :], -1.0, ALU.mult, 1.0, ALU.add, r=[tbb], w=[tbb])
                C.act(bb[:, :], bb[:, :], AF.Sqrt, r=[tbb], w=[tbb])
                C.tt(bb[:, :], bb[:, :], ii[:, :], ALU.mult, r=[tbb, tii], w=[tbb])
                C.tt(bb[:, :], bb[:, :], cx[:, :], ALU.mult, r=[tbb, tcx], w=[tbb])
                for n in range(ND):
                    col = dec_base(n) + 2 - c0
                    if 0 <= col < PC:
                        C.memset(aa[:, col:col + 1], 0.0, w=[taa])
                        C.cp(bb[:, col:col + 1], h0[:, blk, n:n + 1], r=[th0], w=[tbb])
                hh, thh = hhrot.get()
                init = 0.0 if hprev is None else hprev[0][:, PC - 1:PC]
                rdeps = [taa, tbb] + ([] if hprev is None else [hprev[1]])
                C.S.op("vector", lambda e, hh=hh, init=init: e.tensor_tensor_scan(out=hh[:, :], data0=aa[:, :], data1=bb[:, :],
                                                                               initial=init, op0=ALU.mult, op1=ALU.add), r=rdeps, w=[thh])
                hprev = (hh, thh)
                C.act(cg[:, :], cg[:, :], AF.Silu, r=[tcg], w=[tcg])
                ys, tys = ysrot.get()
                C.tt(ys[:, :], hh[:, :], cg[:, :], ALU.mult, r=[thh, tcg], w=[tys])
                C.dma("sync", C.ysT[2048 + blk * 128:2048 + (blk + 1) * 128, c0:c0 + PC], ys[:, :], r=[tys])
                def col_out(dst_ap, src_ap, trk):
                    C.S.op("sync", lambda e: e.dma_start(out=dst_ap, in_=src_ap, allow_slow_non_contiguous=True), r=[trk], dma=True)
                segs = [(0, T - 1)] + [(1 + n, dec_base(n) + 3) for n in range(ND)]
                for si, last in segs:
                    if c0 <= last < c0 + PC:
                        lc = last - c0
                        col_out(o_lru[l, si, blk * 128:(blk + 1) * 128].rearrange("(p o) -> p o", o=1), hh[:, lc:lc + 1], thh)
                        col_out(o_lruc[l, si, :, blk * 128:(blk + 1) * 128].rearrange("j p -> p j"), xp[:, lc + 1:lc + 4], txp)
        C.S.emit()


def phase_dense(C, l, wbl, wol, lng, lnb, ident_d, xres, ydst, xTdst):
    DN_ALPHA = (2.0 * DEPTH) ** 0.25
    C.ph = ExitStack()
    with C.ph:
        strot = Rot([(C.sb([128, 8, 1024], BF16), Trk()) for _ in range(2)])
        for b in range(4):
            for jh in range(2):
                stg, tstg = strot.get()
                C.dma("gpsimd", stg[:, :, :], wbl[b, jh * 8:(jh + 1) * 8].rearrange("j p c -> p j c"), w=[tstg])
                C.dma("sync", C.wb16[b, jh * 8:(jh + 1) * 8].rearrange("j p c -> p j c"), stg[:, :, :], r=[tstg])
        C.S.emit()
    C.ph = ExitStack()
    with C.ph:
        ident = C.sb([128, 128]); tid = Trk()
        C.dma("sync", ident[:, :], ident_d[:, :], w=[tid])
        wot = C.sb([128, 16, D], BF16); two = Trk()
        wov = wol.ap().rearrange("p (kc c) -> p kc c", c=D)
        for kc in range(16):
            C.dma("gpsimd", wot[:, kc, :], wov[:, kc, :], w=[two])
        gam = C.sb([128, D]); bet = C.sb([128, D]); tgb = Trk()
        C.dma("sync", gam[:, :], lng[l], w=[tgb])
        C.dma("sync", bet[:, :], lnb[l], w=[tgb])
        ysr = Rot([(C.sb([128, 32, 512], BF16), Trk()) for _ in range(1)])
        mg = C.sb([128, 16, 512], BF16); tmg = [Trk() for _ in range(16)]
        wbr = Rot([(C.sb([128, 8, 128], BF16), Trk()) for _ in range(4)])
        gtr = Rot([(C.sb([128, 512], BF16), Trk()) for _ in range(4)])
        acc = C.sb([128, 512]); tacc = Trk()
        tmp = C.sb([128, 512]); ttmp = Trk()
        xr = Rot([(C.sb([128, D]), Trk()) for _ in range(2)])
        zr = Rot([(C.sb([128, D]), Trk()) for _ in range(2)])
        st = C.sb([128, 4, 6]); tst = Trk()
        mv = C.sb([128, 2]); tmv = Trk()
        rstd = C.sb([128, 1]); trs = Trk()
        xts = Rot([(C.sb([128, 16, 128], BF16), Trk()) for _ in range(2)])
        prot = Rot([(C.ps([128, 512], F32), PTrk()) for _ in range(5)])
        ptr = Rot([(C.ps([128, 512], F32), PTrk()) for _ in range(2)])
        ysv = C.ysT.ap().rearrange("(kc p) t -> p kc t", p=128)
        for t in range(NST):
            ys, tys = ysr.get()
            for q in range(4):
                C.dma("sync", ys[:, q * 8:(q + 1) * 8, :], ysv[:, q * 8:(q + 1) * 8, t * 512:(t + 1) * 512], w=[tys])
            for j in range(16):
                for b in range(4):
                    wt, twt = wbr.get()
                    C.dma("sync", wt[:, :, :], C.wb16[b, j].rearrange("p (kc c) -> p kc c", c=128), w=[twt])
                    gt, tgt = gtr.get()
                    C.dma("sync", gt[:, :], C.G[(b * 16 + j) * 128:(b * 16 + j + 1) * 128, t * 512:(t + 1) * 512], w=[tgt])
                    ps, tps = prot.get()
                    for kc in range(8):
                        C.mm(ps[:, :], wt[:, kc, :], ys[:, b * 8 + kc, :], start=(kc == 0), stop=(kc == 7), r=[twt, tys], w=[tps])
                    if b == 0:
                        C.tt(acc[:, :], ps[:, :], gt[:, :], ALU.mult, r=[tps, tgt], w=[tacc])
                    else:
                        C.tt(tmp[:, :], ps[:, :], gt[:, :], ALU.mult, r=[tps, tgt], w=[ttmp])
                        if b < 3:
                            C.tt(acc[:, :], acc[:, :], tmp[:, :], ALU.add, r=[tacc, ttmp], w=[tacc], eng="gpsimd")
                        else:
                            C.tt(mg[:, j, :], acc[:, :], tmp[:, :], ALU.add, r=[tacc, ttmp], w=[tmg[j]], eng="gpsimd")
            for sub in range(4):
                r0 = t * 512 + sub * 128
                x, tx = xr.get()
                C.dma("sync", x[:, :], xres[r0:r0 + 128, :], w=[tx])
                z, tz = zr.get()
                for jc in range(4):
                    ps, tps = prot.get()
                    for kc in range(16):
                        C.mm(ps[:, :], mg[:, kc, sub * 128:(sub + 1) * 128], wot[:, kc, jc * 512:(jc + 1) * 512],
                             start=(kc == 0), stop=(kc == 15), r=[tmg[kc], two], w=[tps])
                    C.stt(z[:, jc * 512:(jc + 1) * 512], x[:, jc * 512:(jc + 1) * 512], DN_ALPHA, ps[:, :], ALU.mult, ALU.add,
                          r=[tx, tps], w=[tz])
                    C.S.op("vector", lambda e, z=z, jc=jc: e.bn_stats(out=st[:, jc, :], in_=z[:, jc * 512:(jc + 1) * 512]), r=[tz], w=[tst])
                C.S.op("vector", lambda e: e.bn_aggr(out=mv[:, :], in_=st[:, :, :].rearrange("p a b -> p (a b)")), r=[tst], w=[tmv])
                C.ts(rstd[:, :], mv[:, 1:2], 1e-5, ALU.add, r=[tmv], w=[trs])
                C.act(rstd[:, :], rstd[:, :], AF.Ln, r=[trs], w=[trs])
                C.act(rstd[:, :], rstd[:, :], AF.Exp, scale=-0.5, r=[trs], w=[trs])
                C.ts(z[:, :], z[:, :], mv[:, 0:1], ALU.subtract, rstd[:, 0:1], ALU.mult, r=[tz, tmv, trs], w=[tz])
                C.tt(z[:, :], z[:, :], gam[:, :], ALU.mult, r=[tz, tgb], w=[tz], eng="gpsimd")
                C.tt(z[:, :], z[:, :], bet[:, :], ALU.add, r=[tz, tgb], w=[tz], eng="gpsimd")
                C.dma("sync", ydst[r0:r0 + 128, :], z[:, :], r=[tz])
                if xTdst is not None:
                    xt, txt = xts.get()
                    for q in range(4):
                        pt, tpt = ptr.get()
                        for kk in range(4):
                            kc = q * 4 + kk
                            C.tr(pt[:, kk * 128:(kk + 1) * 128], z[:, kc * 128:(kc + 1) * 128], ident[:, :], r=[tz, tid], w=[tpt])
                        C.cp(xt[:, q * 4:(q + 1) * 4, :], pt[:, :].rearrange("p (a b) -> p a b", b=128), r=[tpt], w=[txt], eng="scalar")
                    C.dma("sync", xTdst.ap().rearrange("(kc p) t -> p kc t", p=128)[:, :, r0:r0 + 128], xt[:, :, :], r=[txt])
        C.S.emit()
    if xTdst is not None:
        C.ph = ExitStack()
        with C.ph:
            chain = [Trk() for _ in range(4)]
            for n in range(ND):
                b0 = dec_base(n)
                for r0 in range(0, D, 128):
                    C.S.op("sync", lambda e, b0=b0, r0=r0: e.dma_start(out=xTdst[r0:r0 + 128, b0:b0 + 1], in_=xTdst[r0:r0 + 128, b0 + 3:b0 + 4],
                                                                       allow_slow_non_contiguous=True), w=[chain[(r0 // 128) % 4]], dma=True)
                C.dma("sync", ydst[b0:b0 + 1, :], ydst[b0 + 3:b0 + 4, :])
            C.S.emit()


def _group_cols():
    cols = np.zeros((NG, 128), np.int64)
    valid = np.zeros((NG, 128), bool)
    for g, (name, c0, n) in enumerate(GROUPS):
        cols[g, :n] = c0 + np.arange(n)
        valid[g, :n] = True
    return cols, valid


def host_prep(inputs):
    f = lambda k: np.asarray(inputs[k], dtype=np.float32)
    xp, xs = f("x_prompt"), f("x_sample")
    w_in, w_br, w_out = f("w_in"), f("w_branch"), f("w_out")
    cols, valid = _group_cols()
    shared = {}
    for l in range(DEPTH):
        W = w_in[l][:, cols.reshape(-1)].reshape(16, 128, NG, 128)
        W = W * valid[None, None, :, :].astype(np.float32)
        shared[f"wg{l}"] = np.ascontiguousarray(W.transpose(2, 1, 0, 3)).reshape(NG, 128, 2048)
        Wb = w_br[l].reshape(4, 8, 128, 16, 128)
        shared[f"wb{l}"] = np.ascontiguousarray(Wb.transpose(0, 3, 2, 1, 4)).reshape(4, 16, 128, 1024)
        Wo = w_out[l].reshape(16, 128, D)
        shared[f"wo{l}"] = np.ascontiguousarray(Wo.transpose(1, 0, 2)).reshape(128, 16 * D)
    shared["lng"] = np.ascontiguousarray(np.broadcast_to(f("ln_g")[:, None, :], (DEPTH, 128, D)))
    shared["lnb"] = np.ascontiguousarray(np.broadcast_to(f("ln_b")[:, None, :], (DEPTH, 128, D)))
    shared["ident"] = np.eye(128, dtype=np.float32)
    lp = np.zeros((DEPTH, 128, 8, 8), np.float32)
    lcw = f("lru_conv_w").reshape(DEPTH, 4, 8, 128)
    for j in range(4):
        lp[:, :, :, j] = lcw[:, j].transpose(0, 2, 1)
    for j, k in ((4, "lru_conv_b"), (5, "lru_ba"), (6, "lru_bx"), (7, "lru_lambda")):
        lp[:, :, :, j] = f(k).reshape(DEPTH, 8, 128).transpose(0, 2, 1)
    shared["lru_par"] = lp.reshape(DEPTH, 128, 64)
    lw = np.stack([f("lru_wa"), f("lru_wx")], axis=1)
    shared["lru_w"] = np.ascontiguousarray(lw.transpose(0, 1, 3, 2, 4)).reshape(DEPTH, 2, 128, 1024)
    pp_ = np.arange(128)[:, None]
    ff_ = np.arange(128)[None, :]
    shared["gmask"] = np.stack([np.where(ff_ > pp_, 0.0, -1.0e4), np.where(ff_ >= pp_, 0.0, -1.0e4),
                                np.where(pp_ > ff_, 0.0, 1.0e4)]).astype(np.float32)
    tm = np.zeros((2, 8, TT), np.float32)
    tm[0, :, :T] = 1.0
    for n in range(ND):
        tm[0, :, dec_base(n) + 3] = 1.0
    tm[1] = 1.0
    tm[1, :, ::128] = 0.0
    shared["tmask"] = tm
    gc_ = f("gdn_conv_w").reshape(DEPTH, 4, 24, 128)
    shared["gcw"] = np.ascontiguousarray(gc_.transpose(0, 3, 2, 1)).reshape(DEPTH, 128, 96)
    shared["gal"] = np.ascontiguousarray(np.stack([f("gdn_a_log"), f("gdn_dt_bias")], axis=-1))
    shared["gnw"] = np.ascontiguousarray(f("gdn_norm_w")[:, :, None])
    shared["rel"] = f("rel_bias")
    oh = np.zeros((32, 3, 129), np.float32)
    for g in range(3):
        dist = (DILS[g] * np.arange(129)).astype(np.int64)
        n = dist.astype(np.float32)
        large = 16 + (np.log(np.maximum(n, np.float32(1.0)) / np.float32(16)) / np.float32(math.log(2048 / 16)) * np.float32(16)).astype(np.int32)
        large = np.minimum(large, 31)
        bucket = np.where(dist < 16, dist, large)
        oh[bucket, g, np.arange(129)] = 1.0
    shared["oh"] = oh.reshape(32, 3 * 129)
    sel8 = np.zeros((8, 8, 128), np.float32)
    for h in range(8):
        sel8[h, h, :] = 1.0
    shared["sel8"] = sel8.reshape(8, 1024)
    jj = np.arange(128)[:, None]
    ii = np.arange(128)[None, :]
    shared["masks"] = np.stack([np.where(jj <= ii, 0.0, NEG), np.where(jj >= ii, 0.0, NEG)]).astype(np.float32)
    pos = np.concatenate([np.arange(T), np.full(TT - T, PAST)]).astype(np.float32)
    inv = (np.float32(150000.0) ** (-np.arange(32, dtype=np.float32) / np.float32(32))).astype(np.float32)
    ang = (pos[None, :] * inv[:, None]).astype(np.float32)
    cosr = np.cos(ang.astype(np.float64)).astype(np.float32)
    sinr = np.sin(ang.astype(np.float64)).astype(np.float32)
    shared["rope"] = np.stack([np.concatenate([cosr, cosr]), np.concatenate([-sinr, sinr])]).astype(np.float32)
    shared["sink"] = np.ascontiguousarray(np.broadcast_to(f("swa_sink")[:, None, :], (DEPTH, 128, 16)))
    per_core = []
    for c in range(8):
        s = c % 2
        m = {}
        x0 = np.zeros((TT, D), np.float32)
        x0[:T] = xp[s]
        for n in range(ND):
            x0[dec_base(n)] = xs[4 * c + n, 0]
            x0[dec_base(n) + 3] = xs[4 * c + n, 0]
        m["x0"] = x0
        m["xT0"] = np.ascontiguousarray(x0.T)
        m["st_lru"] = np.ascontiguousarray(f("state_rglru")[:, 4 * c:4 * c + 4])
        m["st_lruc"] = np.ascontiguousarray(f("state_rglru_conv")[:, 4 * c:4 * c + 4])
        for g, k in enumerate(("cache_dil_w128", "cache_dil_w512", "cache_dil_w2048")):
            m[f"cd{g}"] = np.ascontiguousarray(f(k)[:, 4 * c:4 * c + 4])
        m["cswa"] = np.ascontiguousarray(f("cache_swa")[:, 4 * c:4 * c + 4])
        m["st_gdn"] = np.ascontiguousarray(f("state_gdn")[:, 4 * c:4 * c + 4])
        m["st_gdnc"] = np.ascontiguousarray(f("state_gdn_conv")[:, 4 * c:4 * c + 4])
        m.update(shared)
        per_core.append(m)
    return per_core


_PROG = None


def run_device(inputs):
    global _PROG
    if _PROG is None:
        _PROG = build_program()
    in_maps = host_prep(inputs)
    res = run_bass_kernel_spmd(_PROG.nc, in_maps, core_ids=list(range(8)))
    return res.results


NEG = -30000.0
DILS = (1, 4, 16)
WINS = (128, 512, 2048)


def phase_bias_setup(C, rel_d, oh_d, sel_d, bvd, btile):
    C.ph = ExitStack()
    with C.ph:
        rel = C.sb([32, 24]); trel = Trk()
        C.dma("sync", rel[:, :], rel_d[:, :], w=[trel])
        oh = C.sb([32, 3 * 129]); toh = Trk()
        C.dma("sync", oh[:, :], oh_d[:, :], w=[toh])
        pall = C.sb([8, 3 * 2 * 255]); tp = Trk()
        C.memset(pall[:, :], NEG, w=[tp])
        ps = C.ps([8, 512]); tps = PTrk()
        for g in range(3):
            C.mm(ps[:, 0:129], rel[:, g * 8:(g + 1) * 8], oh[:, g * 129:(g + 1) * 129], r=[trel, toh], w=[tps])
            o0 = (g * 2 + 0) * 255
            o1 = (g * 2 + 1) * 255
            C.ts(pall[:, o0 + 127:o0 + 255], ps[:, 0:128], math.sqrt(128.0), ALU.mult, r=[tps], w=[tp])
            C.ts(pall[:, o1:o1 + 128], ps[:, 1:129], math.sqrt(128.0), ALU.mult, r=[tps], w=[tp])
        sel = C.sb([8, 8 * 128]); tsel = Trk()
        C.dma("sync", sel[:, :], sel_d[:, :], w=[tsel])
        reps = Rot([(C.sb([128, 1530]), Trk()) for _ in range(2)])
        prot = Rot([(C.ps([128, 512]), PTrk()) for _ in range(3)])
        for h in range(8):
            rp, trp = reps.get()
            for c3 in range(3):
                pp, tpp = prot.get()
                C.mm(pp[:, 0:510], sel[:, h * 128:(h + 1) * 128], pall[:, c3 * 510:(c3 + 1) * 510], r=[tsel, tp], w=[tpp])
                C.cp(rp[:, c3 * 510:(c3 + 1) * 510], pp[:, 0:510], r=[tpp], w=[trp], eng=("scalar" if c3 % 2 else "vector"))
            tb = Trk()
            C.dma("sync", bvd[h], rp[:, :], r=[trp], w=[tb])
            for g in range(3):
                for wch in range(2):
                    src = bass.AP(bvd, h * 128 * 1530 + (g * 2 + wch) * 255 + 127, [[1529, 128], [1, 128]])
                    C.dma("sync", btile[g, h, wch], src, r=[tb])
        C.S.emit()


def attn_core(C, hd, KT, tKT, VT, tVT, QT, tQT, bc, bp, tbias, dil, scale, ident, tid, ones, tones,
              oacc, toacc, dacc, tdacc, first, kprev_dec, vprev_dec, tprevd, pools, aug, textra=()):
    srot, orot, vrot, prot_sb, vt = pools
    nb = T // (128 * dil)
    hv = hd + 1 if aug else hd

    def cols(X, unit):
        kind, a, b = unit
        if kind == "p":
            s0 = a + dil * 128 * b
            return X[:, s0:s0 + dil * 127 + 1:dil]
        return X[:, dec_base(a):dec_base(a) + 128]

    units = [("p", r, b) for r in range(dil) for b in range(nb)] + [("d", n, 0) for n in range(ND)]
    tvt = {}
    for ui, u in enumerate(units):
        pt, tpt = vrot.get()
        C.tr(pt[:, 0:hd], cols(VT, u), ident[0:hd, 0:hd], r=[tVT, tid], w=[tpt])
        tvt[ui] = Trk()
        C.cp(vt[:, ui, 0:hd], pt[:, 0:hd], r=[tpt], w=[tvt[ui]], eng=("scalar" if ui % 2 else "vector"))
    for ui, u in enumerate(units):
        kind, a, b = u
        has_prev = (kind == "d") or b > 0
        sp, tsp = srot.get()
        C.mm(sp[:, 0:128], cols(KT, u), cols(QT, u), start=True, stop=False, r=[tKT, tQT], w=[tsp])
        C.mm(sp[:, 0:128], ident[:, :], bc, start=False, stop=True, r=[tid, tbias], w=[tsp])
        if has_prev:
            if kind == "p":
                kprev = cols(KT, ("p", a, b - 1))
                rk = [tKT]
            else:
                kprev = kprev_dec[:, a, :]
                rk = [tprevd]
            C.mm(sp[:, 128:256], kprev, cols(QT, u), start=True, stop=False, r=rk + [tQT], w=[tsp])
            C.mm(sp[:, 128:256], ident[:, :], bp, start=False, stop=True, r=[tid, tbias], w=[tsp])
        width = 256 if has_prev else 128
        pb, tpb = prot_sb.get()
        C.act(pb[:, 0:width], sp[:, 0:width], AF.Exp, scale=scale, r=[tsp], w=[tpb])
        op_, top = orot.get()
        if has_prev:
            vprev = vt[:, ui - 1, 0:hv] if kind == "p" else vprev_dec[:, a, 0:hv]
            rv_ = [tvt[ui - 1]] if kind == "p" else [tprevd]
        C.mm(op_[0:hv, 0:128], vt[:, ui, 0:hv], pb[:, 0:128], start=True, stop=not has_prev, r=[tvt[ui], tpb] + list(textra), w=[top])
        if has_prev:
            C.mm(op_[0:hv, 0:128], vprev, pb[:, 128:256], start=False, stop=True, r=rv_ + [tpb], w=[top])
        if not aug:
            C.mm(op_[0:1, 128:256], ones[:, 0:1], pb[:, 0:128], start=True, stop=not has_prev, r=[tones, tpb], w=[top])
            if has_prev:
                C.mm(op_[0:1, 128:256], ones[:, 0:1], pb[:, 128:256], start=False, stop=True, r=[tones, tpb], w=[top])
        oc = cols(oacc, u)
        if first:
            C.cp(oc[0:hv, :], op_[0:hv, 0:128], r=[top], w=[toacc], eng="vector")
            if not aug:
                C.cp(cols(dacc, u), op_[0:1, 128:256], r=[top], w=[tdacc], eng="scalar")
        else:
            C.tt(oc[0:hv, :], oc[0:hv, :], op_[0:hv, 0:128], ALU.add, r=[top, toacc], w=[toacc])
            C.tt(cols(dacc, u), cols(dacc, u), op_[0:1, 128:256], ALU.add, r=[top, tdacc], w=[tdacc])


def nat_rows_out(C, XT, tXT, hd, ident, tid, vrot, strot, c0, nblk, dst_fn):
    for i in range(nblk):
        pt, tpt = vrot.get()
        C.tr(pt[:, 0:hd], XT[:, c0 + 128 * i:c0 + 128 * (i + 1)], ident[0:hd, 0:hd], r=[tXT, tid], w=[tpt])
        s_, ts_ = strot.get()
        C.cp(s_[:, 0:hd], pt[:, 0:hd], r=[tpt], w=[ts_], eng=("scalar" if i % 2 else "vector"))
        C.S.op("sync", lambda e, d=dst_fn(i), s_=s_: e.dma_start(out=d, in_=s_[:, 0:hd], allow_slow_non_contiguous=True), r=[ts_], dma=True)


def phase_dil(C, l, btile, ident_d, caches, o_caches):
    scale = 128.0 ** -0.5
    C.ph = ExitStack()
    with C.ph:
        ident = C.sb([128, 128]); tid = Trk()
        C.dma("sync", ident[:, :], ident_d[:, :], w=[tid])
        ones = C.sb([128, 128]); tones = Trk()
        C.memset(ones[:, :], 1.0, w=[tones])
        KT = C.sb([128, TT]); tKT = Trk()
        VT = C.sb([128, TT]); tVT = Trk()
        QT = C.sb([128, TT]); tQT = Trk()
        oacc = C.sb([128, TT]); toacc = Trk()
        dacc = C.sb([1, TT]); tdacc = Trk()
        bias = C.sb([128, 2, 128]); tbias = Trk()
        vt = C.sb([128, NCH, 128])
        kpd = C.sb([128, ND, 128]); vpd = C.sb([128, ND, 128]); tprevd = Trk()
        kraw = C.sb([128, ND, 128]); tkraw = Trk()
        srot = Rot([(C.ps([128, 512]), PTrk()) for _ in range(2)])
        orot = Rot([(C.ps([128, 512]), PTrk()) for _ in range(2)])
        vrot = Rot([(C.ps([128, 512]), PTrk()) for _ in range(2)])
        brot = Rot([(C.ps([128, 512]), PTrk()) for _ in range(2)])
        prot_sb = Rot([(C.sb([128, 256]), Trk()) for _ in range(3)])
        strot = Rot([(C.sb([128, 128]), Trk()) for _ in range(3)])
        pools = (srot, orot, vrot, prot_sb, vt)
        ys = C.sb([128, TT], BF16); tys = Trk()
        for qh in range(8):
            kh = qh // 4
            for g in range(3):
                if 'one' in DILOPT and (qh > 0 or g != int(next((x[1:] for x in DILOPT if x.startswith('g')), '0'))):
                    continue
                dil = DILS[g]
                win = WINS[g]
                gk, gv, gq = GIDX[f"bk{g}_{kh}"], GIDX[f"bv{g}_{kh}"], GIDX[f"bq{g}_{qh}"]
                C.dma("sync", KT[:, :], C.hT[gk * 128:(gk + 1) * 128, :], w=[tKT])
                C.dma("sync", VT[:, :], C.hT[gv * 128:(gv + 1) * 128, :], w=[tVT])
                C.dma("sync", QT[:, :], C.hT[gq * 128:(gq + 1) * 128, :], w=[tQT])
                C.dma("sync", bias[:, 0, :], btile[g, qh, 0], w=[tbias])
                C.dma("sync", bias[:, 1, :], btile[g, qh, 1], w=[tbias])
                cin = caches[g]
                for n in range(ND):
                    C.dma("sync", kraw[:, n, :], cin[l, n, 0:win:dil, 0, kh, :], w=[tkraw])
                    C.dma("sync", vpd[:, n, :], cin[l, n, 0:win:dil, 1, kh, :], w=[tprevd])
                for n in range(ND):
                    pt, tpt = vrot.get()
                    C.tr(pt[:, 0:128], kraw[:, n, :], ident[:, :], r=[tkraw, tid], w=[tpt])
                    C.cp(kpd[:, n, :], pt[:, 0:128], r=[tpt], w=[tprevd])
                if 'nocore' not in DILOPT:
                    attn_core(C, 128, KT, tKT, VT, tVT, QT, tQT, bias[:, 0, :], bias[:, 1, :], tbias, dil, scale, ident, tid,
                              ones, tones, oacc, toacc, dacc, tdacc, g == 0, kpd, vpd, tprevd, pools, False)
                if qh % 4 == 0 and 'nocache' not in DILOPT:
                    oc = o_caches[g]
                    for which, XT, tX in ((0, KT, tKT), (1, VT, tVT)):
                        nat_rows_out(C, XT, tX, 128, ident, tid, vrot, strot, T - win, win // 128,
                                     lambda i, which=which, oc=oc: oc[l, 0, 128 * i:128 * (i + 1), which, kh, :])
                        for n in range(ND):
                            b0 = dec_base(n)
                            for r0_ in range(0, win - 1, 256):
                                r1_ = min(win - 1, r0_ + 256)
                                C.dma("sync", oc[l, 1 + n, r0_:r1_, which, kh, :], cin[l, n, 1 + r0_:1 + r1_, which, kh, :])
                            C.S.op("sync", lambda e, XT=XT, b0=b0, n=n, which=which, oc=oc, win=win, kh=kh: e.dma_start(
                                out=oc[l, 1 + n, win - 1, which, kh, :].rearrange("(p o) -> p o", o=1), in_=XT[:, b0:b0 + 1],
                                allow_slow_non_contiguous=True), r=[tX], dma=True)
            gb = GIDX[f"bg{qh}"]
            C.dma("sync", QT[:, :], C.hT[gb * 128:(gb + 1) * 128, :], w=[tQT])
            C.act(QT[:, :], QT[:, :], AF.Silu, r=[tQT], w=[tQT])
            C.S.op("vector", lambda e: e.reciprocal(out=dacc[:, :], in_=dacc[:, :]), r=[tdacc], w=[tdacc])
            for t in range(NST):
                pb_, tpb_ = brot.get()
                sl = slice(t * 512, (t + 1) * 512)
                C.mm(pb_[:, :], ones[0:1, :], dacc[0:1, sl], r=[tones, tdacc], w=[tpb_])
                C.tt(oacc[:, sl], oacc[:, sl], pb_[:, :], ALU.mult, r=[toacc, tpb_], w=[toacc])
                C.tt(ys[:, sl], oacc[:, sl], QT[:, sl], ALU.mult, r=[toacc, tQT], w=[tys], eng="gpsimd")
            C.dma("sync", C.ysT[1024 + qh * 128:1024 + (qh + 1) * 128, :], ys[:, :], r=[tys])
        C.S.emit()


def phase_swa(C, l, masks_d, ident_d, rope_d, sink_d, cswa, o_cswa):
    scale = 64.0 ** -0.5
    C.ph = ExitStack()
    with C.ph:
        ident = C.sb([128, 128]); tid = Trk()
        C.dma("sync", ident[:, :], ident_d[:, :], w=[tid])
        ones = C.sb([128, 128]); tones = Trk()
        C.memset(ones[:, :], 1.0, w=[tones])
        msk = C.sb([128, 2, 128]); tmsk = Trk()
        C.dma("sync", msk[:, 0, :], masks_d[0], w=[tmsk])
        C.dma("sync", msk[:, 1, :], masks_d[1], w=[tmsk])
        cs = C.sb([64, 2, TT]); tcs = Trk()
        C.dma("sync", cs[:, 0, :], rope_d[0], w=[tcs])
        C.dma("sync", cs[:, 1, :], rope_d[1], w=[tcs])
        snk = C.sb([128, 16]); tsnk = Trk()
        C.dma("sync", snk[:, :], sink_d[l], w=[tsnk])
        C.act(snk[:, :], snk[:, :], AF.Exp, r=[tsnk], w=[tsnk])
        KT = C.sb([64, TT]); tKT = Trk()
        VT = C.sb([64, TT]); tVT = Trk()
        QT = C.sb([64, TT]); tQT = Trk()
        SW = C.sb([64, TT]); tSW = Trk()
        oacc = C.sb([65, TT]); toacc = Trk()
        vt = C.sb([128, NCH, 65])
        tvtones = Trk()
        C.memset(vt[:, :, 64:65], 1.0, w=[tvtones])
        kpd = C.sb([64, ND, 128]); vpd = C.sb([128, ND, 65]); tprevd = Trk()
        C.memset(vpd[:, :, 64:65], 1.0, w=[tprevd])
        kraw = C.sb([128, ND, 64]); tkraw = Trk()
        srot = Rot([(C.ps([128, 512]), PTrk()) for _ in range(2)])
        orot = Rot([(C.ps([128, 512]), PTrk()) for _ in range(2)])
        vrot = Rot([(C.ps([128, 512]), PTrk()) for _ in range(2)])
        brot = Rot([(C.ps([128, 512]), PTrk()) for _ in range(2)])
        prot_sb = Rot([(C.sb([128, 256]), Trk()) for _ in range(3)])
        strot = Rot([(C.sb([128, 128]), Trk()) for _ in range(3)])
        pools = (srot, orot, vrot, prot_sb, vt)
        ys = C.sb([64, TT], BF16); tys = Trk()
        gk, gv = GIDX["dk"], GIDX["dv"]

        def rope(X, tX, src_rows):
            C.dma("sync", X[:, :], C.hT[src_rows:src_rows + 64, :], w=[tX])
            C.dma("sync", SW[0:32, :], C.hT[src_rows + 32:src_rows + 64, :], w=[tSW])
            C.dma("sync", SW[32:64, :], C.hT[src_rows:src_rows + 32, :], w=[tSW])
            C.tt(X[:, :], X[:, :], cs[:, 0, :], ALU.mult, r=[tX, tcs], w=[tX])
            C.tt(SW[:, :], SW[:, :], cs[:, 1, :], ALU.mult, r=[tSW, tcs], w=[tSW], eng="gpsimd")
            C.tt(X[:, :], X[:, :], SW[:, :], ALU.add, r=[tX, tSW], w=[tX])

        for kh in range(2):
            rope(KT, tKT, gk * 128 + kh * 64)
            C.dma("sync", VT[0:64, :], C.hT[gv * 128 + kh * 64:gv * 128 + kh * 64 + 64, :], w=[tVT])
            for n in range(ND):
                C.dma("sync", kraw[:, n, :], cswa[l, n, :, 0, kh, :], w=[tkraw])
                C.dma("sync", vpd[:, n, 0:64], cswa[l, n, :, 1, kh, :], w=[tprevd])
            for n in range(ND):
                pt, tpt = vrot.get()
                C.tr(pt[0:64, 0:128], kraw[:, n, :], ident[:, :], r=[tkraw, tid], w=[tpt])
                C.cp(kpd[:, n, :], pt[0:64, 0:128], r=[tpt], w=[tprevd])
            for which, XT, tX in ((0, KT, tKT), (1, VT, tVT)):
                nat_rows_out(C, XT, tX, 64, ident, tid, vrot, strot, T - 128, 1,
                             lambda i, which=which: o_cswa[l, 0, :, which, kh, :])
                for n in range(ND):
                    b0 = dec_base(n)
                    C.dma("sync", o_cswa[l, 1 + n, 0:127, which, kh, :], cswa[l, n, 1:128, which, kh, :])
                    C.S.op("sync", lambda e, XT=XT, b0=b0, n=n, which=which, kh=kh: e.dma_start(
                        out=o_cswa[l, 1 + n, 127, which, kh, :].rearrange("(p o) -> p o", o=1), in_=XT[0:64, b0:b0 + 1],
                        allow_slow_non_contiguous=True), r=[tX], dma=True)
            for hh in range(8):
                h = kh * 8 + hh
                gq = GIDX[f"dq{h // 2}"]
                rope(QT, tQT, gq * 128 + (h % 2) * 64)
                attn_core(C, 64, KT, tKT, VT, tVT, QT, tQT, msk[:, 0, :], msk[:, 1, :], tmsk, 1, scale, ident, tid,
                          ones, tones, oacc, toacc, None, None, True, kpd, vpd, tprevd, pools, True, textra=[tvtones])
                C.ts(oacc[64:65, :], oacc[64:65, :], snk[64:65, h:h + 1], ALU.add, r=[toacc, tsnk], w=[toacc])
                C.S.op("vector", lambda e: e.reciprocal(out=oacc[64:65, :], in_=oacc[64:65, :]), r=[toacc], w=[toacc])
                gg = GIDX[f"dg{h // 2}"]
                C.dma("sync", QT[:, :], C.hT[gg * 128 + (h % 2) * 64:gg * 128 + (h % 2) * 64 + 64, :], w=[tQT])
                C.act(QT[:, :], QT[:, :], AF.Silu, r=[tQT], w=[tQT])
                for t in range(NST):
                    pb_, tpb_ = brot.get()
                    sl = slice(t * 512, (t + 1) * 512)
                    C.mm(pb_[0:64, :], ones[64:65, 0:64], oacc[64:65, sl], r=[tones, toacc], w=[tpb_])
                    C.tt(oacc[0:64, sl], oacc[0:64, sl], pb_[0:64, :], ALU.mult, r=[toacc, tpb_], w=[toacc])
                    C.tt(ys[:, sl], oacc[0:64, sl], QT[:, sl], ALU.mult, r=[toacc, tQT], w=[tys], eng="gpsimd")
                C.dma("sync", C.ysT[3072 + h * 64:3072 + (h + 1) * 64, :], ys[:, :], r=[tys])
        C.S.emit()


def phase_mirror_ys(C):
    C.ph = ExitStack()
    with C.ph:
        chain = [Trk() for _ in range(4)]
        for n in range(ND):
            b0 = dec_base(n)
            for rb in (1024, 3072):
                for r0 in range(rb, rb + 1024, 128):
                    C.S.op("sync", lambda e, b0=b0, r0=r0: e.dma_start(out=C.ysT[r0:r0 + 128, b0 + 3:b0 + 4], in_=C.ysT[r0:r0 + 128, b0:b0 + 1],
                                                                       allow_slow_non_contiguous=True), w=[chain[(r0 // 128) % 4]], dma=True)
        C.S.emit()


def phase_gdn(C, l, ident_d, sel_d, gmask_d, tmask_d, gcw_d, gal_d, gnw_d, st_gdn, o_gdn, o_gdnc):
    BIG = 1.0e4
    C.ph = ExitStack()
    with C.ph:
        ident = C.sb([128, 128]); tid = Trk()
        C.dma("sync", ident[:, :], ident_d[:, :], w=[tid])
        ones = C.sb([128, 128]); tones = Trk()
        C.memset(ones[:, :], 1.0, w=[tones])
        sel = C.sb([8, 8 * 128]); tsel = Trk()
        C.dma("sync", sel[:, :], sel_d[:, :], w=[tsel])
        gm = C.sb([128, 3, 128]); tgm = Trk()
        for i in range(3):
            C.dma("sync", gm[:, i, :], gmask_d[i], w=[tgm])
        cw = C.sb([128, 24 * 4]); tcw = Trk()
        C.dma("sync", cw[:, :], gcw_d[l], w=[tcw])
        gal = C.sb([8, 2]); tgal = Trk()
        C.dma("sync", gal[:, :], gal_d[l], w=[tgal])
        C.act(gal[:, 0:1], gal[:, 0:1], AF.Exp, r=[tgal], w=[tgal])
        C.ts(gal[:, 0:1], gal[:, 0:1], -1.0, ALU.mult, r=[tgal], w=[tgal])
        nw = C.sb([128, 1]); tnw = Trk()
        C.dma("sync", nw[:, :], gnw_d[l], w=[tnw])
        S = [C.sb([128, 128]) for _ in range(8)]
        tS = [Trk() for _ in range(8)]
        RW = {k: (C.sb([8, 512]), Trk()) for k in ("b", "a", "tm", "rm", "beta", "lb", "g", "gc", "gcl", "eg", "bg", "ekd")}
        egl = C.sb([8, 4]); tegl = Trk()
        colsb = C.sb([128, 4, 5, 8]); tcols = Trk()
        def mkset():
            U = dict(halo=[C.sb([128, 515]) for _ in range(3)], thalo=[Trk() for _ in range(3)],
                     zt=C.sb([128, 512]), tzt=Trk(), qs=C.sb([128, 512]), ks=C.sb([128, 512]), vs=C.sb([128, 512]),
                     tqs=Trk(), tks=Trk(), tvs=Trk(), sq=C.sb([128, 512]), tsq=Trk(), rrow=C.sb([1, 2, 512]), trrow=Trk(),
                     gcb=C.sb([128, 512]), gclb=C.sb([128, 512]), tgcb=Trk(), tgclb=Trk(), qg=C.sb([128, 512]), tqg=Trk(),
                     eglb=C.sb([128, 4]), teglb=Trk(), oT=C.sb([128, 512]), toT=Trk(), ysb=C.sb([128, 512], BF16), tysb=Trk(),
                     vn=C.sb([128, 128]), tvn=Trk(), streams=[])
            for _ in range(2):
                U["streams"].append(dict(
                    Rm=C.sb([128, 256]), tR=Trk(), kd=C.sb([128, 128]), tkd=Trk(),
                    Lp=[C.sb([128, 128]) for _ in range(2)], tLp=[Trk(), Trk()],
                    Ltp=[C.sb([128, 128]) for _ in range(2)], tLtp=[Trk(), Trk()],
                    dtile=C.sb([128, 3, 128]), tdt=[Trk(), Trk(), Trk()],
                    qkm=C.sb([128, 128]), tqkm=Trk(), wT=C.sb([128, 128]), twT=Trk(),
                    bank=(C.ps([128, 512]), PTrk())))
            return U
        USETS = [mkset(), mkset()]
        pa = Rot([(C.ps([128, 512]), PTrk()) for _ in range(2)])
        pb = Rot([(C.ps([128, 512]), PTrk()) for _ in range(2)])
        for s in range(NST):
            c0 = s * 512
            sl = slice(c0, c0 + 512)
            ga = GIDX["aba"]
            for k_, rows in (("b", 0), ("a", 8)):
                C.dma("sync", RW[k_][0][:, :], C.hT[ga * 128 + rows:ga * 128 + rows + 8, sl], w=[RW[k_][1]])
            C.dma("sync", RW["tm"][0][:, :], tmask_d[0, :, sl], w=[RW["tm"][1]])
            C.dma("sync", RW["rm"][0][:, :], tmask_d[1, :, sl], w=[RW["rm"][1]])
            R_ = lambda k: RW[k][0][:, :]
            t_ = lambda k: RW[k][1]
            C.act(R_("beta"), R_("b"), AF.Sigmoid, r=[t_("b")], w=[t_("beta")])
            C.tt(R_("beta"), R_("beta"), R_("tm"), ALU.mult, r=[t_("beta"), t_("tm")], w=[t_("beta")])
            C.ts(R_("lb"), R_("beta"), 1e-18, ALU.add, r=[t_("beta")], w=[t_("lb")])
            C.act(R_("lb"), R_("lb"), AF.Ln, r=[t_("lb")], w=[t_("lb")])
            C.act(R_("g"), R_("a"), AF.Exp, bias=gal[:, 1:2], r=[t_("a"), tgal], w=[t_("g")])
            C.act(R_("g"), R_("g"), AF.Ln, bias=1.0, r=[t_("g")], w=[t_("g")])
            C.ts(R_("g"), R_("g"), gal[:, 0:1], ALU.mult, r=[t_("g"), tgal], w=[t_("g")])
            C.tt(R_("g"), R_("g"), R_("tm"), ALU.mult, r=[t_("g"), t_("tm")], w=[t_("g")])
            C.S.op("vector", lambda e: e.tensor_tensor_scan(out=RW["gc"][0][:, :], data0=RW["rm"][0][:, :], data1=RW["g"][0][:, :],
                                                           initial=0.0, op0=ALU.mult, op1=ALU.add), r=[t_("rm"), t_("g")], w=[t_("gc")])
            C.tt(R_("gcl"), R_("gc"), R_("lb"), ALU.add, r=[t_("gc"), t_("lb")], w=[t_("gcl")])
            C.act(R_("eg"), R_("gc"), AF.Exp, r=[t_("gc")], w=[t_("eg")])
            C.tt(R_("bg"), R_("beta"), R_("eg"), ALU.mult, r=[t_("beta"), t_("eg")], w=[t_("bg")])
            for k in range(4):
                cs_ = slice(k * 128, (k + 1) * 128)
                C.act(RW["ekd"][0][:, cs_], RW["gc"][0][:, cs_], AF.Exp, scale=-1.0, bias=RW["gc"][0][:, k * 128 + 127:k * 128 + 128],
                      r=[t_("gc")], w=[t_("ekd")])
                C.act(egl[:, k:k + 1], RW["gc"][0][:, k * 128 + 127:k * 128 + 128], AF.Exp, r=[t_("gc")], w=[tegl])
            pc_, tpc_ = pa.get()
            for k in range(4):
                for qi, q in enumerate(("beta", "bg", "ekd", "gc", "gcl")):
                    o_ = (k * 5 + qi) * 8
                    C.tr(pc_[:, o_:o_ + 8], RW[q][0][:, k * 128:(k + 1) * 128], ident[0:8, 0:8], r=[t_(q), tid], w=[tpc_])
            C.cp(colsb[:, :, :, :].rearrange("p a b c -> p (a b c)"), pc_[:, 0:160], r=[tpc_], w=[tcols])
            def head_gen(h, U):
                halo, thalo, zt, tzt, qs, ks, vs = U["halo"], U["thalo"], U["zt"], U["tzt"], U["qs"], U["ks"], U["vs"]
                tqs, tks, tvs, sq, tsq, rrow, trrow = U["tqs"], U["tks"], U["tvs"], U["sq"], U["tsq"], U["rrow"], U["trrow"]
                gcb, gclb, tgcb, tgclb, qg, tqg = U["gcb"], U["gclb"], U["tgcb"], U["tgclb"], U["qg"], U["tqg"]
                eglb, teglb, oT, toT, ysb, tysb, vn, tvn = U["eglb"], U["teglb"], U["oT"], U["toT"], U["ysb"], U["tysb"], U["vn"], U["tvn"]
                streams = U["streams"]
                col = lambda k, qi: colsb[:, k, qi, h:h + 1]
                selh = sel[:, h * 128:(h + 1) * 128]
                srcs = (GIDX[f"aq{h}"], GIDX[f"ak{h}"], GIDX[f"av{h}"])
                for i3 in range(3):
                    g_ = srcs[i3]
                    if s == 0:
                        C.memset(halo[i3][:, 0:3], 0.0, w=[thalo[i3]])
                        C.dma("sync", halo[i3][:, 3:515], C.hT[g_ * 128:(g_ + 1) * 128, 0:512], w=[thalo[i3]])
                    else:
                        C.dma("sync", halo[i3][:, :], C.hT[g_ * 128:(g_ + 1) * 128, c0 - 3:c0 + 512], w=[thalo[i3]])
                gz = GIDX[f"az{h}"]
                C.dma("sync", zt[:, :], C.hT[gz * 128:(gz + 1) * 128, sl], w=[tzt])
                segs = [(0, T - 1)] + [(1 + n, dec_base(n) + 3) for n in range(ND)]
                for si, last in segs:
                    if c0 <= last < c0 + 512:
                        lc = last - c0
                        for i3 in range(3):
                            cc0 = i3 * 1024 + h * 128
                            C.S.op("sync", lambda e, si=si, lc=lc, i3=i3, cc0=cc0: e.dma_start(
                                out=o_gdnc[l, si, :, cc0:cc0 + 128].rearrange("j p -> p j"), in_=halo[i3][:, lc + 1:lc + 4],
                                allow_slow_non_contiguous=True), r=[thalo[i3]], dma=True)
                for i3, (dst, tdst) in enumerate(((qs, tqs), (ks, tks), (vs, tvs))):
                    wv = lambda j, i3=i3: cw[:, (i3 * 8 + h) * 4 + j:(i3 * 8 + h) * 4 + j + 1]
                    C.act(dst[:, :], halo[i3][:, 0:512], AF.Identity, scale=wv(0), r=[thalo[i3], tcw], w=[tdst])
                    for j in range(1, 4):
                        C.stt(dst[:, :], halo[i3][:, j:j + 512], wv(j), dst[:, :], ALU.mult, ALU.add, r=[thalo[i3], tcw, tdst], w=[tdst])
                    C.act(dst[:, :], dst[:, :], AF.Silu, r=[tdst], w=[tdst])
                    yield
                for i2, (src, tsrc, extra) in enumerate(((qs, tqs, math.log(128.0 ** -0.5)), (ks, tks, 0.0))):
                    C.act(sq[:, :], src[:, :], AF.Square, r=[tsrc], w=[tsq])
                    pp, tpp = pa.get()
                    C.mm(pp[0:1, :], ones[:, 0:1], sq[:, :], r=[tones, tsq], w=[tpp])
                    C.ts(rrow[:, i2, :], pp[0:1, :], 1e-6, ALU.add, r=[tpp], w=[trrow])
                    C.act(rrow[:, i2, :], rrow[:, i2, :], AF.Ln, r=[trrow], w=[trrow])
                    C.act(rrow[:, i2, :], rrow[:, i2, :], AF.Exp, scale=-0.5, bias=extra, r=[trrow], w=[trrow])
                    pp2, tpp2 = pa.get()
                    C.mm(pp2[:, :], ones[0:1, :], rrow[:, i2, :], r=[tones, trrow], w=[tpp2])
                    C.tt(src[:, :], src[:, :], pp2[:, :], ALU.mult, r=[tsrc, tpp2], w=[tsrc])
                    yield
                pp, tpp = pa.get()
                C.mm(pp[:, :], selh, RW["gc"][0][:, :], r=[tsel, t_("gc")], w=[tpp])
                C.cp(gcb[:, :], pp[:, :], r=[tpp], w=[tgcb], eng="scalar")
                yield
                pp, tpp = pa.get()
                C.mm(pp[:, :], selh, RW["gcl"][0][:, :], r=[tsel, t_("gcl")], w=[tpp])
                C.cp(gclb[:, :], pp[:, :], r=[tpp], w=[tgclb], eng="scalar")
                yield
                pp, tpp = pa.get()
                C.mm(pp[:, :], selh, RW["eg"][0][:, :], r=[tsel, t_("eg")], w=[tpp])
                C.tt(qg[:, :], qs[:, :], pp[:, :], ALU.mult, r=[tqs, tpp], w=[tqg])
                pp, tpp = pa.get()
                C.mm(pp[:, 0:4], selh, egl[:, :], r=[tsel, tegl], w=[tpp])
                C.cp(eglb[:, :], pp[:, 0:4], r=[tpp], w=[teglb])
                yield
                def solve_gen(k, B):
                    cs_ = slice(k * 128, (k + 1) * 128)
                    bk, tbk = B["bank"]
                    Rm, tR, kd, tkd = B["Rm"], B["tR"], B["kd"], B["tkd"]
                    Lp, tLp, Ltp, tLtp = B["Lp"], B["tLp"], B["Ltp"], B["tLtp"]
                    dtile, tdt, qkm, tqkm, wT, twT = B["dtile"], B["tdt"], B["qkm"], B["tqkm"], B["wT"], B["twT"]
                    C.tr(bk[:, 0:128], ks[:, cs_], ident[:, :], r=[tks, tid], w=[tbk])
                    C.tr(bk[:, 128:256], vs[:, cs_], ident[:, :], r=[tvs, tid], w=[tbk])
                    yield
                    C.ts(Rm[:, 128:256], bk[:, 0:128], col(k, 1), ALU.mult, r=[tbk, tcols], w=[tR])
                    C.ts(kd[:, :], bk[:, 0:128], col(k, 2), ALU.mult, r=[tbk, tcols], w=[tkd])
                    C.ts(Rm[:, 0:128], bk[:, 128:256], col(k, 0), ALU.mult, r=[tbk, tcols], w=[tR])
                    yield
                    C.mm(bk[:, 256:384], ks[:, cs_], ks[:, cs_], r=[tks], w=[tbk])
                    C.mm(bk[:, 384:512], ks[:, cs_], qs[:, cs_], r=[tks, tqs], w=[tbk])
                    C.stt(dtile[:, 0, :], gclb[:, cs_], col(k, 3), gm[:, 0, :], ALU.subtract, ALU.min, r=[tgclb, tcols, tgm], w=[tdt[0]])
                    C.act(dtile[:, 0, :], dtile[:, 0, :], AF.Exp, r=[tdt[0]], w=[tdt[0]])
                    C.stt(dtile[:, 1, :], gcb[:, cs_], col(k, 4), gm[:, 2, :], ALU.subtract, ALU.max, r=[tgcb, tcols, tgm], w=[tdt[1]])
                    C.act(dtile[:, 1, :], dtile[:, 1, :], AF.Exp, scale=-1.0, r=[tdt[1]], w=[tdt[1]])
                    C.stt(dtile[:, 2, :], gcb[:, cs_], col(k, 3), gm[:, 1, :], ALU.subtract, ALU.min, r=[tgcb, tcols, tgm], w=[tdt[2]])
                    C.act(dtile[:, 2, :], dtile[:, 2, :], AF.Exp, r=[tdt[2]], w=[tdt[2]])
                    yield
                    C.tt(Ltp[0][:, :], bk[:, 256:384], dtile[:, 0, :], ALU.mult, r=[tbk, tdt[0]], w=[tLtp[0]])
                    C.tt(Lp[0][:, :], bk[:, 256:384], dtile[:, 1, :], ALU.mult, r=[tbk, tdt[1]], w=[tLp[0]])
                    C.tt(qkm[:, :], bk[:, 384:512], dtile[:, 2, :], ALU.mult, r=[tbk, tdt[2]], w=[tqkm])
                    yield
                    C.mm(bk[:, 0:256], Ltp[0][:, :], Rm[:, :], r=[tLtp[0], tR], w=[tbk])
                    yield
                    C.tt(Rm[:, :], Rm[:, :], bk[:, 0:256], ALU.subtract, r=[tR, tbk], w=[tR])
                    cur = 0
                    for lvl in range(1, 7):
                        nxt = 1 - cur
                        C.mm(bk[:, 256:384], Lp[cur][:, :], Ltp[cur][:, :], r=[tLp[cur], tLtp[cur]], w=[tbk])
                        if lvl < 6:
                            C.mm(bk[:, 384:512], Ltp[cur][:, :], Lp[cur][:, :], r=[tLp[cur], tLtp[cur]], w=[tbk])
                        yield
                        C.cp(Ltp[nxt][:, :], bk[:, 256:384], r=[tbk], w=[tLtp[nxt]], eng="scalar")
                        if lvl < 6:
                            C.cp(Lp[nxt][:, :], bk[:, 384:512], r=[tbk], w=[tLp[nxt]], eng="scalar")
                        yield
                        C.mm(bk[:, 0:256], Ltp[nxt][:, :], Rm[:, :], r=[tLtp[nxt], tR], w=[tbk])
                        yield
                        C.tt(Rm[:, :], Rm[:, :], bk[:, 0:256], ALU.add, r=[tR, tbk], w=[tR])
                        cur = nxt
                    C.tr(bk[:, 0:128], Rm[:, 128:256], ident[:, :], r=[tR, tid], w=[tbk])
                    yield
                    C.cp(wT[:, :], bk[:, 0:128], r=[tbk], w=[twT], eng="scalar")

                for pair in ((0, 1), (2, 3)):
                    gens = [solve_gen(k, streams[i]) for i, k in enumerate(pair)]
                    alive = list(gens)
                    while alive:
                        for g_ in list(alive):
                            try:
                                next(g_)
                            except StopIteration:
                                alive.remove(g_)
                        yield
                    for i, k in enumerate(pair):
                        ci = 4 * s + k
                        cs_ = slice(k * 128, (k + 1) * 128)
                        B = streams[i]
                        seg_end = (ci == 31) or (ci >= 32)
                        if ci == 0:
                            C.memset(S[h][:, :], 0.0, w=[tS[h]])
                        elif ci >= 32:
                            C.dma("sync", S[h][:, :], st_gdn[l, ci - 32, h], w=[tS[h]])
                        p6, tp6 = pb.get()
                        C.mm(p6[:, 0:128], B["wT"][:, :], S[h][:, :], r=[B["twT"], tS[h]], w=[tp6])
                        yield
                        C.tt(vn[:, :], B["Rm"][:, 0:128], p6[:, 0:128], ALU.subtract, r=[B["tR"], tp6], w=[tvn])
                        yield
                        p7, tp7 = pb.get()
                        C.mm(p7[:, 0:128], S[h][:, :], qg[:, cs_], start=True, stop=False, r=[tS[h], tqg], w=[tp7])
                        C.mm(p7[:, 0:128], vn[:, :], B["qkm"][:, :], start=False, stop=True, r=[tvn, B["tqkm"]], w=[tp7])
                        C.mm(p7[:, 128:256], B["kd"][:, :], vn[:, :], r=[B["tkd"], tvn], w=[tp7])
                        yield
                        C.stt(S[h][:, :], S[h][:, :], eglb[:, k:k + 1], p7[:, 128:256], ALU.mult, ALU.add, r=[tS[h], teglb, tp7], w=[tS[h]])
                        C.cp(oT[:, cs_], p7[:, 0:128], r=[tp7], w=[toT], eng="vector")
                        if seg_end:
                            si = 0 if ci == 31 else 1 + (ci - 32)
                            C.dma("sync", o_gdn[l, si, h], S[h][:, :], r=[tS[h]])
                        yield
                C.act(sq[:, :], oT[:, :], AF.Square, r=[toT], w=[tsq])
                pp, tpp = pa.get()
                C.mm(pp[0:1, :], ones[:, 0:1], sq[:, :], r=[tones, tsq], w=[tpp])
                C.ts(rrow[:, 0, :], pp[0:1, :], 1.0 / 128.0, ALU.mult, 1e-6, ALU.add, r=[tpp], w=[trrow])
                C.act(rrow[:, 0, :], rrow[:, 0, :], AF.Ln, r=[trrow], w=[trrow])
                C.act(rrow[:, 0, :], rrow[:, 0, :], AF.Exp, scale=-0.5, r=[trrow], w=[trrow])
                pp2, tpp2 = pa.get()
                C.mm(pp2[:, :], ones[0:1, :], rrow[:, 0, :], r=[tones, trrow], w=[tpp2])
                C.stt(oT[:, :], oT[:, :], nw[:, 0:1], pp2[:, :], ALU.mult, ALU.mult, r=[toT, tnw, tpp2], w=[toT])
                C.act(zt[:, :], zt[:, :], AF.Silu, r=[tzt], w=[tzt])
                C.tt(ysb[:, :], oT[:, :], zt[:, :], ALU.mult, r=[toT, tzt], w=[tysb])
                C.dma("sync", C.ysT[h * 128:(h + 1) * 128, sl], ysb[:, :], r=[tysb])

            for hp in range(0, 8, 2):
                hg = [head_gen(hp, USETS[0]), head_gen(hp + 1, USETS[1])]
                live = list(hg)
                while live:
                    for g_ in list(live):
                        try:
                            next(g_)
                        except StopIteration:
                            live.remove(g_)
        C.S.emit()


def kernel(**inputs):
    res = run_device(inputs)

    def gather(name):
        arrs = [np.asarray(res[c][name], dtype=np.float32) for c in range(8)]
        p = np.stack([arrs[0][:, 0], arrs[1][:, 0]], axis=1)
        s_ = np.concatenate([a[:, 1:] for a in arrs], axis=1)
        return np.ascontiguousarray(p), np.ascontiguousarray(s_)

    ys = [np.asarray(res[c]["y"], dtype=np.float32) for c in range(8)]
    y_prompt = np.stack([ys[0][:T], ys[1][:T]], axis=0)
    y_sample = np.stack([ys[c][dec_base(n) + 3] for c in range(8) for n in range(ND)], axis=0)[:, None, :]
    gp, gs = gather("o_gdn")
    gcp, gcs = gather("o_gdnc")
    d0p, d0s = gather("o_cd0")
    d1p, d1s = gather("o_cd1")
    d2p, d2s = gather("o_cd2")
    swp, sws = gather("o_cswa")
    lp_, ls_ = gather("o_lru")
    lcp, lcs = gather("o_lruc")
    return (np.ascontiguousarray(y_prompt), np.ascontiguousarray(y_sample), gp, gs, gcp, gcs, d0p, d0s, d1p, d1s, d2p, d2s,
            swp, sws, lp_, ls_, lcp, lcs)


def attn_core(C, hd, KT, tKT, VT, tVT, QT, tQT, bc, bp, tbias, dil, scale, ident, tid, ones, tones,
              oacc, toacc, dacc, tdacc, first, kprev_dec, vprev_dec, tprevd, pools, aug, textra=()):
    srot, orot, vrot, prot_sb, vt = pools
    nb = T // (128 * dil)
    hv = hd + 1 if aug else hd

    def cols(X, unit):
        kind, a, b = unit
        if kind == "p":
            s0 = a + dil * 128 * b
            return X[:, s0:s0 + dil * 127 + 1:dil]
        return X[:, dec_base(a):dec_base(a) + 128]

    units = [("p", r, b) for r in range(dil) for b in range(nb)] + [("d", n, 0) for n in range(ND)]
    tvt = {}
    for ui, u in enumerate(units):
        pt, tpt = vrot.get()
        C.tr(pt[:, 0:hd], cols(VT, u), ident[0:hd, 0:hd], r=[tVT, tid], w=[tpt])
        tvt[ui] = Trk()
        C.cp(vt[:, ui, 0:hd], pt[:, 0:hd], r=[tpt], w=[tvt[ui]], eng=("scalar" if ui % 2 else "vector"))
    for ui, u in enumerate(units):
        kind, a, b = u
        has_prev = (kind == "d") or b > 0
        sp, tsp = srot.get()
        C.mm(sp[:, 0:128], cols(KT, u), cols(QT, u), start=True, stop=False, r=[tKT, tQT], w=[tsp])
        C.mm(sp[:, 0:128], ident[:, :], bc, start=False, stop=True, r=[tid, tbias], w=[tsp])
        if has_prev:
            if kind == "p":
                kprev = cols(KT, ("p", a, b - 1))
                rk = [tKT]
            else:
                kprev = kprev_dec[:, a, :]
                rk = [tprevd]
            C.mm(sp[:, 128:256], kprev, cols(QT, u), start=True, stop=False, r=rk + [tQT], w=[tsp])
            C.mm(sp[:, 128:256], ident[:, :], bp, start=False, stop=True, r=[tid, tbias], w=[tsp])
        width = 256 if has_prev else 128
        pb, tpb = prot_sb.get()
        C.act(pb[:, 0:width], sp[:, 0:width], AF.Exp, scale=scale, r=[tsp], w=[tpb])
        op_, top = orot.get()
        if has_prev:
            vprev = vt[:, ui - 1, 0:hv] if kind == "p" else vprev_dec[:, a, 0:hv]
            rv_ = [tvt[ui - 1]] if kind == "p" else [tprevd]
        C.mm(op_[0:hv, 0:128], vt[:, ui, 0:hv], pb[:, 0:128], start=True, stop=not has_prev, r=[tvt[ui], tpb] + list(textra), w=[top])
        if has_prev:
            C.mm(op_[0:hv, 0:128], vprev, pb[:, 128:256], start=False, stop=True, r=rv_ + [tpb], w=[top])
        if not aug:
            C.mm(op_[0:1, 128:256], ones[:, 0:1], pb[:, 0:128], start=True, stop=not has_prev, r=[tones, tpb], w=[top])
            if has_prev:
                C.mm(op_[0:1, 128:256], ones[:, 0:1], pb[:, 128:256], start=False, stop=True, r=[tones, tpb], w=[top])
        oc = cols(oacc, u)
        if first:
            C.cp(oc[0:hv, :], op_[0:hv, 0:128], r=[top], w=[toacc], eng="vector")
            if not aug:
                C.cp(cols(dacc, u), op_[0:1, 128:256], r=[top], w=[tdacc], eng="vector")
        else:
            C.tt(oc[0:hv, :], oc[0:hv, :], op_[0:hv, 0:128], ALU.add, r=[top, toacc], w=[toacc])
            C.tt(cols(dacc, u), cols(dacc, u), op_[0:1, 128:256], ALU.add, r=[top, tdacc], w=[tdacc])


def phase_dil(C, l, btile, ident_d, caches, o_caches):
    scale = 128.0 ** -0.5
    C.ph = ExitStack()
    with C.ph:
        identF = C.sb([128, 128]); tidF = Trk()
        C.dma("sync", identF[:, :], ident_d[:, :], w=[tidF])
        ident = C.sb([128, 128], BF16); tid = Trk()
        C.dma("gpsimd", ident[:, :], ident_d[:, :], w=[tid])
        onesF = C.sb([128, 128]); tonesF = Trk()
        C.memset(onesF[:, :], 1.0, w=[tonesF])
        ones = C.sb([128, 128], BF16); tones = Trk()
        C.memset(ones[:, :], 1.0, w=[tones])
        sets = []
        for _ in range(2):
            sets.append(dict(KT=C.sb([128, TT], BF16), tKT=Trk(), VT=C.sb([128, TT], BF16), tVT=Trk(),
                             QT=C.sb([128, TT], BF16), tQT=Trk(), bias=C.sb([128, 2, 128], BF16), tbias=Trk(),
                             kpd=C.sb([128, ND, 128], BF16), vpd=C.sb([128, ND, 128], BF16), tprevd=Trk(),
                             kraw=C.sb([128, ND, 128], BF16), tkraw=Trk()))
        it = 0
        GT = C.sb([128, TT]); tGT = Trk()
        XF = C.sb([128, 2560]); tXF = Trk()
        oacc = C.sb([128, TT]); toacc = Trk()
        dacc = C.sb([1, TT]); tdacc = Trk()
        vt = C.sb([128, NCH, 128], BF16)
        srot = Rot([(C.ps([128, 512]), PTrk()) for _ in range(3)])
        orot = Rot([(C.ps([128, 512]), PTrk()) for _ in range(3)])
        vrot = Rot([(C.ps([128, 1024], BF16), PTrk()) for _ in range(2)])
        prot_sb = Rot([(C.sb([128, 256], BF16), Trk()) for _ in range(4)])
        strot = Rot([(C.sb([128, 128]), Trk()) for _ in range(3)])
        pools = (srot, orot, vrot, prot_sb, vt)
        ys = C.sb([128, TT], BF16); tys = Trk()
        for qh in range(8):
            kh = qh // 4
            for g in range(3):
                dil = DILS[g]
                win = WINS[g]
                Z = sets[it % 2]
                it += 1
                KT, tKT, VT, tVT, QT, tQT = Z["KT"], Z["tKT"], Z["VT"], Z["tVT"], Z["QT"], Z["tQT"]
                bias, tbias, kpd, vpd, tprevd, kraw, tkraw = Z["bias"], Z["tbias"], Z["kpd"], Z["vpd"], Z["tprevd"], Z["kraw"], Z["tkraw"]
                gk, gv, gq = GIDX[f"bk{g}_{kh}"], GIDX[f"bv{g}_{kh}"], GIDX[f"bq{g}_{qh}"]
                C.dma("gpsimd", KT[:, :], C.hT[gk * 128:(gk + 1) * 128, :], w=[tKT])
                C.dma("gpsimd", VT[:, :], C.hT[gv * 128:(gv + 1) * 128, :], w=[tVT])
                C.dma("gpsimd", QT[:, :], C.hT[gq * 128:(gq + 1) * 128, :], w=[tQT])
                C.dma("gpsimd", bias[:, 0, :], btile[g, qh, 0], w=[tbias])
                C.dma("gpsimd", bias[:, 1, :], btile[g, qh, 1], w=[tbias])
                cin = caches[g]
                for n in range(ND):
                    C.dma("gpsimd", kraw[:, n, :], cin[l, n, 0:win:dil, 0, kh, :], w=[tkraw])
                    C.dma("gpsimd", vpd[:, n, :], cin[l, n, 0:win:dil, 1, kh, :], w=[tprevd])
                for n in range(ND):
                    pt, tpt = vrot.get()
                    C.tr(pt[:, 0:128], kraw[:, n, :], ident[:, :], r=[tkraw, tid], w=[tpt])
                    C.cp(kpd[:, n, :], pt[:, 0:128], r=[tpt], w=[tprevd])
                attn_core(C, 128, KT, tKT, VT, tVT, QT, tQT, bias[:, 0, :], bias[:, 1, :], tbias, dil, scale, ident, tid,
                          ones, tones, oacc, toacc, dacc, tdacc, g == 0, kpd, vpd, tprevd, pools, False)
                if qh % 4 == 0:
                    oc = o_caches[g]
                    for which, gsrc in ((0, gk), (1, gv)):
                        C.dma("sync", XF[:, 0:win], C.hT[gsrc * 128:(gsrc + 1) * 128, T - win:T], w=[tXF])
                        C.dma("sync", XF[:, 2048:2560], C.hT[gsrc * 128:(gsrc + 1) * 128, T:TT], w=[tXF])
                        nat_rows_out(C, XF, tXF, 128, identF, tidF, orot, strot, 0, win // 128,
                                     lambda i, which=which, oc=oc, kh=kh: oc[l, 0, 128 * i:128 * (i + 1), which, kh, :])
                        for n in range(ND):
                            for r0_ in range(0, win - 1, 256):
                                r1_ = min(win - 1, r0_ + 256)
                                C.dma("sync", oc[l, 1 + n, r0_:r1_, which, kh, :], cin[l, n, 1 + r0_:1 + r1_, which, kh, :])
                            C.S.op("sync", lambda e, n=n, which=which, oc=oc, win=win, kh=kh: e.dma_start(
                                out=oc[l, 1 + n, win - 1, which, kh, :].rearrange("(p o) -> p o", o=1),
                                in_=XF[:, 2048 + 128 * n:2048 + 128 * n + 1], allow_slow_non_contiguous=True), r=[tXF], dma=True)
            gb = GIDX[f"bg{qh}"]
            C.dma("sync", GT[:, :], C.hT[gb * 128:(gb + 1) * 128, :], w=[tGT])
            C.act(GT[:, :], GT[:, :], AF.Silu, r=[tGT], w=[tGT])
            C.S.op("vector", lambda e: e.reciprocal(out=dacc[:, :], in_=dacc[:, :]), r=[tdacc], w=[tdacc])
            for t in range(NST):
                pb_, tpb_ = srot.get()
                sl = slice(t * 512, (t + 1) * 512)
                C.mm(pb_[:, :], onesF[0:1, :], dacc[0:1, sl], r=[tonesF, tdacc], w=[tpb_])
                C.tt(oacc[:, sl], oacc[:, sl], pb_[:, :], ALU.mult, r=[toacc, tpb_], w=[toacc])
                C.tt(ys[:, sl], oacc[:, sl], GT[:, sl], ALU.mult, r=[toacc, tGT], w=[tys])
            C.dma("sync", C.ysT[1024 + qh * 128:1024 + (qh + 1) * 128, :], ys[:, :], r=[tys])
        C.S.emit()


def phase_swa(C, l, masks_d, ident_d, rope_d, sink_d, cswa, o_cswa):
    scale = 64.0 ** -0.5
    C.ph = ExitStack()
    with C.ph:
        identF = C.sb([128, 128]); tidF = Trk()
        C.dma("sync", identF[:, :], ident_d[:, :], w=[tidF])
        ident = C.sb([128, 128], BF16); tid = Trk()
        C.dma("gpsimd", ident[:, :], ident_d[:, :], w=[tid])
        onesF = C.sb([128, 128]); tonesF = Trk()
        C.memset(onesF[:, :], 1.0, w=[tonesF])
        msk = C.sb([128, 2, 128], BF16); tmsk = Trk()
        C.dma("gpsimd", msk[:, 0, :], masks_d[0], w=[tmsk])
        C.dma("gpsimd", msk[:, 1, :], masks_d[1], w=[tmsk])
        cs = C.sb([64, 2, TT]); tcs = Trk()
        C.dma("sync", cs[:, 0, :], rope_d[0], w=[tcs])
        C.dma("sync", cs[:, 1, :], rope_d[1], w=[tcs])
        snk = C.sb([128, 16]); tsnk = Trk()
        C.dma("sync", snk[:, :], sink_d[l], w=[tsnk])
        C.act(snk[:, :], snk[:, :], AF.Exp, r=[tsnk], w=[tsnk])
        KF = C.sb([64, TT]); tKF = Trk()
        VF = C.sb([64, TT]); tVF = Trk()
        XF = C.sb([64, TT]); tXF = Trk()
        SW = C.sb([64, TT]); tSW = Trk()
        GF = C.sb([64, TT]); tGF = Trk()
        KT = C.sb([64, TT], BF16); tKT = Trk()
        VT = C.sb([64, TT], BF16); tVT = Trk()
        QTs = [(C.sb([64, TT], BF16), Trk()) for _ in range(2)]
        oacc = C.sb([65, TT]); toacc = Trk()
        vt = C.sb([128, NCH, 65], BF16)
        tvtones = Trk()
        C.memset(vt[:, :, 64:65], 1.0, w=[tvtones])
        kpd = C.sb([64, ND, 128], BF16); vpd = C.sb([128, ND, 65], BF16); tprevd = Trk()
        C.memset(vpd[:, :, 64:65], 1.0, w=[tprevd])
        kraw = C.sb([128, ND, 64], BF16); tkraw = Trk()
        srot = Rot([(C.ps([128, 512]), PTrk()) for _ in range(3)])
        orot = Rot([(C.ps([128, 512]), PTrk()) for _ in range(3)])
        vrot = Rot([(C.ps([128, 1024], BF16), PTrk()) for _ in range(2)])
        prot_sb = Rot([(C.sb([128, 256], BF16), Trk()) for _ in range(4)])
        strot = Rot([(C.sb([128, 128]), Trk()) for _ in range(3)])
        pools = (srot, orot, vrot, prot_sb, vt)
        ys = C.sb([64, TT], BF16); tys = Trk()
        gk, gv = GIDX["dk"], GIDX["dv"]

        def rope(X, tX, src_rows, X16, tX16):
            C.dma("sync", X[:, :], C.hT[src_rows:src_rows + 64, :], w=[tX])
            C.dma("sync", SW[0:32, :], C.hT[src_rows + 32:src_rows + 64, :], w=[tSW])
            C.dma("sync", SW[32:64, :], C.hT[src_rows:src_rows + 32, :], w=[tSW])
            C.tt(X[:, :], X[:, :], cs[:, 0, :], ALU.mult, r=[tX, tcs], w=[tX])
            C.tt(SW[:, :], SW[:, :], cs[:, 1, :], ALU.mult, r=[tSW, tcs], w=[tSW])
            C.tt(X[:, :], X[:, :], SW[:, :], ALU.add, r=[tX, tSW], w=[tX])
            C.cp(X16[:, :], X[:, :], r=[tX], w=[tX16], eng="scalar")

        for kh in range(2):
            rope(KF, tKF, gk * 128 + kh * 64, KT, tKT)
            C.dma("sync", VF[:, :], C.hT[gv * 128 + kh * 64:gv * 128 + kh * 64 + 64, :], w=[tVF])
            C.dma("gpsimd", VT[:, :], C.hT[gv * 128 + kh * 64:gv * 128 + kh * 64 + 64, :], w=[tVT])
            for n in range(ND):
                C.dma("gpsimd", kraw[:, n, :], cswa[l, n, :, 0, kh, :], w=[tkraw])
                C.dma("gpsimd", vpd[:, n, 0:64], cswa[l, n, :, 1, kh, :], w=[tprevd])
            for n in range(ND):
                pt, tpt = vrot.get()
                C.tr(pt[0:64, 0:128], kraw[:, n, :], ident[:, :], r=[tkraw, tid], w=[tpt])
                C.cp(kpd[:, n, :], pt[0:64, 0:128], r=[tpt], w=[tprevd])
            for which, XT, tX in ((0, KF, tKF), (1, VF, tVF)):
                nat_rows_out(C, XT, tX, 64, identF, tidF, orot, strot, T - 128, 1,
                             lambda i, which=which, kh=kh: o_cswa[l, 0, :, which, kh, :])
                for n in range(ND):
                    b0 = dec_base(n)
                    C.dma("sync", o_cswa[l, 1 + n, 0:127, which, kh, :], cswa[l, n, 1:128, which, kh, :])
                    C.S.op("sync", lambda e, XT=XT, b0=b0, n=n, which=which, kh=kh: e.dma_start(
                        out=o_cswa[l, 1 + n, 127, which, kh, :].rearrange("(p o) -> p o", o=1), in_=XT[0:64, b0:b0 + 1],
                        allow_slow_non_contiguous=True), r=[tX], dma=True)
            for hh in range(8):
                h = kh * 8 + hh
                gq = GIDX[f"dq{h // 2}"]
                QT, tQT = QTs[h % 2]
                rope(XF, tXF, gq * 128 + (h % 2) * 64, QT, tQT)
                attn_core(C, 64, KT, tKT, VT, tVT, QT, tQT, msk[:, 0, :], msk[:, 1, :], tmsk, 1, scale, ident, tid,
                          None, None, oacc, toacc, None, None, True, kpd, vpd, tprevd, pools, True, textra=[tvtones])
                C.ts(oacc[64:65, :], oacc[64:65, :], snk[64:65, h:h + 1], ALU.add, r=[toacc, tsnk], w=[toacc])
                C.S.op("vector", lambda e: e.reciprocal(out=oacc[64:65, :], in_=oacc[64:65, :]), r=[toacc], w=[toacc])
                gg = GIDX[f"dg{h // 2}"]
                C.dma("sync", GF[:, :], C.hT[gg * 128 + (h % 2) * 64:gg * 128 + (h % 2) * 64 + 64, :], w=[tGF])
                C.act(GF[:, :], GF[:, :], AF.Silu, r=[tGF], w=[tGF])
                for t in range(NST):
                    pb_, tpb_ = srot.get()
                    sl = slice(t * 512, (t + 1) * 512)
                    C.mm(pb_[0:64, :], onesF[64:65, 0:64], oacc[64:65, sl], r=[tonesF, toacc], w=[tpb_])
                    C.tt(oacc[0:64, sl], oacc[0:64, sl], pb_[0:64, :], ALU.mult, r=[toacc, tpb_], w=[toacc])
                    C.tt(ys[:, sl], oacc[0:64, sl], GF[:, sl], ALU.mult, r=[toacc, tGF], w=[tys])
                C.dma("gpsimd", C.ysT[3072 + h * 64:3072 + (h + 1) * 64, :], ys[:, :], r=[tys])
        C.S.emit()
```

```python
import math
from contextlib import ExitStack
import numpy as np
import concourse.bass as bass
import concourse.mybir as mybir
from concourse.bass_utils import run_bass_kernel_spmd

F32 = mybir.dt.float32
BF16 = mybir.dt.bfloat16
AF = mybir.ActivationFunctionType
ALU = mybir.AluOpType

D = 2048
T = 4096
ND = 4
TT = T + 128 * ND
NST = TT // 512
NCH = TT // 128
DEPTH = 2
PAST = 16384
ENGS = ("tensor", "vector", "scalar", "gpsimd", "sync")
DEBUG = False
PHASES = ('P', 'inject', 'dense', 'bias', 'lru', 'dil', 'swa', 'gdn')
DILOPT = set()
LAYERS = DEPTH

GROUPS = []
GIDX = {}


def _add(name, c0, n=128):
    GIDX[name] = len(GROUPS)
    GROUPS.append((name, c0, n))


for h in range(8):
    _add(f"aq{h}", 0 + h * 128)
for h in range(8):
    _add(f"ak{h}", 1024 + h * 128)
for h in range(8):
    _add(f"av{h}", 2048 + h * 128)
for h in range(8):
    _add(f"az{h}", 3072 + h * 128)
_add("aba", 4096, 16)
for gi in range(3):
    for qh in range(8):
        _add(f"bq{gi}_{qh}", 4112 + (gi * 8 + qh) * 128)
for gi in range(3):
    for kh in range(2):
        _add(f"bk{gi}_{kh}", 7184 + (gi * 2 + kh) * 128)
for gi in range(3):
    for kh in range(2):
        _add(f"bv{gi}_{kh}", 7952 + (gi * 2 + kh) * 128)
for qh in range(8):
    _add(f"bg{qh}", 8720 + qh * 128)
for b in range(8):
    _add(f"cx{b}", 9744 + b * 128)
for b in range(8):
    _add(f"cg{b}", 10768 + b * 128)
for p in range(8):
    _add(f"dq{p}", 11792 + p * 128)
_add("dk", 12816)
_add("dv", 12944)
for p in range(8):
    _add(f"dg{p}", 13072 + p * 128)
NGM = len(GROUPS)
for b in range(4):
    for j in range(16):
        _add(f"mg{b}_{j}", 14096 + b * 2048 + j * 128)
NG = len(GROUPS)


class Trk:
    __slots__ = ("lw", "rd")
    excl = False

    def __init__(self):
        self.lw = None
        self.rd = []


class PTrk(Trk):
    __slots__ = ()
    excl = True


class Op:
    __slots__ = ("eng", "fn", "deps", "dma", "sig", "sem", "val", "prev")

    def __init__(self, eng, fn, dma):
        self.eng, self.fn, self.dma = eng, fn, dma
        self.deps = []
        self.sig = False
        self.sem = None
        self.val = 0
        self.prev = None


class Sched:
    def __init__(self, nc, gstack, n_dma=40, cap=8000):
        self.nc, self.gstack, self.cap = nc, gstack, cap
        self.esem = {e: [] for e in ENGS}
        self.ecnt = {e: cap for e in ENGS}
        self.dsem = [gstack.enter_context(nc.semaphore(f"sd{i}")) for i in range(n_dma)]
        self.dcnt = [0] * n_dma
        self.dlast = [None] * n_dma
        self.rr = 0
        self.reset()

    def reset(self):
        self.ops = {e: [] for e in ENGS}
        self.all = []

    def op(self, eng, fn, r=(), w=(), dma=False):
        o = Op(eng, fn, dma)
        w = list(w) + [t for t in r if t.excl]
        r = [t for t in r if not t.excl]
        deps = set()
        for t in r:
            if t.lw is not None:
                deps.add(t.lw)
        for t in w:
            if t.lw is not None:
                deps.add(t.lw)
            deps.update(t.rd)
        for t in r:
            t.rd.append(o)
        for t in w:
            t.lw = o
            t.rd = []
        deps.discard(o)
        o.deps = list(deps)
        self.ops[eng].append(o)
        self.all.append(o)
        return o

    def emit(self):
        nc = self.nc
        for o in self.all:
            if o.dma:
                o.sig = True
            for d in o.deps:
                if d.dma or not (d.eng == "tensor" and o.eng == "tensor" and not o.dma):
                    d.sig = True
        for o in self.all:
            if not o.sig:
                continue
            if o.dma:
                i = self.rr % len(self.dsem)
                self.rr += 1
                self.dcnt[i] += 16
                o.sem, o.val, o.prev = self.dsem[i], self.dcnt[i], self.dlast[i]
                self.dlast[i] = o
            else:
                e = o.eng
                if self.ecnt[e] >= self.cap:
                    self.esem[e].append(self.gstack.enter_context(nc.semaphore(f"s{e}{len(self.esem[e])}")))
                    self.ecnt[e] = 0
                self.ecnt[e] += 1
                o.sem, o.val = self.esem[e][-1], self.ecnt[e]
        finals = [x for x in self.dlast if x is not None]
        phase_ops = set(map(id, self.all))

        def make(e):
            def body(h):
                waited = {}
                for o in self.ops[e]:
                    need = {}
                    deps = list(o.deps)
                    if o.dma and o.prev is not None and id(o.prev) in phase_ops:
                        deps.append(o.prev)
                    for d in deps:
                        if (not d.dma) and d.eng == "tensor" and e == "tensor" and not o.dma:
                            continue
                        k = id(d.sem)
                        if waited.get(k, 0) >= d.val:
                            continue
                        if k not in need or need[k][1] < d.val:
                            need[k] = (d.sem, d.val)
                    for k, (s, v) in need.items():
                        h.wait_ge(s, v)
                        waited[k] = v
                    ins = o.fn(h)
                    if o.sig:
                        ins.then_inc(o.sem, 16 if o.dma else 1)
                if e == "sync":
                    for d in finals:
                        if waited.get(id(d.sem), 0) < d.val:
                            h.wait_ge(d.sem, d.val)
            return body

        with nc.Block() as block:
            block.tensor(make("tensor"))
            block.vector(make("vector"))
            block.scalar(make("scalar"))
            block.gpsimd(make("gpsimd"))
            block.sync(make("sync"))
        self.reset()


class Ctx:
    def __init__(self):
        self.nc = bass.Bass("TRN2", target_bir_lowering=False)
        self.g = ExitStack()
        self.S = Sched(self.nc, self.g)
        self.ins = {}
        self.outs = {}
        self.ph = None
        self.n = 0

    def inp(self, name, shape, dt=F32):
        t = self.nc.dram_tensor(name, list(shape), dt, kind="ExternalInput")
        self.ins[name] = t
        return t

    def outp(self, name, shape, dt=F32):
        t = self.nc.dram_tensor(name, list(shape), dt, kind="ExternalOutput")
        self.outs[name] = t
        return t

    def scratch(self, name, shape, dt=F32):
        if DEBUG:
            return self.outp(name, shape, dt)
        return self.nc.dram_tensor(name, list(shape), dt)

    def sb(self, shape, dt=F32):
        self.n += 1
        return self.ph.enter_context(self.nc.sbuf_tensor(f"sb{self.n}", list(shape), dt))

    def ps(self, shape, dt=F32):
        self.n += 1
        return self.ph.enter_context(self.nc.psum_tensor(f"ps{self.n}", list(shape), dt))

    def dma(self, eng, out, in_, r=(), w=()):
        return self.S.op(eng, lambda e: e.dma_start(out=out, in_=in_), r=r, w=w, dma=True)

    def mm(self, out, lhsT, rhs, start=True, stop=True, r=(), w=()):
        return self.S.op("tensor", lambda e: e.matmul(out, lhsT=lhsT, rhs=rhs, start=start, stop=stop), r=r, w=w)

    def tr(self, out, in_, ident, r=(), w=()):
        return self.S.op("tensor", lambda e: e.transpose(out, in_, ident), r=r, w=w)

    def act(self, out, in_, func, r=(), w=(), bias=None, scale=None, eng="scalar"):
        kw = {}
        if bias is not None:
            kw["bias"] = bias
        if scale is not None:
            kw["scale"] = scale
        return self.S.op("scalar", lambda e: e.activation(out=out, in_=in_, func=func, **kw), r=r, w=w)

    def tt(self, out, in0, in1, op, r=(), w=(), eng="vector"):
        return self.S.op(eng, lambda e: e.tensor_tensor(out=out, in0=in0, in1=in1, op=op), r=r, w=w)

    def ts(self, out, in0, s1, op0, s2=None, op1=None, r=(), w=(), eng="vector"):
        if op1 is None:
            return self.S.op(eng, lambda e: e.tensor_scalar(out=out, in0=in0, scalar1=s1, scalar2=None, op0=op0), r=r, w=w)
        return self.S.op(eng, lambda e: e.tensor_scalar(out=out, in0=in0, scalar1=s1, scalar2=s2, op0=op0, op1=op1), r=r, w=w)

    def stt(self, out, in0, scalar, in1, op0, op1, r=(), w=()):
        return self.S.op("vector", lambda e: e.scalar_tensor_tensor(out=out, in0=in0, scalar=scalar, in1=in1, op0=op0, op1=op1), r=r, w=w)

    def cp(self, out, in_, r=(), w=(), eng="vector"):
        if eng == "scalar":
            return self.S.op("scalar", lambda e: e.activation(out=out, in_=in_, func=AF.Copy), r=r, w=w)
        return self.S.op(eng, lambda e: e.tensor_copy(out=out, in_=in_), r=r, w=w)

    def memset(self, ap, val, w=(), eng="vector"):
        return self.S.op(eng, lambda e: e.memset(ap, val), w=w)


class Rot:
    def __init__(self, items):
        self.items = items
        self.i = 0

    def get(self):
        x = self.items[self.i % len(self.items)]
        self.i += 1
        return x


def dec_base(n):
    return T + 128 * n


def build_program():
    C = Ctx()
    nc = C.nc
    small = 'P' not in PHASES
    sm = lambda shp: [1, 1] if small else shp
    xT0 = C.inp("xT0", sm([D, TT]))
    x0 = C.inp("x0", sm([TT, D]))
    wg = [C.inp(f"wg{l}", sm([NG, 128, 2048])) for l in range(DEPTH)]
    wb = [C.inp(f"wb{l}", sm([4, 16, 128, 1024])) for l in range(DEPTH)]
    wo = [C.inp(f"wo{l}", sm([128, 16 * D])) for l in range(DEPTH)]
    lng = C.inp("lng", sm([DEPTH, 128, D]))
    lnb = C.inp("lnb", sm([DEPTH, 128, D]))
    ident_d = C.inp("ident", [128, 128])
    lru_par = C.inp("lru_par", [DEPTH, 128, 8 * 8])
    lru_w = C.inp("lru_w", [DEPTH, 2, 128, 8 * 128])
    st_lru = C.inp("st_lru", [DEPTH, ND, 1024])
    st_lruc = C.inp("st_lruc", [DEPTH, ND, 3, 1024])
    gmask_d = C.inp("gmask", [3, 128, 128])
    tmask_d = C.inp("tmask", [2, 8, TT])
    gcw_d = C.inp("gcw", [DEPTH, 128, 96])
    gal_d = C.inp("gal", [DEPTH, 8, 2])
    gnw_d = C.inp("gnw", [DEPTH, 128, 1])
    st_gdn = C.inp("st_gdn", [DEPTH, ND, 8, 128, 128])
    st_gdnc = C.inp("st_gdnc", [DEPTH, ND, 3, 3072])
    rel_d = C.inp("rel", [32, 24])
    oh_d = C.inp("oh", [32, 3 * 129])
    sel_d = C.inp("sel8", [8, 8 * 128])
    masks_d = C.inp("masks", [2, 128, 128])
    rope_d = C.inp("rope", [2, 64, TT])
    sink_d = C.inp("sink", [DEPTH, 128, 16])
    caches = [C.inp(f"cd{g}", [DEPTH, ND, WINS[g], 2, 2, 128]) for g in range(3)]
    cswa = C.inp("cswa", [DEPTH, ND, 128, 2, 2, 64])
    y_out = C.outp("y", [TT, D])
    o_lru = C.outp("o_lru", [DEPTH, 1 + ND, 1024])
    o_lruc = C.outp("o_lruc", [DEPTH, 1 + ND, 3, 1024])
    o_gdn = C.outp("o_gdn", [DEPTH, 1 + ND, 8, 128, 128])
    o_gdnc = C.outp("o_gdnc", [DEPTH, 1 + ND, 3, 3072])
    o_caches = [C.outp(f"o_cd{g}", [DEPTH, 1 + ND, WINS[g], 2, 2, 128]) for g in range(3)]
    o_cswa = C.outp("o_cswa", [DEPTH, 1 + ND, 128, 2, 2, 64])
    hT = C.scratch("hT", [NGM * 128, TT])
    G = C.scratch("G", [64 * 128, TT], BF16)
    ysT = C.scratch("ysT", [4096, TT], BF16)
    y0 = C.scratch("y0", [TT, D])
    xT1 = C.scratch("xT1", [D, TT], BF16)
    C.hT, C.G, C.ysT = hT, G, ysT
    C.wb16 = C.scratch("wb16", [4, 16, 128, 1024], BF16)
    bvd = C.scratch("bvd", [8, 128, 1530])
    btile = C.scratch("btile", [3, 8, 2, 128, 128])
    if "bias" in PHASES:
        phase_bias_setup(C, rel_d, oh_d, sel_d, bvd, btile)

    for l in range(LAYERS):
        if 'P' in PHASES:
            phase_P(C, l, xT0 if l == 0 else xT1, wg[l], cast=(l == 0))
        if 'inject' in PHASES:
            phase_state_inject(C, l, st_lruc, st_gdnc)
        if "gdn" in PHASES:
            phase_gdn(C, l, ident_d, sel_d, gmask_d, tmask_d, gcw_d, gal_d, gnw_d, st_gdn, o_gdn, o_gdnc)
        if "lru" in PHASES:
            phase_LRU(C, l, lru_par, lru_w, st_lru, o_lru, o_lruc)
        if "dil" in PHASES:
            phase_dil(C, l, btile, ident_d, caches, o_caches)
        if "swa" in PHASES:
            phase_swa(C, l, masks_d, ident_d, rope_d, sink_d, cswa, o_cswa)
        if 'dense' in PHASES:
            phase_mirror_ys(C)
            phase_dense(C, l, wb[l], wo[l], lng, lnb, ident_d, x0 if l == 0 else y0,
                        y0 if l == 0 else y_out, xT1 if l == 0 else None)
    C.g.close()
    return C


def phase_P(C, l, xsrc, wgl, cast):
    C.ph = ExitStack()
    with C.ph:
        xT = C.sb([128, 16, TT], BF16)
        txk = [Trk() for _ in range(16)]
        src = xsrc.ap().rearrange("(kc p) t -> p kc t", p=128)
        for kc in range(16):
            C.dma("gpsimd" if cast else "sync", xT[:, kc, :], src[:, kc, :], w=[txk[kc]])
        wrot = Rot([(C.sb([128, 16, 128], BF16), Trk()) for _ in range(3)])
        srot = Rot([(C.sb([128, 1536], F32), Trk()) for _ in range(2)])
        grot = Rot([(C.sb([128, 1536], BF16), Trk()) for _ in range(2)])
        prot = Rot([(C.ps([128, 512], F32), PTrk()) for _ in range(6)])
        k = 0
        for g in range(NG):
            wt, twt = wrot.get()
            C.dma("gpsimd", wt[:, :, :], wgl[g].rearrange("p (kc c) -> p kc c", c=128), w=[twt])
            gate = g >= NGM
            for t3 in range(3):
                stg, tst = (grot if gate else srot).get()
                for tl in range(3):
                    t = t3 * 3 + tl
                    ps, tps = prot.get()
                    for kc in range(16):
                        C.mm(ps[:, :], wt[:, kc, :], xT[:, kc, t * 512:(t + 1) * 512], start=(kc == 0), stop=(kc == 15),
                             r=[twt, txk[kc]], w=[tps])
                    o = stg[:, tl * 512:(tl + 1) * 512]
                    if gate:
                        C.act(o, ps[:, :], AF.Sigmoid, r=[tps], w=[tst])
                    else:
                        k += 1
                        C.cp(o, ps[:, :], r=[tps], w=[tst], eng=("vector" if k % 2 else "scalar"))
                if gate:
                    dst = C.G[(g - NGM) * 128:(g - NGM + 1) * 128, t3 * 1536:(t3 + 1) * 1536]
                else:
                    dst = C.hT[g * 128:(g + 1) * 128, t3 * 1536:(t3 + 1) * 1536]
                C.dma("sync", dst, stg[:, :], r=[tst])
        C.S.emit()


def phase_state_inject(C, l, st_lruc, st_gdnc):
    C.ph = ExitStack()
    with C.ph:
        chain = [Trk() for _ in range(4)]
        ic = [0]

        def nxt():
            ic[0] += 1
            return [chain[ic[0] % 4]]
        for n in range(ND):
            b0 = dec_base(n)
            for blk in range(8):
                g = GIDX[f"cx{blk}"]
                src = st_lruc[l, n, :, blk * 128:(blk + 1) * 128].rearrange("j p -> p j")
                C.S.op("sync", lambda e, g=g, b0=b0, src=src: e.dma_start(
                    out=C.hT[g * 128:(g + 1) * 128, b0:b0 + 3], in_=src, allow_slow_non_contiguous=True), w=nxt(), dma=True)
            for i3, nm in enumerate(("aq", "ak", "av")):
                for h in range(8):
                    g = GIDX[f"{nm}{h}"]
                    cc0 = i3 * 1024 + h * 128
                    src = st_gdnc[l, n, :, cc0:cc0 + 128].rearrange("j p -> p j")
                    C.S.op("sync", lambda e, g=g, b0=b0, src=src: e.dma_start(
                        out=C.hT[g * 128:(g + 1) * 128, b0:b0 + 3], in_=src, allow_slow_non_contiguous=True), w=nxt(), dma=True)
        C.S.emit()


def phase_LRU(C, l, lru_par, lru_w, st_lru, o_lru, o_lruc):
    PC = 1536
    C.ph = ExitStack()
    with C.ph:
        par = C.sb([128, 64])
        tpar = Trk()
        C.dma("sync", par[:, :], lru_par[l], w=[tpar])
        wax = C.sb([128, 2, 1024])
        twax = Trk()
        C.dma("sync", wax[:, 0, :], lru_w[l, 0], w=[twax])
        C.dma("sync", wax[:, 1, :], lru_w[l, 1], w=[twax])
        h0 = C.sb([128, 8, ND])
        th0 = Trk()
        for blk in range(8):
            C.S.op("sync", lambda e, blk=blk: e.dma_start(out=h0[:, blk, :], in_=st_lru[l][:, blk * 128:(blk + 1) * 128].rearrange("n p -> p n"),
                                                          allow_slow_non_contiguous=True), w=[th0], dma=True)
        ccol = C.sb([128, 8])
        tcc = Trk()
        for blk in range(8):
            C.act(ccol[:, blk:blk + 1], par[:, blk * 8 + 7:blk * 8 + 8], AF.Exp, scale=-1.0, r=[tpar], w=[tcc])
        C.act(ccol[:, :], ccol[:, :], AF.Ln, bias=1.0, r=[tcc], w=[tcc])
        C.ts(ccol[:, :], ccol[:, :], -8.0, ALU.mult, r=[tcc], w=[tcc])
        xprot = Rot([(C.sb([128, 3 + PC]), Trk()) for _ in range(2)])
        cgrot = Rot([(C.sb([128, PC]), Trk()) for _ in range(2)])
        hhrot = Rot([(C.sb([128, PC]), Trk()) for _ in range(2)])
        ysrot = Rot([(C.sb([128, PC], BF16), Trk()) for _ in range(2)])
        cx = C.sb([128, PC]); tcx = Trk()
        rr = C.sb([128, PC]); trr = Trk()
        ii = C.sb([128, PC]); tii = Trk()
        aa = C.sb([128, PC]); taa = Trk()
        bb = C.sb([128, PC]); tbb = Trk()
        prot = Rot([(C.ps([128, 512], F32), PTrk()) for _ in range(4)])
        for blk in range(8):
            g = GIDX[f"cx{blk}"]
            gg = GIDX[f"cg{blk}"]
            pw = lambda j, blk=blk: par[:, blk * 8 + j:blk * 8 + j + 1]
            hprev = None
            for pc in range(3):
                c0 = pc * PC
                xp, txp = xprot.get()
                if pc == 0:
                    C.memset(xp[:, 0:3], 0.0, w=[txp])
                    C.dma("sync", xp[:, 3:3 + PC], C.hT[g * 128:(g + 1) * 128, 0:PC], w=[txp])
                else:
                    C.dma("sync", xp[:, :], C.hT[g * 128:(g + 1) * 128, c0 - 3:c0 + PC], w=[txp])
                cg, tcg = cgrot.get()
                C.dma("sync", cg[:, :], C.hT[gg * 128:(gg + 1) * 128, c0:c0 + PC], w=[tcg])
                C.act(cx[:, :], xp[:, 0:PC], AF.Identity, scale=pw(0), bias=pw(4), r=[txp, tpar], w=[tcx])
                for j in range(1, 4):
                    C.stt(cx[:, :], xp[:, j:j + PC], pw(j), cx[:, :], ALU.mult, ALU.add, r=[txp, tpar, tcx], w=[tcx])
                for t in range(3):
                    for which, dst, tdst, bj in ((0, rr, trr, 5), (1, ii, tii, 6)):
                        ps, tps = prot.get()
                        C.mm(ps[:, :], wax[:, which, blk * 128:(blk + 1) * 128], cx[:, t * 512:(t + 1) * 512], r=[twax, tcx], w=[tps])
                        C.act(dst[:, t * 512:(t + 1) * 512], ps[:, :], AF.Sigmoid, bias=pw(bj), r=[tps, tpar], w=[tdst])
                C.act(aa[:, :], rr[:, :], AF.Exp, scale=ccol[:, blk:blk + 1], r=[trr, tcc], w=[taa])
                C.tt(bb[:, :], aa[:, :], aa[:, :], ALU.mult, r=[taa], w=[tbb])
                C.ts(bb[:, :], bb[:, :], -1.0, ALU.mult, 1.0, ALU.add, r=[tbb], w=[tbb])
                C.act(bb[:, :], bb[:, :], AF.Sqrt, r=[tbb], w=[tbb])
                C.tt(bb[:, :], bb[:, :], ii[:, :], ALU.mult, r=[tbb, tii], w=[tbb])
                C.tt(bb[:, :], bb[:, :], cx[:, :], ALU.mult, r=[tbb, tcx], w=[tbb])
                for n in range(ND):
                    col = dec_base(n) + 2 - c0
                    if 0 <= col < PC:
                        C.memset(aa[:, col:col + 1], 0.0, w=[taa])
                        C.cp(bb[:, col:col + 1], h0[:, blk, n:n + 1], r=[th0], w=[tbb])
                hh, thh = hhrot.get()
                init = 0.0 if hprev is None else hprev[0][:, PC - 1:PC]
                rdeps = [taa, tbb] + ([] if hprev is None else [hprev[1]])
                C.S.op("vector", lambda e, hh=hh, init=init: e.tensor_tensor_scan(out=hh[:, :], data0=aa[:, :], data1=bb[:, :],
                                                                               initial=init, op0=ALU.mult, op1=ALU.add), r=rdeps, w=[thh])
                hprev = (hh, thh)
                C.act(cg[:, :], cg[:, :], AF.Silu, r=[tcg], w=[tcg])
                ys, tys = ysrot.get()
                C.tt(ys[:, :], hh[:, :], cg[:, :], ALU.mult, r=[thh, tcg], w=[tys])
                C.dma("sync", C.ysT[2048 + blk * 128:2048 + (blk + 1) * 128, c0:c0 + PC], ys[:, :], r=[tys])
                def col_out(dst_ap, src_ap, trk):
                    C.S.op("sync", lambda e: e.dma_start(out=dst_ap, in_=src_ap, allow_slow_non_contiguous=True), r=[trk], dma=True)
                segs = [(0, T - 1)] + [(1 + n, dec_base(n) + 3) for n in range(ND)]
                for si, last in segs:
                    if c0 <= last < c0 + PC:
                        lc = last - c0
                        col_out(o_lru[l, si, blk * 128:(blk + 1) * 128].rearrange("(p o) -> p o", o=1), hh[:, lc:lc + 1], thh)
                        col_out(o_lruc[l, si, :, blk * 128:(blk + 1) * 128].rearrange("j p -> p j"), xp[:, lc + 1:lc + 4], txp)
        C.S.emit()


def phase_dense(C, l, wbl, wol, lng, lnb, ident_d, xres, ydst, xTdst):
    DN_ALPHA = (2.0 * DEPTH) ** 0.25
    C.ph = ExitStack()
    with C.ph:
        strot = Rot([(C.sb([128, 8, 1024], BF16), Trk()) for _ in range(2)])
        for b in range(4):
            for jh in range(2):
                stg, tstg = strot.get()
                C.dma("gpsimd", stg[:, :, :], wbl[b, jh * 8:(jh + 1) * 8].rearrange("j p c -> p j c"), w=[tstg])
                C.dma("sync", C.wb16[b, jh * 8:(jh + 1) * 8].rearrange("j p c -> p j c"), stg[:, :, :], r=[tstg])
        C.S.emit()
    C.ph = ExitStack()
    with C.ph:
        ident = C.sb([128, 128]); tid = Trk()
        C.dma("sync", ident[:, :], ident_d[:, :], w=[tid])
        wot = C.sb([128, 16, D], BF16); two = Trk()
        wov = wol.ap().rearrange("p (kc c) -> p kc c", c=D)
        for kc in range(16):
            C.dma("gpsimd", wot[:, kc, :], wov[:, kc, :], w=[two])
        gam = C.sb([128, D]); bet = C.sb([128, D]); tgb = Trk()
        C.dma("sync", gam[:, :], lng[l], w=[tgb])
        C.dma("sync", bet[:, :], lnb[l], w=[tgb])
        ysr = Rot([(C.sb([128, 32, 512], BF16), Trk()) for _ in range(1)])
        mg = C.sb([128, 16, 512], BF16); tmg = [Trk() for _ in range(16)]
        wbr = Rot([(C.sb([128, 8, 128], BF16), Trk()) for _ in range(4)])
        gtr = Rot([(C.sb([128, 512], BF16), Trk()) for _ in range(4)])
        acc = C.sb([128, 512]); tacc = Trk()
        tmp = C.sb([128, 512]); ttmp = Trk()
        xr = Rot([(C.sb([128, D]), Trk()) for _ in range(2)])
        zr = Rot([(C.sb([128, D]), Trk()) for _ in range(2)])
        st = C.sb([128, 4, 6]); tst = Trk()
        mv = C.sb([128, 2]); tmv = Trk()
        rstd = C.sb([128, 1]); trs = Trk()
        xts = Rot([(C.sb([128, 16, 128], BF16), Trk()) for _ in range(2)])
        prot = Rot([(C.ps([128, 512], F32), PTrk()) for _ in range(5)])
        ptr = Rot([(C.ps([128, 512], F32), PTrk()) for _ in range(2)])
        ysv = C.ysT.ap().rearrange("(kc p) t -> p kc t", p=128)
        for t in range(NST):
            ys, tys = ysr.get()
            for q in range(4):
                C.dma("sync", ys[:, q * 8:(q + 1) * 8, :], ysv[:, q * 8:(q + 1) * 8, t * 512:(t + 1) * 512], w=[tys])
            for j in range(16):
                for b in range(4):
                    wt, twt = wbr.get()
                    C.dma("sync", wt[:, :, :], C.wb16[b, j].rearrange("p (kc c) -> p kc c", c=128), w=[twt])
                    gt, tgt = gtr.get()
                    C.dma("sync", gt[:, :], C.G[(b * 16 + j) * 128:(b * 16 + j + 1) * 128, t * 512:(t + 1) * 512], w=[tgt])
                    ps, tps = prot.get()
                    for kc in range(8):
                        C.mm(ps[:, :], wt[:, kc, :], ys[:, b * 8 + kc, :], start=(kc == 0), stop=(kc == 7), r=[twt, tys], w=[tps])
                    if b == 0:
                        C.tt(acc[:, :], ps[:, :], gt[:, :], ALU.mult, r=[tps, tgt], w=[tacc])
                    else:
                        C.tt(tmp[:, :], ps[:, :], gt[:, :], ALU.mult, r=[tps, tgt], w=[ttmp])
                        if b < 3:
                            C.tt(acc[:, :], acc[:, :], tmp[:, :], ALU.add, r=[tacc, ttmp], w=[tacc], eng="gpsimd")
                        else:
                            C.tt(mg[:, j, :], acc[:, :], tmp[:, :], ALU.add, r=[tacc, ttmp], w=[tmg[j]], eng="gpsimd")
            for sub in range(4):
                r0 = t * 512 + sub * 128
                x, tx = xr.get()
                C.dma("sync", x[:, :], xres[r0:r0 + 128, :], w=[tx])
                z, tz = zr.get()
                for jc in range(4):
                    ps, tps = prot.get()
                    for kc in range(16):
                        C.mm(ps[:, :], mg[:, kc, sub * 128:(sub + 1) * 128], wot[:, kc, jc * 512:(jc + 1) * 512],
                             start=(kc == 0), stop=(kc == 15), r=[tmg[kc], two], w=[tps])
                    C.stt(z[:, jc * 512:(jc + 1) * 512], x[:, jc * 512:(jc + 1) * 512], DN_ALPHA, ps[:, :], ALU.mult, ALU.add,
                          r=[tx, tps], w=[tz])
                    C.S.op("vector", lambda e, z=z, jc=jc: e.bn_stats(out=st[:, jc, :], in_=z[:, jc * 512:(jc + 1) * 512]), r=[tz], w=[tst])
                C.S.op("vector", lambda e: e.bn_aggr(out=mv[:, :], in_=st[:, :, :].rearrange("p a b -> p (a b)")), r=[tst], w=[tmv])
                C.ts(rstd[:, :], mv[:, 1:2], 1e-5, ALU.add, r=[tmv], w=[trs])
                C.act(rstd[:, :], rstd[:, :], AF.Ln, r=[trs], w=[trs])
                C.act(rstd[:, :], rstd[:, :], AF.Exp, scale=-0.5, r=[trs], w=[trs])
                C.ts(z[:, :], z[:, :], mv[:, 0:1], ALU.subtract, rstd[:, 0:1], ALU.mult, r=[tz, tmv, trs], w=[tz])
                C.tt(z[:, :], z[:, :], gam[:, :], ALU.mult, r=[tz, tgb], w=[tz], eng="gpsimd")
                C.tt(z[:, :], z[:, :], bet[:, :], ALU.add, r=[tz, tgb], w=[tz], eng="gpsimd")
                C.dma("sync", ydst[r0:r0 + 128, :], z[:, :], r=[tz])
                if xTdst is not None:
                    xt, txt = xts.get()
                    for q in range(4):
                        pt, tpt = ptr.get()
                        for kk in range(4):
                            kc = q * 4 + kk
                            C.tr(pt[:, kk * 128:(kk + 1) * 128], z[:, kc * 128:(kc + 1) * 128], ident[:, :], r=[tz, tid], w=[tpt])
                        C.cp(xt[:, q * 4:(q + 1) * 4, :], pt[:, :].rearrange("p (a b) -> p a b", b=128), r=[tpt], w=[txt], eng="scalar")
                    C.dma("sync", xTdst.ap().rearrange("(kc p) t -> p kc t", p=128)[:, :, r0:r0 + 128], xt[:, :, :], r=[txt])
        C.S.emit()
    if xTdst is not None:
        C.ph = ExitStack()
        with C.ph:
            chain = [Trk() for _ in range(4)]
            for n in range(ND):
                b0 = dec_base(n)
                for r0 in range(0, D, 128):
                    C.S.op("sync", lambda e, b0=b0, r0=r0: e.dma_start(out=xTdst[r0:r0 + 128, b0:b0 + 1], in_=xTdst[r0:r0 + 128, b0 + 3:b0 + 4],
                                                                       allow_slow_non_contiguous=True), w=[chain[(r0 // 128) % 4]], dma=True)
                C.dma("sync", ydst[b0:b0 + 1, :], ydst[b0 + 3:b0 + 4, :])
            C.S.emit()


def _group_cols():
    cols = np.zeros((NG, 128), np.int64)
    valid = np.zeros((NG, 128), bool)
    for g, (name, c0, n) in enumerate(GROUPS):
        cols[g, :n] = c0 + np.arange(n)
        valid[g, :n] = True
    return cols, valid


def host_prep(inputs):
    f = lambda k: np.asarray(inputs[k], dtype=np.float32)
    xp, xs = f("x_prompt"), f("x_sample")
    w_in, w_br, w_out = f("w_in"), f("w_branch"), f("w_out")
    cols, valid = _group_cols()
    shared = {}
    for l in range(DEPTH):
        W = w_in[l][:, cols.reshape(-1)].reshape(16, 128, NG, 128)
        W = W * valid[None, None, :, :].astype(np.float32)
        shared[f"wg{l}"] = np.ascontiguousarray(W.transpose(2, 1, 0, 3)).reshape(NG, 128, 2048)
        Wb = w_br[l].reshape(4, 8, 128, 16, 128)
        shared[f"wb{l}"] = np.ascontiguousarray(Wb.transpose(0, 3, 2, 1, 4)).reshape(4, 16, 128, 1024)
        Wo = w_out[l].reshape(16, 128, D)
        shared[f"wo{l}"] = np.ascontiguousarray(Wo.transpose(1, 0, 2)).reshape(128, 16 * D)
    shared["lng"] = np.ascontiguousarray(np.broadcast_to(f("ln_g")[:, None, :], (DEPTH, 128, D)))
    shared["lnb"] = np.ascontiguousarray(np.broadcast_to(f("ln_b")[:, None, :], (DEPTH, 128, D)))
    shared["ident"] = np.eye(128, dtype=np.float32)
    lp = np.zeros((DEPTH, 128, 8, 8), np.float32)
    lcw = f("lru_conv_w").reshape(DEPTH, 4, 8, 128)
    for j in range(4):
        lp[:, :, :, j] = lcw[:, j].transpose(0, 2, 1)
    for j, k in ((4, "lru_conv_b"), (5, "lru_ba"), (6, "lru_bx"), (7, "lru_lambda")):
        lp[:, :, :, j] = f(k).reshape(DEPTH, 8, 128).transpose(0, 2, 1)
    shared["lru_par"] = lp.reshape(DEPTH, 128, 64)
    lw = np.stack([f("lru_wa"), f("lru_wx")], axis=1)
    shared["lru_w"] = np.ascontiguousarray(lw.transpose(0, 1, 3, 2, 4)).reshape(DEPTH, 2, 128, 1024)
    pp_ = np.arange(128)[:, None]
    ff_ = np.arange(128)[None, :]
    shared["gmask"] = np.stack([np.where(ff_ > pp_, 0.0, -1.0e4), np.where(ff_ >= pp_, 0.0, -1.0e4),
                                np.where(pp_ > ff_, 0.0, 1.0e4)]).astype(np.float32)
    tm = np.zeros((2, 8, TT), np.float32)
    tm[0, :, :T] = 1.0
    for n in range(ND):
        tm[0, :, dec_base(n) + 3] = 1.0
    tm[1] = 1.0
    tm[1, :, ::128] = 0.0
    shared["tmask"] = tm
    gc_ = f("gdn_conv_w").reshape(DEPTH, 4, 24, 128)
    shared["gcw"] = np.ascontiguousarray(gc_.transpose(0, 3, 2, 1)).reshape(DEPTH, 128, 96)
    shared["gal"] = np.ascontiguousarray(np.stack([f("gdn_a_log"), f("gdn_dt_bias")], axis=-1))
    shared["gnw"] = np.ascontiguousarray(f("gdn_norm_w")[:, :, None])
    shared["rel"] = f("rel_bias")
    oh = np.zeros((32, 3, 129), np.float32)
    for g in range(3):
        dist = (DILS[g] * np.arange(129)).astype(np.int64)
        n = dist.astype(np.float32)
        large = 16 + (np.log(np.maximum(n, np.float32(1.0)) / np.float32(16)) / np.float32(math.log(2048 / 16)) * np.float32(16)).astype(np.int32)
        large = np.minimum(large, 31)
        bucket = np.where(dist < 16, dist, large)
        oh[bucket, g, np.arange(129)] = 1.0
    shared["oh"] = oh.reshape(32, 3 * 129)
    sel8 = np.zeros((8, 8, 128), np.float32)
    for h in range(8):
        sel8[h, h, :] = 1.0
    shared["sel8"] = sel8.reshape(8, 1024)
    jj = np.arange(128)[:, None]
    ii = np.arange(128)[None, :]
    shared["masks"] = np.stack([np.where(jj <= ii, 0.0, NEG), np.where(jj >= ii, 0.0, NEG)]).astype(np.float32)
    pos = np.concatenate([np.arange(T), np.full(TT - T, PAST)]).astype(np.float32)
    inv = (np.float32(150000.0) ** (-np.arange(32, dtype=np.float32) / np.float32(32))).astype(np.float32)
    ang = (pos[None, :] * inv[:, None]).astype(np.float32)
    cosr = np.cos(ang.astype(np.float64)).astype(np.float32)
    sinr = np.sin(ang.astype(np.float64)).astype(np.float32)
    shared["rope"] = np.stack([np.concatenate([cosr, cosr]), np.concatenate([-sinr, sinr])]).astype(np.float32)
    shared["sink"] = np.ascontiguousarray(np.broadcast_to(f("swa_sink")[:, None, :], (DEPTH, 128, 16)))
    per_core = []
    for c in range(8):
        s = c % 2
        m = {}
        x0 = np.zeros((TT, D), np.float32)
        x0[:T] = xp[s]
        for n in range(ND):
            x0[dec_base(n)] = xs[4 * c + n, 0]
            x0[dec_base(n) + 3] = xs[4 * c + n, 0]
        m["x0"] = x0
        m["xT0"] = np.ascontiguousarray(x0.T)
        m["st_lru"] = np.ascontiguousarray(f("state_rglru")[:, 4 * c:4 * c + 4])
        m["st_lruc"] = np.ascontiguousarray(f("state_rglru_conv")[:, 4 * c:4 * c + 4])
        for g, k in enumerate(("cache_dil_w128", "cache_dil_w512", "cache_dil_w2048")):
            m[f"cd{g}"] = np.ascontiguousarray(f(k)[:, 4 * c:4 * c + 4])
        m["cswa"] = np.ascontiguousarray(f("cache_swa")[:, 4 * c:4 * c + 4])
        m["st_gdn"] = np.ascontiguousarray(f("state_gdn")[:, 4 * c:4 * c + 4])
        m["st_gdnc"] = np.ascontiguousarray(f("state_gdn_conv")[:, 4 * c:4 * c + 4])
        m.update(shared)
        per_core.append(m)
    return per_core


_PROG = None


def run_device(inputs):
    global _PROG
    if _PROG is None:
        _PROG = build_program()
    in_maps = host_prep(inputs)
    res = run_bass_kernel_spmd(_PROG.nc, in_maps, core_ids=list(range(8)))
    return res.results


NEG = -30000.0
DILS = (1, 4, 16)
WINS = (128, 512, 2048)


def phase_bias_setup(C, rel_d, oh_d, sel_d, bvd, btile):
    C.ph = ExitStack()
    with C.ph:
        rel = C.sb([32, 24]); trel = Trk()
        C.dma("sync", rel[:, :], rel_d[:, :], w=[trel])
        oh = C.sb([32, 3 * 129]); toh = Trk()
        C.dma("sync", oh[:, :], oh_d[:, :], w=[toh])
        pall = C.sb([8, 3 * 2 * 255]); tp = Trk()
        C.memset(pall[:, :], NEG, w=[tp])
        ps = C.ps([8, 512]); tps = PTrk()
        for g in range(3):
            C.mm(ps[:, 0:129], rel[:, g * 8:(g + 1) * 8], oh[:, g * 129:(g + 1) * 129], r=[trel, toh], w=[tps])
            o0 = (g * 2 + 0) * 255
            o1 = (g * 2 + 1) * 255
            C.ts(pall[:, o0 + 127:o0 + 255], ps[:, 0:128], math.sqrt(128.0), ALU.mult, r=[tps], w=[tp])
            C.ts(pall[:, o1:o1 + 128], ps[:, 1:129], math.sqrt(128.0), ALU.mult, r=[tps], w=[tp])
        sel = C.sb([8, 8 * 128]); tsel = Trk()
        C.dma("sync", sel[:, :], sel_d[:, :], w=[tsel])
        reps = Rot([(C.sb([128, 1530]), Trk()) for _ in range(2)])
        prot = Rot([(C.ps([128, 512]), PTrk()) for _ in range(3)])
        for h in range(8):
            rp, trp = reps.get()
            for c3 in range(3):
                pp, tpp = prot.get()
                C.mm(pp[:, 0:510], sel[:, h * 128:(h + 1) * 128], pall[:, c3 * 510:(c3 + 1) * 510], r=[tsel, tp], w=[tpp])
                C.cp(rp[:, c3 * 510:(c3 + 1) * 510], pp[:, 0:510], r=[tpp], w=[trp], eng=("scalar" if c3 % 2 else "vector"))
            tb = Trk()
            C.dma("sync", bvd[h], rp[:, :], r=[trp], w=[tb])
            for g in range(3):
                for wch in range(2):
                    src = bass.AP(bvd, h * 128 * 1530 + (g * 2 + wch) * 255 + 127, [[1529, 128], [1, 128]])
                    C.dma("sync", btile[g, h, wch], src, r=[tb])
        C.S.emit()


def attn_core(C, hd, KT, tKT, VT, tVT, QT, tQT, bc, bp, tbias, dil, scale, ident, tid, ones, tones,
              oacc, toacc, dacc, tdacc, first, kprev_dec, vprev_dec, tprevd, pools, aug, textra=()):
    srot, orot, vrot, prot_sb, vt = pools
    nb = T // (128 * dil)
    hv = hd + 1 if aug else hd

    def cols(X, unit):
        kind, a, b = unit
        if kind == "p":
            s0 = a + dil * 128 * b
            return X[:, s0:s0 + dil * 127 + 1:dil]
        return X[:, dec_base(a):dec_base(a) + 128]

    units = [("p", r, b) for r in range(dil) for b in range(nb)] + [("d", n, 0) for n in range(ND)]
    tvt = {}
    for ui, u in enumerate(units):
        pt, tpt = vrot.get()
        C.tr(pt[:, 0:hd], cols(VT, u), ident[0:hd, 0:hd], r=[tVT, tid], w=[tpt])
        tvt[ui] = Trk()
        C.cp(vt[:, ui, 0:hd], pt[:, 0:hd], r=[tpt], w=[tvt[ui]], eng=("scalar" if ui % 2 else "vector"))
    for ui, u in enumerate(units):
        kind, a, b = u
        has_prev = (kind == "d") or b > 0
        sp, tsp = srot.get()
        C.mm(sp[:, 0:128], cols(KT, u), cols(QT, u), start=True, stop=False, r=[tKT, tQT], w=[tsp])
        C.mm(sp[:, 0:128], ident[:, :], bc, start=False, stop=True, r=[tid, tbias], w=[tsp])
        if has_prev:
            if kind == "p":
                kprev = cols(KT, ("p", a, b - 1))
                rk = [tKT]
            else:
                kprev = kprev_dec[:, a, :]
                rk = [tprevd]
            C.mm(sp[:, 128:256], kprev, cols(QT, u), start=True, stop=False, r=rk + [tQT], w=[tsp])
            C.mm(sp[:, 128:256], ident[:, :], bp, start=False, stop=True, r=[tid, tbias], w=[tsp])
        width = 256 if has_prev else 128
        pb, tpb = prot_sb.get()
        C.act(pb[:, 0:width], sp[:, 0:width], AF.Exp, scale=scale, r=[tsp], w=[tpb])
        op_, top = orot.get()
        if has_prev:
            vprev = vt[:, ui - 1, 0:hv] if kind == "p" else vprev_dec[:, a, 0:hv]
            rv_ = [tvt[ui - 1]] if kind == "p" else [tprevd]
        C.mm(op_[0:hv, 0:128], vt[:, ui, 0:hv], pb[:, 0:128], start=True, stop=not has_prev, r=[tvt[ui], tpb] + list(textra), w=[top])
        if has_prev:
            C.mm(op_[0:hv, 0:128], vprev, pb[:, 128:256], start=False, stop=True, r=rv_ + [tpb], w=[top])
        if not aug:
            C.mm(op_[0:1, 128:256], ones[:, 0:1], pb[:, 0:128], start=True, stop=not has_prev, r=[tones, tpb], w=[top])
            if has_prev:
                C.mm(op_[0:1, 128:256], ones[:, 0:1], pb[:, 128:256], start=False, stop=True, r=[tones, tpb], w=[top])
        oc = cols(oacc, u)
        if first:
            C.cp(oc[0:hv, :], op_[0:hv, 0:128], r=[top], w=[toacc], eng="vector")
            if not aug:
                C.cp(cols(dacc, u), op_[0:1, 128:256], r=[top], w=[tdacc], eng="scalar")
        else:
            C.tt(oc[0:hv, :], oc[0:hv, :], op_[0:hv, 0:128], ALU.add, r=[top, toacc], w=[toacc])
            C.tt(cols(dacc, u), cols(dacc, u), op_[0:1, 128:256], ALU.add, r=[top, tdacc], w=[tdacc])


def nat_rows_out(C, XT, tXT, hd, ident, tid, vrot, strot, c0, nblk, dst_fn):
    for i in range(nblk):
        pt, tpt = vrot.get()
        C.tr(pt[:, 0:hd], XT[:, c0 + 128 * i:c0 + 128 * (i + 1)], ident[0:hd, 0:hd], r=[tXT, tid], w=[tpt])
        s_, ts_ = strot.get()
        C.cp(s_[:, 0:hd], pt[:, 0:hd], r=[tpt], w=[ts_], eng=("scalar" if i % 2 else "vector"))
        C.S.op("sync", lambda e, d=dst_fn(i), s_=s_: e.dma_start(out=d, in_=s_[:, 0:hd], allow_slow_non_contiguous=True), r=[ts_], dma=True)


def phase_dil(C, l, btile, ident_d, caches, o_caches):
    scale = 128.0 ** -0.5
    C.ph = ExitStack()
    with C.ph:
        ident = C.sb([128, 128]); tid = Trk()
        C.dma("sync", ident[:, :], ident_d[:, :], w=[tid])
        ones = C.sb([128, 128]); tones = Trk()
        C.memset(ones[:, :], 1.0, w=[tones])
        KT = C.sb([128, TT]); tKT = Trk()
        VT = C.sb([128, TT]); tVT = Trk()
        QT = C.sb([128, TT]); tQT = Trk()
        oacc = C.sb([128, TT]); toacc = Trk()
        dacc = C.sb([1, TT]); tdacc = Trk()
        bias = C.sb([128, 2, 128]); tbias = Trk()
        vt = C.sb([128, NCH, 128])
        kpd = C.sb([128, ND, 128]); vpd = C.sb([128, ND, 128]); tprevd = Trk()
        kraw = C.sb([128, ND, 128]); tkraw = Trk()
        srot = Rot([(C.ps([128, 512]), PTrk()) for _ in range(2)])
        orot = Rot([(C.ps([128, 512]), PTrk()) for _ in range(2)])
        vrot = Rot([(C.ps([128, 512]), PTrk()) for _ in range(2)])
        brot = Rot([(C.ps([128, 512]), PTrk()) for _ in range(2)])
        prot_sb = Rot([(C.sb([128, 256]), Trk()) for _ in range(3)])
        strot = Rot([(C.sb([128, 128]), Trk()) for _ in range(3)])
        pools = (srot, orot, vrot, prot_sb, vt)
        ys = C.sb([128, TT], BF16); tys = Trk()
        for qh in range(8):
            kh = qh // 4
            for g in range(3):
                if 'one' in DILOPT and (qh > 0 or g != int(next((x[1:] for x in DILOPT if x.startswith('g')), '0'))):
                    continue
                dil = DILS[g]
                win = WINS[g]
                gk, gv, gq = GIDX[f"bk{g}_{kh}"], GIDX[f"bv{g}_{kh}"], GIDX[f"bq{g}_{qh}"]
                C.dma("sync", KT[:, :], C.hT[gk * 128:(gk + 1) * 128, :], w=[tKT])
                C.dma("sync", VT[:, :], C.hT[gv * 128:(gv + 1) * 128, :], w=[tVT])
                C.dma("sync", QT[:, :], C.hT[gq * 128:(gq + 1) * 128, :], w=[tQT])
                C.dma("sync", bias[:, 0, :], btile[g, qh, 0], w=[tbias])
                C.dma("sync", bias[:, 1, :], btile[g, qh, 1], w=[tbias])
                cin = caches[g]
                for n in range(ND):
                    C.dma("sync", kraw[:, n, :], cin[l, n, 0:win:dil, 0, kh, :], w=[tkraw])
                    C.dma("sync", vpd[:, n, :], cin[l, n, 0:win:dil, 1, kh, :], w=[tprevd])
                for n in range(ND):
                    pt, tpt = vrot.get()
                    C.tr(pt[:, 0:128], kraw[:, n, :], ident[:, :], r=[tkraw, tid], w=[tpt])
                    C.cp(kpd[:, n, :], pt[:, 0:128], r=[tpt], w=[tprevd])
                if 'nocore' not in DILOPT:
                    attn_core(C, 128, KT, tKT, VT, tVT, QT, tQT, bias[:, 0, :], bias[:, 1, :], tbias, dil, scale, ident, tid,
                              ones, tones, oacc, toacc, dacc, tdacc, g == 0, kpd, vpd, tprevd, pools, False)
                if qh % 4 == 0 and 'nocache' not in DILOPT:
                    oc = o_caches[g]
                    for which, XT, tX in ((0, KT, tKT), (1, VT, tVT)):
                        nat_rows_out(C, XT, tX, 128, ident, tid, vrot, strot, T - win, win // 128,
                                     lambda i, which=which, oc=oc: oc[l, 0, 128 * i:128 * (i + 1), which, kh, :])
                        for n in range(ND):
                            b0 = dec_base(n)
                            for r0_ in range(0, win - 1, 256):
                                r1_ = min(win - 1, r0_ + 256)
                                C.dma("sync", oc[l, 1 + n, r0_:r1_, which, kh, :], cin[l, n, 1 + r0_:1 + r1_, which, kh, :])
                            C.S.op("sync", lambda e, XT=XT, b0=b0, n=n, which=which, oc=oc, win=win, kh=kh: e.dma_start(
                                out=oc[l, 1 + n, win - 1, which, kh, :].rearrange("(p o) -> p o", o=1), in_=XT[:, b0:b0 + 1],
                                allow_slow_non_contiguous=True), r=[tX], dma=True)
            gb = GIDX[f"bg{qh}"]
            C.dma("sync", QT[:, :], C.hT[gb * 128:(gb + 1) * 128, :], w=[tQT])
            C.act(QT[:, :], QT[:, :], AF.Silu, r=[tQT], w=[tQT])
            C.S.op("vector", lambda e: e.reciprocal(out=dacc[:, :], in_=dacc[:, :]), r=[tdacc], w=[tdacc])
            for t in range(NST):
                pb_, tpb_ = brot.get()
                sl = slice(t * 512, (t + 1) * 512)
                C.mm(pb_[:, :], ones[0:1, :], dacc[0:1, sl], r=[tones, tdacc], w=[tpb_])
                C.tt(oacc[:, sl], oacc[:, sl], pb_[:, :], ALU.mult, r=[toacc, tpb_], w=[toacc])
                C.tt(ys[:, sl], oacc[:, sl], QT[:, sl], ALU.mult, r=[toacc, tQT], w=[tys], eng="gpsimd")
            C.dma("sync", C.ysT[1024 + qh * 128:1024 + (qh + 1) * 128, :], ys[:, :], r=[tys])
        C.S.emit()


def phase_swa(C, l, masks_d, ident_d, rope_d, sink_d, cswa, o_cswa):
    scale = 64.0 ** -0.5
    C.ph = ExitStack()
    with C.ph:
        ident = C.sb([128, 128]); tid = Trk()
        C.dma("sync", ident[:, :], ident_d[:, :], w=[tid])
        ones = C.sb([128, 128]); tones = Trk()
        C.memset(ones[:, :], 1.0, w=[tones])
        msk = C.sb([128, 2, 128]); tmsk = Trk()
        C.dma("sync", msk[:, 0, :], masks_d[0], w=[tmsk])
        C.dma("sync", msk[:, 1, :], masks_d[1], w=[tmsk])
        cs = C.sb([64, 2, TT]); tcs = Trk()
        C.dma("sync", cs[:, 0, :], rope_d[0], w=[tcs])
        C.dma("sync", cs[:, 1, :], rope_d[1], w=[tcs])
        snk = C.sb([128, 16]); tsnk = Trk()
        C.dma("sync", snk[:, :], sink_d[l], w=[tsnk])
        C.act(snk[:, :], snk[:, :], AF.Exp, r=[tsnk], w=[tsnk])
        KT = C.sb([64, TT]); tKT = Trk()
        VT = C.sb([64, TT]); tVT = Trk()
        QT = C.sb([64, TT]); tQT = Trk()
        SW = C.sb([64, TT]); tSW = Trk()
        oacc = C.sb([65, TT]); toacc = Trk()
        vt = C.sb([128, NCH, 65])
        tvtones = Trk()
        C.memset(vt[:, :, 64:65], 1.0, w=[tvtones])
        kpd = C.sb([64, ND, 128]); vpd = C.sb([128, ND, 65]); tprevd = Trk()
        C.memset(vpd[:, :, 64:65], 1.0, w=[tprevd])
        kraw = C.sb([128, ND, 64]); tkraw = Trk()
        srot = Rot([(C.ps([128, 512]), PTrk()) for _ in range(2)])
        orot = Rot([(C.ps([128, 512]), PTrk()) for _ in range(2)])
        vrot = Rot([(C.ps([128, 512]), PTrk()) for _ in range(2)])
        brot = Rot([(C.ps([128, 512]), PTrk()) for _ in range(2)])
        prot_sb = Rot([(C.sb([128, 256]), Trk()) for _ in range(3)])
        strot = Rot([(C.sb([128, 128]), Trk()) for _ in range(3)])
        pools = (srot, orot, vrot, prot_sb, vt)
        ys = C.sb([64, TT], BF16); tys = Trk()
        gk, gv = GIDX["dk"], GIDX["dv"]

        def rope(X, tX, src_rows):
            C.dma("sync", X[:, :], C.hT[src_rows:src_rows + 64, :], w=[tX])
            C.dma("sync", SW[0:32, :], C.hT[src_rows + 32:src_rows + 64, :], w=[tSW])
            C.dma("sync", SW[32:64, :], C.hT[src_rows:src_rows + 32, :], w=[tSW])
            C.tt(X[:, :], X[:, :], cs[:, 0, :], ALU.mult, r=[tX, tcs], w=[tX])
            C.tt(SW[:, :], SW[:, :], cs[:, 1, :], ALU.mult, r=[tSW, tcs], w=[tSW], eng="gpsimd")
            C.tt(X[:, :], X[:, :], SW[:, :], ALU.add, r=[tX, tSW], w=[tX])

        for kh in range(2):
            rope(KT, tKT, gk * 128 + kh * 64)
            C.dma("sync", VT[0:64, :], C.hT[gv * 128 + kh * 64:gv * 128 + kh * 64 + 64, :], w=[tVT])
            for n in range(ND):
                C.dma("sync", kraw[:, n, :], cswa[l, n, :, 0, kh, :], w=[tkraw])
                C.dma("sync", vpd[:, n, 0:64], cswa[l, n, :, 1, kh, :], w=[tprevd])
            for n in range(ND):
                pt, tpt = vrot.get()
                C.tr(pt[0:64, 0:128], kraw[:, n, :], ident[:, :], r=[tkraw, tid], w=[tpt])
                C.cp(kpd[:, n, :], pt[0:64, 0:128], r=[tpt], w=[tprevd])
            for which, XT, tX in ((0, KT, tKT), (1, VT, tVT)):
                nat_rows_out(C, XT, tX, 64, ident, tid, vrot, strot, T - 128, 1,
                             lambda i, which=which: o_cswa[l, 0, :, which, kh, :])
                for n in range(ND):
                    b0 = dec_base(n)
                    C.dma("sync", o_cswa[l, 1 + n, 0:127, which, kh, :], cswa[l, n, 1:128, which, kh, :])
                    C.S.op("sync", lambda e, XT=XT, b0=b0, n=n, which=which, kh=kh: e.dma_start(
                        out=o_cswa[l, 1 + n, 127, which, kh, :].rearrange("(p o) -> p o", o=1), in_=XT[0:64, b0:b0 + 1],
                        allow_slow_non_contiguous=True), r=[tX], dma=True)
            for hh in range(8):
                h = kh * 8 + hh
                gq = GIDX[f"dq{h // 2}"]
                rope(QT, tQT, gq * 128 + (h % 2) * 64)
                attn_core(C, 64, KT, tKT, VT, tVT, QT, tQT, msk[:, 0, :], msk[:, 1, :], tmsk, 1, scale, ident, tid,
                          ones, tones, oacc, toacc, None, None, True, kpd, vpd, tprevd, pools, True, textra=[tvtones])
                C.ts(oacc[64:65, :], oacc[64:65, :], snk[64:65, h:h + 1], ALU.add, r=[toacc, tsnk], w=[toacc])
                C.S.op("vector", lambda e: e.reciprocal(out=oacc[64:65, :], in_=oacc[64:65, :]), r=[toacc], w=[toacc])
                gg = GIDX[f"dg{h // 2}"]
                C.dma("sync", QT[:, :], C.hT[gg * 128 + (h % 2) * 64:gg * 128 + (h % 2) * 64 + 64, :], w=[tQT])
                C.act(QT[:, :], QT[:, :], AF.Silu, r=[tQT], w=[tQT])
                for t in range(NST):
                    pb_, tpb_ = brot.get()
                    sl = slice(t * 512, (t + 1) * 512)
                    C.mm(pb_[0:64, :], ones[64:65, 0:64], oacc[64:65, sl], r=[tones, toacc], w=[tpb_])
                    C.tt(oacc[0:64, sl], oacc[0:64, sl], pb_[0:64, :], ALU.mult, r=[toacc, tpb_], w=[toacc])
                    C.tt(ys[:, sl], oacc[0:64, sl], QT[:, sl], ALU.mult, r=[toacc, tQT], w=[tys], eng="gpsimd")
                C.dma("sync", C.ysT[3072 + h * 64:3072 + (h + 1) * 64, :], ys[:, :], r=[tys])
        C.S.emit()


def phase_mirror_ys(C):
    C.ph = ExitStack()
    with C.ph:
        chain = [Trk() for _ in range(4)]
        for n in range(ND):
            b0 = dec_base(n)
            for rb in (1024, 3072):
                for r0 in range(rb, rb + 1024, 128):
                    C.S.op("sync", lambda e, b0=b0, r0=r0: e.dma_start(out=C.ysT[r0:r0 + 128, b0 + 3:b0 + 4], in_=C.ysT[r0:r0 + 128, b0:b0 + 1],
                                                                       allow_slow_non_contiguous=True), w=[chain[(r0 // 128) % 4]], dma=True)
        C.S.emit()


def phase_gdn(C, l, ident_d, sel_d, gmask_d, tmask_d, gcw_d, gal_d, gnw_d, st_gdn, o_gdn, o_gdnc):
    BIG = 1.0e4
    C.ph = ExitStack()
    with C.ph:
        ident = C.sb([128, 128]); tid = Trk()
        C.dma("sync", ident[:, :], ident_d[:, :], w=[tid])
        ones = C.sb([128, 128]); tones = Trk()
        C.memset(ones[:, :], 1.0, w=[tones])
        sel = C.sb([8, 8 * 128]); tsel = Trk()
        C.dma("sync", sel[:, :], sel_d[:, :], w=[tsel])
        gm = C.sb([128, 3, 128]); tgm = Trk()
        for i in range(3):
            C.dma("sync", gm[:, i, :], gmask_d[i], w=[tgm])
        cw = C.sb([128, 24 * 4]); tcw = Trk()
        C.dma("sync", cw[:, :], gcw_d[l], w=[tcw])
        gal = C.sb([8, 2]); tgal = Trk()
        C.dma("sync", gal[:, :], gal_d[l], w=[tgal])
        C.act(gal[:, 0:1], gal[:, 0:1], AF.Exp, r=[tgal], w=[tgal])
        C.ts(gal[:, 0:1], gal[:, 0:1], -1.0, ALU.mult, r=[tgal], w=[tgal])
        nw = C.sb([128, 1]); tnw = Trk()
        C.dma("sync", nw[:, :], gnw_d[l], w=[tnw])
        S = [C.sb([128, 128]) for _ in range(8)]
        tS = [Trk() for _ in range(8)]
        RW = {k: (C.sb([8, 512]), Trk()) for k in ("b", "a", "tm", "rm", "beta", "lb", "g", "gc", "gcl", "eg", "bg", "ekd")}
        egl = C.sb([8, 4]); tegl = Trk()
        colsb = C.sb([128, 4, 5, 8]); tcols = Trk()
        def mkset():
            U = dict(halo=[C.sb([128, 515]) for _ in range(3)], thalo=[Trk() for _ in range(3)],
                     zt=C.sb([128, 512]), tzt=Trk(), qs=C.sb([128, 512]), ks=C.sb([128, 512]), vs=C.sb([128, 512]),
                     tqs=Trk(), tks=Trk(), tvs=Trk(), sq=C.sb([128, 512]), tsq=Trk(), rrow=C.sb([1, 2, 512]), trrow=Trk(),
                     gcb=C.sb([128, 512]), gclb=C.sb([128, 512]), tgcb=Trk(), tgclb=Trk(), qg=C.sb([128, 512]), tqg=Trk(),
                     eglb=C.sb([128, 4]), teglb=Trk(), oT=C.sb([128, 512]), toT=Trk(), ysb=C.sb([128, 512], BF16), tysb=Trk(),
                     vn=C.sb([128, 128]), tvn=Trk(), streams=[])
            for _ in range(2):
                U["streams"].append(dict(
                    Rm=C.sb([128, 256]), tR=Trk(), kd=C.sb([128, 128]), tkd=Trk(),
                    Lp=[C.sb([128, 128]) for _ in range(2)], tLp=[Trk(), Trk()],
                    Ltp=[C.sb([128, 128]) for _ in range(2)], tLtp=[Trk(), Trk()],
                    dtile=C.sb([128, 3, 128]), tdt=[Trk(), Trk(), Trk()],
                    qkm=C.sb([128, 128]), tqkm=Trk(), wT=C.sb([128, 128]), twT=Trk(),
                    bank=(C.ps([128, 512]), PTrk())))
            return U
        USETS = [mkset(), mkset()]
        pa = Rot([(C.ps([128, 512]), PTrk()) for _ in range(2)])
        pb = Rot([(C.ps([128, 512]), PTrk()) for _ in range(2)])
        for s in range(NST):
            c0 = s * 512
            sl = slice(c0, c0 + 512)
            ga = GIDX["aba"]
            for k_, rows in (("b", 0), ("a", 8)):
                C.dma("sync", RW[k_][0][:, :], C.hT[ga * 128 + rows:ga * 128 + rows + 8, sl], w=[RW[k_][1]])
            C.dma("sync", RW["tm"][0][:, :], tmask_d[0, :, sl], w=[RW["tm"][1]])
            C.dma("sync", RW["rm"][0][:, :], tmask_d[1, :, sl], w=[RW["rm"][1]])
            R_ = lambda k: RW[k][0][:, :]
            t_ = lambda k: RW[k][1]
            C.act(R_("beta"), R_("b"), AF.Sigmoid, r=[t_("b")], w=[t_("beta")])
            C.tt(R_("beta"), R_("beta"), R_("tm"), ALU.mult, r=[t_("beta"), t_("tm")], w=[t_("beta")])
            C.ts(R_("lb"), R_("beta"), 1e-18, ALU.add, r=[t_("beta")], w=[t_("lb")])
            C.act(R_("lb"), R_("lb"), AF.Ln, r=[t_("lb")], w=[t_("lb")])
            C.act(R_("g"), R_("a"), AF.Exp, bias=gal[:, 1:2], r=[t_("a"), tgal], w=[t_("g")])
            C.act(R_("g"), R_("g"), AF.Ln, bias=1.0, r=[t_("g")], w=[t_("g")])
            C.ts(R_("g"), R_("g"), gal[:, 0:1], ALU.mult, r=[t_("g"), tgal], w=[t_("g")])
            C.tt(R_("g"), R_("g"), R_("tm"), ALU.mult, r=[t_("g"), t_("tm")], w=[t_("g")])
            C.S.op("vector", lambda e: e.tensor_tensor_scan(out=RW["gc"][0][:, :], data0=RW["rm"][0][:, :], data1=RW["g"][0][:, :],
                                                           initial=0.0, op0=ALU.mult, op1=ALU.add), r=[t_("rm"), t_("g")], w=[t_("gc")])
            C.tt(R_("gcl"), R_("gc"), R_("lb"), ALU.add, r=[t_("gc"), t_("lb")], w=[t_("gcl")])
            C.act(R_("eg"), R_("gc"), AF.Exp, r=[t_("gc")], w=[t_("eg")])
            C.tt(R_("bg"), R_("beta"), R_("eg"), ALU.mult, r=[t_("beta"), t_("eg")], w=[t_("bg")])
            for k in range(4):
                cs_ = slice(k * 128, (k + 1) * 128)
                C.act(RW["ekd"][0][:, cs_], RW["gc"][0][:, cs_], AF.Exp, scale=-1.0, bias=RW["gc"][0][:, k * 128 + 127:k * 128 + 128],
                      r=[t_("gc")], w=[t_("ekd")])
                C.act(egl[:, k:k + 1], RW["gc"][0][:, k * 128 + 127:k * 128 + 128], AF.Exp, r=[t_("gc")], w=[tegl])
            pc_, tpc_ = pa.get()
            for k in range(4):
                for qi, q in enumerate(("beta", "bg", "ekd", "gc", "gcl")):
                    o_ = (k * 5 + qi) * 8
                    C.tr(pc_[:, o_:o_ + 8], RW[q][0][:, k * 128:(k + 1) * 128], ident[0:8, 0:8], r=[t_(q), tid], w=[tpc_])
            C.cp(colsb[:, :, :, :].rearrange("p a b c -> p (a b c)"), pc_[:, 0:160], r=[tpc_], w=[tcols])
            def head_gen(h, U):
                halo, thalo, zt, tzt, qs, ks, vs = U["halo"], U["thalo"], U["zt"], U["tzt"], U["qs"], U["ks"], U["vs"]
                tqs, tks, tvs, sq, tsq, rrow, trrow = U["tqs"], U["tks"], U["tvs"], U["sq"], U["tsq"], U["rrow"], U["trrow"]
                gcb, gclb, tgcb, tgclb, qg, tqg = U["gcb"], U["gclb"], U["tgcb"], U["tgclb"], U["qg"], U["tqg"]
                eglb, teglb, oT, toT, ysb, tysb, vn, tvn = U["eglb"], U["teglb"], U["oT"], U["toT"], U["ysb"], U["tysb"], U["vn"], U["tvn"]
                streams = U["streams"]
                col = lambda k, qi: colsb[:, k, qi, h:h + 1]
                selh = sel[:, h * 128:(h + 1) * 128]
                srcs = (GIDX[f"aq{h}"], GIDX[f"ak{h}"], GIDX[f"av{h}"])
                for i3 in range(3):
                    g_ = srcs[i3]
                    if s == 0:
                        C.memset(halo[i3][:, 0:3], 0.0, w=[thalo[i3]])
                        C.dma("sync", halo[i3][:, 3:515], C.hT[g_ * 128:(g_ + 1) * 128, 0:512], w=[thalo[i3]])
                    else:
                        C.dma("sync", halo[i3][:, :], C.hT[g_ * 128:(g_ + 1) * 128, c0 - 3:c0 + 512], w=[thalo[i3]])
                gz = GIDX[f"az{h}"]
                C.dma("sync", zt[:, :], C.hT[gz * 128:(gz + 1) * 128, sl], w=[tzt])
                segs = [(0, T - 1)] + [(1 + n, dec_base(n) + 3) for n in range(ND)]
                for si, last in segs:
                    if c0 <= last < c0 + 512:
                        lc = last - c0
                        for i3 in range(3):
                            cc0 = i3 * 1024 + h * 128
                            C.S.op("sync", lambda e, si=si, lc=lc, i3=i3, cc0=cc0: e.dma_start(
                                out=o_gdnc[l, si, :, cc0:cc0 + 128].rearrange("j p -> p j"), in_=halo[i3][:, lc + 1:lc + 4],
                                allow_slow_non_contiguous=True), r=[thalo[i3]], dma=True)
                for i3, (dst, tdst) in enumerate(((qs, tqs), (ks, tks), (vs, tvs))):
                    wv = lambda j, i3=i3: cw[:, (i3 * 8 + h) * 4 + j:(i3 * 8 + h) * 4 + j + 1]
                    C.act(dst[:, :], halo[i3][:, 0:512], AF.Identity, scale=wv(0), r=[thalo[i3], tcw], w=[tdst])
                    for j in range(1, 4):
                        C.stt(dst[:, :], halo[i3][:, j:j + 512], wv(j), dst[:, :], ALU.mult, ALU.add, r=[thalo[i3], tcw, tdst], w=[tdst])
                    C.act(dst[:, :], dst[:, :], AF.Silu, r=[tdst], w=[tdst])
                    yield
                for i2, (src, tsrc, extra) in enumerate(((qs, tqs, math.log(128.0 ** -0.5)), (ks, tks, 0.0))):
                    C.act(sq[:, :], src[:, :], AF.Square, r=[tsrc], w=[tsq])
                    pp, tpp = pa.get()
                    C.mm(pp[0:1, :], ones[:, 0:1], sq[:, :], r=[tones, tsq], w=[tpp])
                    C.ts(rrow[:, i2, :], pp[0:1, :], 1e-6, ALU.add, r=[tpp], w=[trrow])
                    C.act(rrow[:, i2, :], rrow[:, i2, :], AF.Ln, r=[trrow], w=[trrow])
                    C.act(rrow[:, i2, :], rrow[:, i2, :], AF.Exp, scale=-0.5, bias=extra, r=[trrow], w=[trrow])
                    pp2, tpp2 = pa.get()
                    C.mm(pp2[:, :], ones[0:1, :], rrow[:, i2, :], r=[tones, trrow], w=[tpp2])
                    C.tt(src[:, :], src[:, :], pp2[:, :], ALU.mult, r=[tsrc, tpp2], w=[tsrc])
                    yield
                pp, tpp = pa.get()
                C.mm(pp[:, :], selh, RW["gc"][0][:, :], r=[tsel, t_("gc")], w=[tpp])
                C.cp(gcb[:, :], pp[:, :], r=[tpp], w=[tgcb], eng="scalar")
                yield
                pp, tpp = pa.get()
                C.mm(pp[:, :], selh, RW["gcl"][0][:, :], r=[tsel, t_("gcl")], w=[tpp])
                C.cp(gclb[:, :], pp[:, :], r=[tpp], w=[tgclb], eng="scalar")
                yield
                pp, tpp = pa.get()
                C.mm(pp[:, :], selh, RW["eg"][0][:, :], r=[tsel, t_("eg")], w=[tpp])
                C.tt(qg[:, :], qs[:, :], pp[:, :], ALU.mult, r=[tqs, tpp], w=[tqg])
                pp, tpp = pa.get()
                C.mm(pp[:, 0:4], selh, egl[:, :], r=[tsel, tegl], w=[tpp])
                C.cp(eglb[:, :], pp[:, 0:4], r=[tpp], w=[teglb])
                yield
                def solve_gen(k, B):
                    cs_ = slice(k * 128, (k + 1) * 128)
                    bk, tbk = B["bank"]
                    Rm, tR, kd, tkd = B["Rm"], B["tR"], B["kd"], B["tkd"]
                    Lp, tLp, Ltp, tLtp = B["Lp"], B["tLp"], B["Ltp"], B["tLtp"]
                    dtile, tdt, qkm, tqkm, wT, twT = B["dtile"], B["tdt"], B["qkm"], B["tqkm"], B["wT"], B["twT"]
                    C.tr(bk[:, 0:128], ks[:, cs_], ident[:, :], r=[tks, tid], w=[tbk])
                    C.tr(bk[:, 128:256], vs[:, cs_], ident[:, :], r=[tvs, tid], w=[tbk])
                    yield
                    C.ts(Rm[:, 128:256], bk[:, 0:128], col(k, 1), ALU.mult, r=[tbk, tcols], w=[tR])
                    C.ts(kd[:, :], bk[:, 0:128], col(k, 2), ALU.mult, r=[tbk, tcols], w=[tkd])
                    C.ts(Rm[:, 0:128], bk[:, 128:256], col(k, 0), ALU.mult, r=[tbk, tcols], w=[tR])
                    yield
                    C.mm(bk[:, 256:384], ks[:, cs_], ks[:, cs_], r=[tks], w=[tbk])
                    C.mm(bk[:, 384:512], ks[:, cs_], qs[:, cs_], r=[tks, tqs], w=[tbk])
                    C.stt(dtile[:, 0, :], gclb[:, cs_], col(k, 3), gm[:, 0, :], ALU.subtract, ALU.min, r=[tgclb, tcols, tgm], w=[tdt[0]])
                    C.act(dtile[:, 0, :], dtile[:, 0, :], AF.Exp, r=[tdt[0]], w=[tdt[0]])
                    C.stt(dtile[:, 1, :], gcb[:, cs_], col(k, 4), gm[:, 2, :], ALU.subtract, ALU.max, r=[tgcb, tcols, tgm], w=[tdt[1]])
                    C.act(dtile[:, 1, :], dtile[:, 1, :], AF.Exp, scale=-1.0, r=[tdt[1]], w=[tdt[1]])
                    C.stt(dtile[:, 2, :], gcb[:, cs_], col(k, 3), gm[:, 1, :], ALU.subtract, ALU.min, r=[tgcb, tcols, tgm], w=[tdt[2]])
                    C.act(dtile[:, 2, :], dtile[:, 2, :], AF.Exp, r=[tdt[2]], w=[tdt[2]])
                    yield
                    C.tt(Ltp[0][:, :], bk[:, 256:384], dtile[:, 0, :], ALU.mult, r=[tbk, tdt[0]], w=[tLtp[0]])
                    C.tt(Lp[0][:, :], bk[:, 256:384], dtile[:, 1, :], ALU.mult, r=[tbk, tdt[1]], w=[tLp[0]])
                    C.tt(qkm[:, :], bk[:, 384:512], dtile[:, 2, :], ALU.mult, r=[tbk, tdt[2]], w=[tqkm])
                    yield
                    C.mm(bk[:, 0:256], Ltp[0][:, :], Rm[:, :], r=[tLtp[0], tR], w=[tbk])
                    yield
                    C.tt(Rm[:, :], Rm[:, :], bk[:, 0:256], ALU.subtract, r=[tR, tbk], w=[tR])
                    cur = 0
                    for lvl in range(1, 7):
                        nxt = 1 - cur
                        C.mm(bk[:, 256:384], Lp[cur][:, :], Ltp[cur][:, :], r=[tLp[cur], tLtp[cur]], w=[tbk])
                        if lvl < 6:
                            C.mm(bk[:, 384:512], Ltp[cur][:, :], Lp[cur][:, :], r=[tLp[cur], tLtp[cur]], w=[tbk])
                        yield
                        C.cp(Ltp[nxt][:, :], bk[:, 256:384], r=[tbk], w=[tLtp[nxt]], eng="scalar")
                        if lvl < 6:
                            C.cp(Lp[nxt][:, :], bk[:, 384:512], r=[tbk], w=[tLp[nxt]], eng="scalar")
                        yield
                        C.mm(bk[:, 0:256], Ltp[nxt][:, :], Rm[:, :], r=[tLtp[nxt], tR], w=[tbk])
                        yield
                        C.tt(Rm[:, :], Rm[:, :], bk[:, 0:256], ALU.add, r=[tR, tbk], w=[tR])
                        cur = nxt
                    C.tr(bk[:, 0:128], Rm[:, 128:256], ident[:, :], r=[tR, tid], w=[tbk])
                    yield
                    C.cp(wT[:, :], bk[:, 0:128], r=[tbk], w=[twT], eng="scalar")

                for pair in ((0, 1), (2, 3)):
                    gens = [solve_gen(k, streams[i]) for i, k in enumerate(pair)]
                    alive = list(gens)
                    while alive:
                        for g_ in list(alive):
                            try:
                                next(g_)
                            except StopIteration:
                                alive.remove(g_)
                        yield
                    for i, k in enumerate(pair):
                        ci = 4 * s + k
                        cs_ = slice(k * 128, (k + 1) * 128)
                        B = streams[i]
                        seg_end = (ci == 31) or (ci >= 32)
                        if ci == 0:
                            C.memset(S[h][:, :], 0.0, w=[tS[h]])
                        elif ci >= 32:
                            C.dma("sync", S[h][:, :], st_gdn[l, ci - 32, h], w=[tS[h]])
                        p6, tp6 = pb.get()
                        C.mm(p6[:, 0:128], B["wT"][:, :], S[h][:, :], r=[B["twT"], tS[h]], w=[tp6])
                        yield
                        C.tt(vn[:, :], B["Rm"][:, 0:128], p6[:, 0:128], ALU.subtract, r=[B["tR"], tp6], w=[tvn])
                        yield
                        p7, tp7 = pb.get()
                        C.mm(p7[:, 0:128], S[h][:, :], qg[:, cs_], start=True, stop=False, r=[tS[h], tqg], w=[tp7])
                        C.mm(p7[:, 0:128], vn[:, :], B["qkm"][:, :], start=False, stop=True, r=[tvn, B["tqkm"]], w=[tp7])
                        C.mm(p7[:, 128:256], B["kd"][:, :], vn[:, :], r=[B["tkd"], tvn], w=[tp7])
                        yield
                        C.stt(S[h][:, :], S[h][:, :], eglb[:, k:k + 1], p7[:, 128:256], ALU.mult, ALU.add, r=[tS[h], teglb, tp7], w=[tS[h]])
                        C.cp(oT[:, cs_], p7[:, 0:128], r=[tp7], w=[toT], eng="vector")
                        if seg_end:
                            si = 0 if ci == 31 else 1 + (ci - 32)
                            C.dma("sync", o_gdn[l, si, h], S[h][:, :], r=[tS[h]])
                        yield
                C.act(sq[:, :], oT[:, :], AF.Square, r=[toT], w=[tsq])
                pp, tpp = pa.get()
                C.mm(pp[0:1, :], ones[:, 0:1], sq[:, :], r=[tones, tsq], w=[tpp])
                C.ts(rrow[:, 0, :], pp[0:1, :], 1.0 / 128.0, ALU.mult, 1e-6, ALU.add, r=[tpp], w=[trrow])
                C.act(rrow[:, 0, :], rrow[:, 0, :], AF.Ln, r=[trrow], w=[trrow])
                C.act(rrow[:, 0, :], rrow[:, 0, :], AF.Exp, scale=-0.5, r=[trrow], w=[trrow])
                pp2, tpp2 = pa.get()
                C.mm(pp2[:, :], ones[0:1, :], rrow[:, 0, :], r=[tones, trrow], w=[tpp2])
                C.stt(oT[:, :], oT[:, :], nw[:, 0:1], pp2[:, :], ALU.mult, ALU.mult, r=[toT, tnw, tpp2], w=[toT])
                C.act(zt[:, :], zt[:, :], AF.Silu, r=[tzt], w=[tzt])
                C.tt(ysb[:, :], oT[:, :], zt[:, :], ALU.mult, r=[toT, tzt], w=[tysb])
                C.dma("sync", C.ysT[h * 128:(h + 1) * 128, sl], ysb[:, :], r=[tysb])

            for hp in range(0, 8, 2):
                hg = [head_gen(hp, USETS[0]), head_gen(hp + 1, USETS[1])]
                live = list(hg)
                while live:
                    for g_ in list(live):
                        try:
                            next(g_)
                        except StopIteration:
                            live.remove(g_)
        C.S.emit()


def kernel(**inputs):
    res = run_device(inputs)

    def gather(name):
        arrs = [np.asarray(res[c][name], dtype=np.float32) for c in range(8)]
        p = np.stack([arrs[0][:, 0], arrs[1][:, 0]], axis=1)
        s_ = np.concatenate([a[:, 1:] for a in arrs], axis=1)
        return np.ascontiguousarray(p), np.ascontiguousarray(s_)

    ys = [np.asarray(res[c]["y"], dtype=np.float32) for c in range(8)]
    y_prompt = np.stack([ys[0][:T], ys[1][:T]], axis=0)
    y_sample = np.stack([ys[c][dec_base(n) + 3] for c in range(8) for n in range(ND)], axis=0)[:, None, :]
    gp, gs = gather("o_gdn")
    gcp, gcs = gather("o_gdnc")
    d0p, d0s = gather("o_cd0")
    d1p, d1s = gather("o_cd1")
    d2p, d2s = gather("o_cd2")
    swp, sws = gather("o_cswa")
    lp_, ls_ = gather("o_lru")
    lcp, lcs = gather("o_lruc")
    return (np.ascontiguousarray(y_prompt), np.ascontiguousarray(y_sample), gp, gs, gcp, gcs, d0p, d0s, d1p, d1s, d2p, d2s,
            swp, sws, lp_, ls_, lcp, lcs)


def attn_core(C, hd, KT, tKT, VT, tVT, QT, tQT, bc, bp, tbias, dil, scale, ident, tid, ones, tones,
              oacc, toacc, dacc, tdacc, first, kprev_dec, vprev_dec, tprevd, pools, aug, textra=()):
    srot, orot, vrot, prot_sb, vt = pools
    nb = T // (128 * dil)
    hv = hd + 1 if aug else hd

    def cols(X, unit):
        kind, a, b = unit
        if kind == "p":
            s0 = a + dil * 128 * b
            return X[:, s0:s0 + dil * 127 + 1:dil]
        return X[:, dec_base(a):dec_base(a) + 128]

    units = [("p", r, b) for r in range(dil) for b in range(nb)] + [("d", n, 0) for n in range(ND)]
    tvt = {}

    def vtile(ui):
        if ui >= len(units) or ui in tvt:
            return
        pt, tpt = vrot.get()
        C.tr(pt[:, 0:hd], cols(VT, units[ui]), ident[0:hd, 0:hd], r=[tVT, tid], w=[tpt])
        tvt[ui] = Trk()
        C.cp(vt[:, ui, 0:hd], pt[:, 0:hd], r=[tpt], w=[tvt[ui]], eng=("scalar" if ui % 2 else "vector"))

    LOOK = 2
    for ui in range(LOOK):
        vtile(ui)
    for ui, u in enumerate(units):
        vtile(ui + LOOK)
        kind, a, b = u
        has_prev = (kind == "d") or b > 0
        sp, tsp = srot.get()
        C.mm(sp[:, 0:128], cols(KT, u), cols(QT, u), start=True, stop=False, r=[tKT, tQT], w=[tsp])
        C.mm(sp[:, 0:128], ident[:, :], bc, start=False, stop=True, r=[tid, tbias], w=[tsp])
        if has_prev:
            if kind == "p":
                kprev = cols(KT, ("p", a, b - 1))
                rk = [tKT]
            else:
                kprev = kprev_dec[:, a, :]
                rk = [tprevd]
            C.mm(sp[:, 128:256], kprev, cols(QT, u), start=True, stop=False, r=rk + [tQT], w=[tsp])
            C.mm(sp[:, 128:256], ident[:, :], bp, start=False, stop=True, r=[tid, tbias], w=[tsp])
        width = 256 if has_prev else 128
        pb, tpb = prot_sb.get()
        C.act(pb[:, 0:width], sp[:, 0:width], AF.Exp, scale=scale, r=[tsp], w=[tpb])
        op_, top = orot.get()
        if has_prev:
            vprev = vt[:, ui - 1, 0:hv] if kind == "p" else vprev_dec[:, a, 0:hv]
            rv_ = [tvt[ui - 1]] if kind == "p" else [tprevd]
        C.mm(op_[0:hv, 0:128], vt[:, ui, 0:hv], pb[:, 0:128], start=True, stop=not has_prev, r=[tvt[ui], tpb] + list(textra), w=[top])
        if has_prev:
            C.mm(op_[0:hv, 0:128], vprev, pb[:, 128:256], start=False, stop=True, r=rv_ + [tpb], w=[top])
        if not aug:
            C.mm(op_[0:1, 128:256], ones[:, 0:1], pb[:, 0:128], start=True, stop=not has_prev, r=[tones, tpb], w=[top])
            if has_prev:
                C.mm(op_[0:1, 128:256], ones[:, 0:1], pb[:, 128:256], start=False, stop=True, r=[tones, tpb], w=[top])
        oc = cols(oacc, u)
        if first:
            C.cp(oc[0:hv, :], op_[0:hv, 0:128], r=[top], w=[toacc], eng="vector")
            if not aug:
                C.cp(cols(dacc, u), op_[0:1, 128:256], r=[top], w=[tdacc], eng="vector")
        else:
            C.tt(oc[0:hv, :], oc[0:hv, :], op_[0:hv, 0:128], ALU.add, r=[top, toacc], w=[toacc])
            C.tt(cols(dacc, u), cols(dacc, u), op_[0:1, 128:256], ALU.add, r=[top, tdacc], w=[tdacc])


def phase_dil(C, l, btile, ident_d, caches, o_caches):
    scale = 128.0 ** -0.5
    C.ph = ExitStack()
    with C.ph:
        identF = C.sb([128, 128]); tidF = Trk()
        C.dma("sync", identF[:, :], ident_d[:, :], w=[tidF])
        ident = C.sb([128, 128], BF16); tid = Trk()
        C.dma("gpsimd", ident[:, :], ident_d[:, :], w=[tid])
        onesF = C.sb([128, 128]); tonesF = Trk()
        C.memset(onesF[:, :], 1.0, w=[tonesF])
        ones = C.sb([128, 128], BF16); tones = Trk()
        C.memset(ones[:, :], 1.0, w=[tones])
        sets = []
        for _ in range(2):
            sets.append(dict(KT=C.sb([128, TT], BF16), tKT=Trk(), VT=C.sb([128, TT], BF16), tVT=Trk(),
                             QT=C.sb([128, TT], BF16), tQT=Trk(), bias=C.sb([128, 2, 128], BF16), tbias=Trk(),
                             kpd=C.sb([128, ND, 128], BF16), vpd=C.sb([128, ND, 128], BF16), tprevd=Trk(),
                             kraw=C.sb([128, ND, 128], BF16), tkraw=Trk()))
        it = 0
        GT = C.sb([128, TT]); tGT = Trk()
        XF = C.sb([128, 2560]); tXF = Trk()
        oacc = C.sb([128, TT]); toacc = Trk()
        dacc = C.sb([1, TT]); tdacc = Trk()
        vt = C.sb([128, NCH, 128], BF16)
        srot = Rot([(C.ps([128, 512]), PTrk()) for _ in range(3)])
        orot = Rot([(C.ps([128, 512]), PTrk()) for _ in range(3)])
        vrot = Rot([(C.ps([128, 1024], BF16), PTrk()) for _ in range(2)])
        prot_sb = Rot([(C.sb([128, 256], BF16), Trk()) for _ in range(4)])
        strot = Rot([(C.sb([128, 128]), Trk()) for _ in range(3)])
        pools = (srot, orot, vrot, prot_sb, vt)
        ys = C.sb([128, TT], BF16); tys = Trk()
        for qh in range(8):
            kh = qh // 4
            for g in range(3):
                dil = DILS[g]
                win = WINS[g]
                Z = sets[it % 2]
                it += 1
                KT, tKT, VT, tVT, QT, tQT = Z["KT"], Z["tKT"], Z["VT"], Z["tVT"], Z["QT"], Z["tQT"]
                bias, tbias, kpd, vpd, tprevd, kraw, tkraw = Z["bias"], Z["tbias"], Z["kpd"], Z["vpd"], Z["tprevd"], Z["kraw"], Z["tkraw"]
                gk, gv, gq = GIDX[f"bk{g}_{kh}"], GIDX[f"bv{g}_{kh}"], GIDX[f"bq{g}_{qh}"]
                C.dma("gpsimd", KT[:, :], C.hT[gk * 128:(gk + 1) * 128, :], w=[tKT])
                C.dma("gpsimd", VT[:, :], C.hT[gv * 128:(gv + 1) * 128, :], w=[tVT])
                C.dma("gpsimd", QT[:, :], C.hT[gq * 128:(gq + 1) * 128, :], w=[tQT])
                C.dma("gpsimd", bias[:, 0, :], btile[g, qh, 0], w=[tbias])
                C.dma("gpsimd", bias[:, 1, :], btile[g, qh, 1], w=[tbias])
                cin = caches[g]
                for n in range(ND):
                    C.dma("gpsimd", kraw[:, n, :], cin[l, n, 0:win:dil, 0, kh, :], w=[tkraw])
                    C.dma("gpsimd", vpd[:, n, :], cin[l, n, 0:win:dil, 1, kh, :], w=[tprevd])
                for n in range(ND):
                    pt, tpt = vrot.get()
                    C.tr(pt[:, 0:128], kraw[:, n, :], ident[:, :], r=[tkraw, tid], w=[tpt])
                    C.cp(kpd[:, n, :], pt[:, 0:128], r=[tpt], w=[tprevd])
                attn_core(C, 128, KT, tKT, VT, tVT, QT, tQT, bias[:, 0, :], bias[:, 1, :], tbias, dil, scale, ident, tid,
                          ones, tones, oacc, toacc, dacc, tdacc, g == 0, kpd, vpd, tprevd, pools, False)
                if qh % 4 == 0:
                    oc = o_caches[g]
                    for which, gsrc in ((0, gk), (1, gv)):
                        C.dma("sync", XF[:, 0:win], C.hT[gsrc * 128:(gsrc + 1) * 128, T - win:T], w=[tXF])
                        C.dma("sync", XF[:, 2048:2560], C.hT[gsrc * 128:(gsrc + 1) * 128, T:TT], w=[tXF])
                        nat_rows_out(C, XF, tXF, 128, identF, tidF, orot, strot, 0, win // 128,
                                     lambda i, which=which, oc=oc, kh=kh: oc[l, 0, 128 * i:128 * (i + 1), which, kh, :])
                        for n in range(ND):
                            for r0_ in range(0, win - 1, 256):
                                r1_ = min(win - 1, r0_ + 256)
                                C.dma("sync", oc[l, 1 + n, r0_:r1_, which, kh, :], cin[l, n, 1 + r0_:1 + r1_, which, kh, :])
                            C.S.op("sync", lambda e, n=n, which=which, oc=oc, win=win, kh=kh: e.dma_start(
                                out=oc[l, 1 + n, win - 1, which, kh, :].rearrange("(p o) -> p o", o=1),
                                in_=XF[:, 2048 + 128 * n:2048 + 128 * n + 1], allow_slow_non_contiguous=True), r=[tXF], dma=True)
            gb = GIDX[f"bg{qh}"]
            C.dma("sync", GT[:, :], C.hT[gb * 128:(gb + 1) * 128, :], w=[tGT])
            C.act(GT[:, :], GT[:, :], AF.Silu, r=[tGT], w=[tGT])
            C.S.op("vector", lambda e: e.reciprocal(out=dacc[:, :], in_=dacc[:, :]), r=[tdacc], w=[tdacc])
            for t in range(NST):
                pb_, tpb_ = srot.get()
                sl = slice(t * 512, (t + 1) * 512)
                C.mm(pb_[:, :], onesF[0:1, :], dacc[0:1, sl], r=[tonesF, tdacc], w=[tpb_])
                C.tt(oacc[:, sl], oacc[:, sl], pb_[:, :], ALU.mult, r=[toacc, tpb_], w=[toacc])
                C.tt(ys[:, sl], oacc[:, sl], GT[:, sl], ALU.mult, r=[toacc, tGT], w=[tys])
            C.dma("sync", C.ysT[1024 + qh * 128:1024 + (qh + 1) * 128, :], ys[:, :], r=[tys])
        C.S.emit()


def phase_swa(C, l, masks_d, ident_d, rope_d, sink_d, cswa, o_cswa):
    scale = 64.0 ** -0.5
    C.ph = ExitStack()
    with C.ph:
        identF = C.sb([128, 128]); tidF = Trk()
        C.dma("sync", identF[:, :], ident_d[:, :], w=[tidF])
        ident = C.sb([128, 128], BF16); tid = Trk()
        C.dma("gpsimd", ident[:, :], ident_d[:, :], w=[tid])
        onesF = C.sb([128, 128]); tonesF = Trk()
        C.memset(onesF[:, :], 1.0, w=[tonesF])
        msk = C.sb([128, 2, 128], BF16); tmsk = Trk()
        C.dma("gpsimd", msk[:, 0, :], masks_d[0], w=[tmsk])
        C.dma("gpsimd", msk[:, 1, :], masks_d[1], w=[tmsk])
        cs = C.sb([64, 2, TT]); tcs = Trk()
        C.dma("sync", cs[:, 0, :], rope_d[0], w=[tcs])
        C.dma("sync", cs[:, 1, :], rope_d[1], w=[tcs])
        snk = C.sb([128, 16]); tsnk = Trk()
        C.dma("sync", snk[:, :], sink_d[l], w=[tsnk])
        C.act(snk[:, :], snk[:, :], AF.Exp, r=[tsnk], w=[tsnk])
        KF = C.sb([64, TT]); tKF = Trk()
        VF = C.sb([64, TT]); tVF = Trk()
        XF = C.sb([64, TT]); tXF = Trk()
        SW = C.sb([64, TT]); tSW = Trk()
        GF = C.sb([64, TT]); tGF = Trk()
        KT = C.sb([64, TT], BF16); tKT = Trk()
        VT = C.sb([64, TT], BF16); tVT = Trk()
        QTs = [(C.sb([64, TT], BF16), Trk()) for _ in range(2)]
        oacc = C.sb([65, TT]); toacc = Trk()
        vt = C.sb([128, NCH, 65], BF16)
        tvtones = Trk()
        C.memset(vt[:, :, 64:65], 1.0, w=[tvtones])
        kpd = C.sb([64, ND, 128], BF16); vpd = C.sb([128, ND, 65], BF16); tprevd = Trk()
        C.memset(vpd[:, :, 64:65], 1.0, w=[tprevd])
        kraw = C.sb([128, ND, 64], BF16); tkraw = Trk()
        srot = Rot([(C.ps([128, 512]), PTrk()) for _ in range(3)])
        orot = Rot([(C.ps([128, 512]), PTrk()) for _ in range(3)])
        vrot = Rot([(C.ps([128, 1024], BF16), PTrk()) for _ in range(2)])
        prot_sb = Rot([(C.sb([128, 256], BF16), Trk()) for _ in range(4)])
        strot = Rot([(C.sb([128, 128]), Trk()) for _ in range(3)])
        pools = (srot, orot, vrot, prot_sb, vt)
        ys = C.sb([64, TT], BF16); tys = Trk()
        gk, gv = GIDX["dk"], GIDX["dv"]

        def rope(X, tX, src_rows, X16, tX16):
            C.dma("sync", X[:, :], C.hT[src_rows:src_rows + 64, :], w=[tX])
            C.dma("sync", SW[0:32, :], C.hT[src_rows + 32:src_rows + 64, :], w=[tSW])
            C.dma("sync", SW[32:64, :], C.hT[src_rows:src_rows + 32, :], w=[tSW])
            C.tt(X[:, :], X[:, :], cs[:, 0, :], ALU.mult, r=[tX, tcs], w=[tX])
            C.tt(SW[:, :], SW[:, :], cs[:, 1, :], ALU.mult, r=[tSW, tcs], w=[tSW])
            C.tt(X[:, :], X[:, :], SW[:, :], ALU.add, r=[tX, tSW], w=[tX])
            C.cp(X16[:, :], X[:, :], r=[tX], w=[tX16], eng="scalar")

        for kh in range(2):
            rope(KF, tKF, gk * 128 + kh * 64, KT, tKT)
            C.dma("sync", VF[:, :], C.hT[gv * 128 + kh * 64:gv * 128 + kh * 64 + 64, :], w=[tVF])
            C.dma("gpsimd", VT[:, :], C.hT[gv * 128 + kh * 64:gv * 128 + kh * 64 + 64, :], w=[tVT])
            for n in range(ND):
                C.dma("gpsimd", kraw[:, n, :], cswa[l, n, :, 0, kh, :], w=[tkraw])
                C.dma("gpsimd", vpd[:, n, 0:64], cswa[l, n, :, 1, kh, :], w=[tprevd])
            for n in range(ND):
                pt, tpt = vrot.get()
                C.tr(pt[0:64, 0:128], kraw[:, n, :], ident[:, :], r=[tkraw, tid], w=[tpt])
                C.cp(kpd[:, n, :], pt[0:64, 0:128], r=[tpt], w=[tprevd])
            for which, XT, tX in ((0, KF, tKF), (1, VF, tVF)):
                nat_rows_out(C, XT, tX, 64, identF, tidF, orot, strot, T - 128, 1,
                             lambda i, which=which, kh=kh: o_cswa[l, 0, :, which, kh, :])
                for n in range(ND):
                    b0 = dec_base(n)
                    C.dma("sync", o_cswa[l, 1 + n, 0:127, which, kh, :], cswa[l, n, 1:128, which, kh, :])
                    C.S.op("sync", lambda e, XT=XT, b0=b0, n=n, which=which, kh=kh: e.dma_start(
                        out=o_cswa[l, 1 + n, 127, which, kh, :].rearrange("(p o) -> p o", o=1), in_=XT[0:64, b0:b0 + 1],
                        allow_slow_non_contiguous=True), r=[tX], dma=True)
            for hh in range(8):
                h = kh * 8 + hh
                gq = GIDX[f"dq{h // 2}"]
                QT, tQT = QTs[h % 2]
                rope(XF, tXF, gq * 128 + (h % 2) * 64, QT, tQT)
                attn_core(C, 64, KT, tKT, VT, tVT, QT, tQT, msk[:, 0, :], msk[:, 1, :], tmsk, 1, scale, ident, tid,
                          None, None, oacc, toacc, None, None, True, kpd, vpd, tprevd, pools, True, textra=[tvtones])
                C.ts(oacc[64:65, :], oacc[64:65, :], snk[64:65, h:h + 1], ALU.add, r=[toacc, tsnk], w=[toacc])
                C.S.op("vector", lambda e: e.reciprocal(out=oacc[64:65, :], in_=oacc[64:65, :]), r=[toacc], w=[toacc])
                gg = GIDX[f"dg{h // 2}"]
                C.dma("sync", GF[:, :], C.hT[gg * 128 + (h % 2) * 64:gg * 128 + (h % 2) * 64 + 64, :], w=[tGF])
                C.act(GF[:, :], GF[:, :], AF.Silu, r=[tGF], w=[tGF])
                for t in range(NST):
                    pb_, tpb_ = srot.get()
                    sl = slice(t * 512, (t + 1) * 512)
                    C.mm(pb_[0:64, :], onesF[64:65, 0:64], oacc[64:65, sl], r=[tonesF, toacc], w=[tpb_])
                    C.tt(oacc[0:64, sl], oacc[0:64, sl], pb_[0:64, :], ALU.mult, r=[toacc, tpb_], w=[toacc])
                    C.tt(ys[:, sl], oacc[0:64, sl], GF[:, sl], ALU.mult, r=[toacc, tGF], w=[tys])
                C.dma("gpsimd", C.ysT[3072 + h * 64:3072 + (h + 1) * 64, :], ys[:, :], r=[tys])
        C.S.emit()
```
